# Optimizing a Trainium2 kernel written in Bass

```python
import math
import jax, jax.numpy as jnp
from jax import lax
import numpy as np

D_MODEL = 1024
BATCH = 8
SEQ = 4096
DEPTH = 1

N_ATTN_HEADS = 4
QK_DIM = 64
V_DIM = 2 * QK_DIM
ATTN_WIDTH = N_ATTN_HEADS * V_DIM
Q_COLS = N_ATTN_HEADS * 2 * QK_DIM
K_COLS = N_ATTN_HEADS * 2 * QK_DIM
V_COLS = N_ATTN_HEADS * V_DIM
CONV_WIDTH = D_MODEL // 2
CONV_GROUPS = 8
CONV_GROUP_DIM = CONV_WIDTH // CONV_GROUPS
CONV_K = 3
MIX_WIDTH = ATTN_WIDTH + CONV_WIDTH
IN_COLS = Q_COLS + K_COLS + V_COLS + 3 * CONV_WIDTH
D_FF = 2816
FFN_RESIDUAL_WEIGHT = 0.5
Q_BLOCK = 128
NORM_EPS = 1e-6

kernel_name = "hymba_diffattn_shortconv_macaron"


def rms_norm(x, g, eps=NORM_EPS):
    xf = x.astype(jnp.float32)
    y = xf * lax.rsqrt(jnp.mean(xf * xf, axis=-1, keepdims=True) + eps)
    return (y * g.astype(jnp.float32)).astype(x.dtype)


def swiglu_ffn(h, w_gate, w_up, w_down):
    return (jax.nn.silu(h @ w_gate) * (h @ w_up)) @ w_down


def alibi_slopes(n_heads):
    idx = jnp.arange(1, n_heads + 1, dtype=jnp.float32)
    return jnp.exp2(-8.0 * idx / n_heads)


def diff_attention(q, k, v, lam, slopes):
    b, s, h, _, dk = q.shape
    nb = s // Q_BLOCK
    scale = dk ** -0.5
    kf = k.astype(jnp.float32)
    pos_k = jnp.arange(s)
    q_blocks = (q.astype(jnp.float32) * scale).reshape(b, nb, Q_BLOCK, h, 2, dk).swapaxes(0, 1)

    def one_block(args):
        q_blk, start = args
        pos_q = start + jnp.arange(Q_BLOCK)
        dist = (pos_q[:, None] - pos_k[None, :]).astype(jnp.float32)
        scores = jnp.einsum('bqhmd,bkhmd->bhmqk', q_blk, kf)
        scores = scores - slopes[None, :, None, None, None] * dist
        scores = jnp.where(dist >= 0, scores, -jnp.inf)
        p = jax.nn.softmax(scores, axis=-1)
        w = p[:, :, 0] - lam * p[:, :, 1]
        return jnp.einsum('bhqk,bkhd->bqhd', w.astype(v.dtype), v)

    out = lax.map(one_block, (q_blocks, jnp.arange(nb) * Q_BLOCK))
    return out.swapaxes(0, 1).reshape(b, s, h, v.shape[-1])


def short_conv(u, w):
    c = u.shape[-1]
    return lax.conv_general_dilated(
        u, w[:, None, :].astype(u.dtype), window_strides=(1,),
        padding=[(CONV_K - 1, 0)], dimension_numbers=('NWC', 'WIO', 'NWC'),
        feature_group_count=c)


def setup_inputs(seed: int = 0) -> dict:
    key = jax.random.key(seed)
    ks = jax.random.split(key, 24)
    f32 = jnp.float32

    def nrm(k, shape, scale):
        return jax.random.normal(k, shape, f32) * scale

    def gain(k, shape):
        return 1.0 + 0.02 * jax.random.normal(k, shape, f32)

    L, D, F = DEPTH, D_MODEL, D_FF
    return {
        "x": jax.random.normal(ks[0], (BATCH, SEQ, D), f32),
        "ffn1_norm": gain(ks[1], (L, D)),
        "ffn1_w_gate": nrm(ks[2], (L, D, F), D ** -0.5),
        "ffn1_w_up": nrm(ks[3], (L, D, F), D ** -0.5),
        "ffn1_w_down": nrm(ks[4], (L, F, D), F ** -0.5),
        "mix_norm": gain(ks[5], (L, D)),
        "w_in": nrm(ks[6], (L, D, IN_COLS), D ** -0.5),
        "q_norm": gain(ks[7], (L, QK_DIM)),
        "k_norm": gain(ks[8], (L, QK_DIM)),
        "lambda_q1": nrm(ks[9], (L, QK_DIM), 0.1),
        "lambda_k1": nrm(ks[10], (L, QK_DIM), 0.1),
        "lambda_q2": nrm(ks[11], (L, QK_DIM), 0.1),
        "lambda_k2": nrm(ks[12], (L, QK_DIM), 0.1),
        "attn_subln": gain(ks[13], (L, V_DIM)),
        "conv_w": nrm(ks[14], (L, CONV_K, CONV_WIDTH), CONV_K ** -0.5),
        "conv_norm": gain(ks[15], (L, CONV_WIDTH)),
        "w_out": nrm(ks[16], (L, MIX_WIDTH, D), MIX_WIDTH ** -0.5),
        "ffn2_norm": gain(ks[17], (L, D)),
        "ffn2_w_gate": nrm(ks[18], (L, D, F), D ** -0.5),
        "ffn2_w_up": nrm(ks[19], (L, D, F), D ** -0.5),
        "ffn2_w_down": nrm(ks[20], (L, F, D), F ** -0.5),
        "final_norm": gain(ks[21], (L, D)),
    }


def reference(x, ffn1_norm, ffn1_w_gate, ffn1_w_up, ffn1_w_down, mix_norm, w_in,
              q_norm, k_norm, lambda_q1, lambda_k1, lambda_q2, lambda_k2, attn_subln,
              conv_w, conv_norm, w_out, ffn2_norm, ffn2_w_gate, ffn2_w_up, ffn2_w_down,
              final_norm):
    b, s, _ = x.shape
    slopes = alibi_slopes(N_ATTN_HEADS)
    splits = np.cumsum([Q_COLS, K_COLS, V_COLS, CONV_WIDTH, CONV_WIDTH]).tolist()
    for i in range(DEPTH):
        lam_init = 0.8 - 0.6 * math.exp(-0.3 * i)
        h = rms_norm(x, ffn1_norm[i])
        x = x + FFN_RESIDUAL_WEIGHT * swiglu_ffn(h, ffn1_w_gate[i], ffn1_w_up[i], ffn1_w_down[i])

        h = rms_norm(x, mix_norm[i])
        proj = h @ w_in[i]
        q, k, v, gate_b, gate_c, hc = jnp.split(proj, splits, axis=-1)

        q = rms_norm(q.reshape(b, s, N_ATTN_HEADS, 2, QK_DIM), q_norm[i])
        k = rms_norm(k.reshape(b, s, N_ATTN_HEADS, 2, QK_DIM), k_norm[i])
        v = v.reshape(b, s, N_ATTN_HEADS, V_DIM)
        lq1 = lambda_q1[i].astype(jnp.float32)
        lk1 = lambda_k1[i].astype(jnp.float32)
        lq2 = lambda_q2[i].astype(jnp.float32)
        lk2 = lambda_k2[i].astype(jnp.float32)
        lam = jnp.exp(jnp.sum(lq1 * lk1)) - jnp.exp(jnp.sum(lq2 * lk2)) + lam_init
        a = diff_attention(q, k, v, lam, slopes)
        a = (rms_norm(a, attn_subln[i]) * (1.0 - lam_init)).reshape(b, s, ATTN_WIDTH)

        c = gate_b * short_conv(gate_c * hc, conv_w[i])
        c = rms_norm(c.reshape(b, s, CONV_GROUPS, CONV_GROUP_DIM),
                     conv_norm[i].reshape(CONV_GROUPS, CONV_GROUP_DIM)).reshape(b, s, CONV_WIDTH)

        mixed = jnp.concatenate([a, c], axis=-1)
        x = x + mixed @ w_out[i]

        h = rms_norm(x, ffn2_norm[i])
        x = x + FFN_RESIDUAL_WEIGHT * swiglu_ffn(h, ffn2_w_gate[i], ffn2_w_up[i], ffn2_w_down[i])

        x = rms_norm(x, final_norm[i])
    return x
```

```python
import math
import numpy as np
import concourse.bass as bass
import concourse.mybir as mybir
from concourse.bass_utils import run_bass_kernel_spmd

F32 = mybir.dt.float32
BF16 = mybir.dt.bfloat16
ALU = mybir.AluOpType
AF = mybir.ActivationFunctionType
AX = mybir.AxisListType

D = 1024
DFF = 2816
NFC = 22
NFG = 11
TT = 512
EPS = 1e-6
LAM_INIT = 0.8 - 0.6 * math.exp(-0.3 * 0)
SLOPES = [2.0 ** (-8.0 * (i + 1) / 4) for i in range(4)]
NREL = 32

P_G1, P_GM, P_G2, P_GF = 0, 8, 16, 24
P_GQ, P_GK, P_GSUB = 32, 33, 34
P_GCN = 35
P_CW = 39
P_LAM = 51
NPRM = P_LAM + 256


class Tl:
    __slots__ = ("name", "w", "r", "excl")

    def __init__(self, name, excl=False):
        self.name = name
        self.w = None
        self.r = []
        self.excl = excl


class DSem:
    def __init__(self, nc, name):
        self.sem = nc.alloc_semaphore(name)
        self.key = name
        self.count = 0


class Eng:
    def __init__(self, nc, name, b):
        self.name = name
        self.b = b
        self.sem = nc.alloc_semaphore("s_" + name)
        self.key = "s_" + name
        self.n = 0
        self.seen = {}
        self.q = []

    def wait(self, ev):
        key, sem, val = ev
        if self.seen.get(key, 0) >= val:
            return
        self.seen[key] = val
        self.q.append(lambda b, sem=sem, val=val: b.wait_ge(sem, val))


class FW:
    def __init__(self, nc):
        self.nc = nc
        self.pe = Eng(nc, "pe", nc.tensor)
        self.act = Eng(nc, "act", nc.scalar)
        self.dve = Eng(nc, "dve", nc.vector)
        self.pool = Eng(nc, "pool", nc.gpsimd)
        self.sp = Eng(nc, "sp", nc.sync)
        self.nds = 0

    def dsem(self, name=None):
        self.nds += 1
        return DSem(self.nc, name or f"d{self.nds}")

    def _deps(self, eng, reads, writes):
        deps = []
        for t in reads:
            if t.w is not None:
                deps.append(t.w)
            if t.excl:
                deps.extend(t.r)
        for t in writes:
            if t.w is not None:
                deps.append(t.w)
            deps.extend(t.r)
        if eng is self.pe:
            deps = [d for d in deps if d[0] != self.pe.key]
        return deps

    def op(self, eng, emit, reads=(), writes=()):
        for d in self._deps(eng, reads, writes):
            eng.wait(d)
        eng.n += 1
        sem = eng.sem
        eng.q.append(lambda b, emit=emit, sem=sem: emit(b).then_inc(sem, 1))
        ev = (eng.key, eng.sem, eng.n)
        for t in reads:
            t.r.append(ev)
        for t in writes:
            t.w = ev
            t.r = []
        return ev

    def dma(self, eng, out, in_, ds, reads=(), writes=()):
        for d in self._deps(eng, reads, writes):
            eng.wait(d)
        ds.count += 16
        sem = ds.sem
        eng.q.append(lambda b, out=out, in_=in_, sem=sem: b.dma_start(out=out, in_=in_).then_inc(sem, 16))
        ev = (ds.key, ds.sem, ds.count)
        for t in reads:
            t.r.append(ev)
        for t in writes:
            t.w = ev
            t.r = []
        return ev

    def finish(self, final_events=()):
        for ev in final_events:
            self.sp.wait(ev)
        with self.nc.Block() as block:
            @block.sync
            def _(e):
                for f in self.sp.q:
                    f(e)

            @block.tensor
            def _(e):
                for f in self.pe.q:
                    f(e)

            @block.scalar
            def _(e):
                for f in self.act.q:
                    f(e)

            @block.vector
            def _(e):
                for f in self.dve.q:
                    f(e)

            @block.gpsimd
            def _(e):
                for f in self.pool.q:
                    f(e)


class Buf:
    __slots__ = ("h", "t", "ds")

    def __init__(self, h, t, ds=None):
        self.h = h
        self.t = t
        self.ds = ds


class Ring:
    def __init__(self, bufs):
        self.bufs = bufs
        self.i = 0

    def alloc(self):
        b = self.bufs[self.i % len(self.bufs)]
        self.i += 1
        return b


def build_nc(NCH, dbg=False):
    S = NCH * TT
    NKB = S // 128
    nc = bass.Bass("TRN2", target_bir_lowering=False)
    fw = FW(nc)
    pe, act, dve, pool, sp = fw.pe, fw.act, fw.dve, fw.pool, fw.sp

    xT = nc.dram_tensor("xT", [D, S], F32, kind="ExternalInput").ap()
    outT = nc.dram_tensor("outT", [D, S], F32, kind="ExternalOutput").ap()
    prm_d = nc.dram_tensor("prm", [128, NPRM], F32, kind="ExternalInput").ap()
    cst_d = nc.dram_tensor("cst", [128, 256], F32, kind="ExternalInput").ap()
    w32 = {}
    wbf = {}
    wshape = {"wgu1": [NFG, 128, 4096], "wd1": [2, 128, NFC * 512], "win": [6, 128, 4096],
              "wout": [2, 128, 4096], "wgu2": [NFG, 128, 4096], "wd2": [2, 128, NFC * 512]}
    for k, shp in wshape.items():
        w32[k] = nc.dram_tensor(k, shp, F32, kind="ExternalInput").ap()
        wbf[k] = nc.dram_tensor(k + "_bf", shp, BF16, kind="Internal").ap()
    wtl = {k: [Tl(f"{k}_{i}") for i in range(shp[0])] for k, shp in wshape.items()}

    def sb(name, shape, dt):
        return nc.alloc_sbuf_tensor(name, shape, dt)

    xbuf = [sb(f"x{i}", [128, 8, TT], F32) for i in range(2)]
    t_x = [[Tl(f"x{i}_{dc}") for dc in range(8)] for i in range(2)]
    ds_x = [fw.dsem(f"dx{i}") for i in range(2)]
    ds_o = [fw.dsem(f"do{i}") for i in range(2)]
    hb = sb("h", [128, 8, TT], BF16)
    t_h = [Tl(f"h{dc}") for dc in range(8)]
    actb = sb("act", [128, NFC, TT], BF16)
    t_act = [Tl(f"act{fc}") for fc in range(NFC)]
    qT = sb("qT", [128, 4, TT], BF16)
    t_q = [Tl(f"q{h}") for h in range(4)]
    kT = sb("kT", [128, 4, S], BF16)
    t_k = [[Tl(f"k{h}_{j}") for j in range(NCH)] for h in range(4)]
    Vb = sb("V", [128, NKB, 512], BF16)
    t_v = [Tl(f"v{kb}") for kb in range(NKB)]
    mixed = sb("mixed", [128, 8, TT], BF16)
    t_mx = [Tl(f"mx{c}") for c in range(8)]
    ubuf = [sb(f"u{cc}", [128, TT + 2], F32) for cc in range(4)]
    t_u = [Tl(f"u{cc}") for cc in range(4)]
    wring = Ring([Buf(sb(f"wr{i}", [128, 4096], BF16), Tl(f"wr{i}"), fw.dsem(f"dwr{i}")) for i in range(4)])
    pring = Ring([Buf(sb(f"pr{i}", [128, TT], BF16), Tl(f"pr{i}")) for i in range(4)])
    sqring = Ring([Buf(sb(f"sq{i}", [128, TT], BF16), Tl(f"sq{i}")) for i in range(2)])
    scr = Ring([Buf(sb(f"sc{i}", [128, TT], F32), Tl(f"sc{i}")) for i in range(6)])
    prm = sb("prm_s", [128, NPRM], F32)
    t_prm = Tl("prm")
    cst = sb("cst_s", [128, 256], F32)
    t_cst = Tl("cst")
    drv = sb("drv", [128, 8], F32)
    t_drv = Tl("drv")
    lamt = sb("lamt", [128, 2, 64], F32)
    lame = sb("lame", [128, 4], F32)
    t_lam = Tl("lam")
    ones_bf = sb("ones_bf", [128, 128], BF16)
    bd_bf = sb("bd_bf", [128, 128], BF16)
    tri_bf = sb("tri_bf", [128, 128], BF16)
    t_cbf = Tl("cbf")

    banks = [Buf(nc.alloc_psum_tensor(f"ps{i}", [128, TT], F32), Tl(f"ps{i}", excl=True)) for i in range(8)]
    psA = Ring(banks[0:4])
    psB = Ring(banks[4:8])

    fw.dma(sp, prm[:, :], prm_d, fw.dsem("dprm"), writes=[t_prm])
    fw.dma(sp, cst[:, :], cst_d, fw.dsem("dcst"), writes=[t_cst])
    for k in ("wgu1", "wd1", "win", "wout", "wgu2", "wd2"):
        for i in range(wshape[k][0]):
            fw.dma(pool, wbf[k][i], w32[k][i], fw.dsem(f"dc_{k}_{i}"), writes=[wtl[k][i]])

    fw.op(dve, lambda b: b.memset(ones_bf[:, :], 1.0), writes=[t_cbf])
    fw.op(dve, lambda b: b.memset(bd_bf[:, :], 0.0), writes=[t_cbf])
    fw.op(dve, lambda b: b.memset(bd_bf[0:64, 0:64], 1.0), writes=[t_cbf])
    fw.op(dve, lambda b: b.memset(bd_bf[64:128, 64:128], 1.0), writes=[t_cbf])
    fw.op(dve, lambda b: b.tensor_copy(out=tri_bf[:, :], in_=cst[:, 0:128]), reads=[t_cst], writes=[t_cbf])
    for cc in range(4):
        fw.op(dve, lambda b, cc=cc: b.memset(ubuf[cc][:, 0:2], 0.0), writes=[t_u[cc]])
    fw.op(dve, lambda b: b.tensor_scalar(out=drv[:, 0:1], in0=prm[:, P_GQ:P_GQ + 1], scalar1=0.125, scalar2=None,
                                          op0=ALU.mult), reads=[t_prm], writes=[t_drv])
    fw.op(dve, lambda b: b.tensor_scalar(out=drv[:, 1:2], in0=prm[:, P_GSUB:P_GSUB + 1], scalar1=1.0 - LAM_INIT,
                                          scalar2=None, op0=ALU.mult), reads=[t_prm], writes=[t_drv])
    fw.op(dve, lambda b: b.memset(drv[:, 3:4], EPS), writes=[t_drv])
    fw.op(dve, lambda b: b.tensor_tensor(out=lamt[:, 0, :], in0=prm[:, P_LAM:P_LAM + 64],
                                          in1=prm[:, P_LAM + 64:P_LAM + 128], op=ALU.mult), reads=[t_prm], writes=[t_lam])
    fw.op(dve, lambda b: b.tensor_tensor(out=lamt[:, 1, :], in0=prm[:, P_LAM + 128:P_LAM + 192],
                                          in1=prm[:, P_LAM + 192:P_LAM + 256], op=ALU.mult), reads=[t_prm], writes=[t_lam])
    fw.op(dve, lambda b: b.tensor_reduce(out=lame[:, 0:2], in_=lamt[:, :, :], axis=AX.X, op=ALU.add),
          reads=[t_lam], writes=[t_lam])
    fw.op(act, lambda b: b.activation(out=lame[:, 2:4], in_=lame[:, 0:2], func=AF.Exp), reads=[t_lam], writes=[t_lam])
    fw.op(dve, lambda b: b.tensor_tensor(out=lame[:, 0:1], in0=lame[:, 3:4], in1=lame[:, 2:3], op=ALU.subtract),
          reads=[t_lam], writes=[t_lam])
    fw.op(dve, lambda b: b.tensor_scalar(out=drv[:, 2:3], in0=lame[:, 0:1], scalar1=-LAM_INIT, scalar2=None,
                                          op0=ALU.add), reads=[t_lam], writes=[t_drv])
    eps_col = drv[:, 3:4]

    def load_piece(key, idx, col0=0, ncol=4096):
        slot = wring.alloc()
        fw.dma(sp, slot.h[:, 0:ncol], wbf[key][idx][:, col0:col0 + ncol], slot.ds,
               reads=[wtl[key][idx]], writes=[slot.t])
        return slot

    def mm_group(out_ap, pairs, start=True, stop=True):
        def emit(b):
            ins = None
            n = len(pairs)
            for i, (l, r) in enumerate(pairs):
                ins = b.matmul(out_ap, lhsT=l, rhs=r, start=(start and i == 0), stop=(stop and i == n - 1))
            return ins
        return emit

    def rstd_from(ps, inv_n):
        r = scr.alloc()
        fw.op(act, lambda b: b.activation(out=r.h[:, :], in_=ps.h[:, :], func=AF.Ln, bias=eps_col, scale=inv_n),
              reads=[ps.t, t_drv], writes=[r.t])
        fw.op(act, lambda b: b.activation(out=r.h[:, :], in_=r.h[:, :], func=AF.Exp, scale=-0.5),
              reads=[r.t], writes=[r.t])
        return r

    def full_norm(xb, gcol, out_h=True):
        x = xbuf[xb]
        ps = psA.alloc()
        for dc in range(8):
            sq = sqring.alloc()
            fw.op(act, lambda b, sq=sq, dc=dc: b.activation(out=sq.h[:, :], in_=x[:, dc, :], func=AF.Square),
                  reads=[t_x[xb][dc]], writes=[sq.t])
            fw.op(pe, lambda b, sq=sq, dc=dc: b.matmul(ps.h[:, :], lhsT=ones_bf[:, :], rhs=sq.h[:, :],
                                                       start=(dc == 0), stop=(dc == 7)),
                  reads=[sq.t, t_cbf], writes=[ps.t])
        r = rstd_from(ps, 1.0 / D)
        for dc in range(8):
            if out_h:
                fw.op(dve, lambda b, dc=dc: b.scalar_tensor_tensor(
                    out=hb[:, dc, :], in0=x[:, dc, :], scalar=prm[:, gcol + dc:gcol + dc + 1], in1=r.h[:, :],
                    op0=ALU.mult, op1=ALU.mult), reads=[t_x[xb][dc], r.t, t_prm], writes=[t_h[dc]])
            else:
                fw.op(dve, lambda b, dc=dc: b.scalar_tensor_tensor(
                    out=x[:, dc, :], in0=x[:, dc, :], scalar=prm[:, gcol + dc:gcol + dc + 1], in1=r.h[:, :],
                    op0=ALU.mult, op1=ALU.mult), reads=[r.t, t_prm], writes=[t_x[xb][dc]])

    def ffn(xb, gcol, kgu, kd):
        x = xbuf[xb]
        full_norm(xb, gcol)
        for fg in range(NFG):
            slot = load_piece(kgu, fg)
            for f2 in range(2):
                fc = fg * 2 + f2
                g_ps = psA.alloc()
                u_ps = psA.alloc()
                for which, ps in ((0, g_ps), (1, u_ps)):
                    pairs = [(slot.h[:, (which * 8 + dc) * 256 + f2 * 128:(which * 8 + dc) * 256 + f2 * 128 + 128],
                              hb[:, dc, :]) for dc in range(8)]
                    fw.op(pe, mm_group(ps.h[:, :], pairs), reads=[slot.t] + t_h, writes=[ps.t])
                sg = scr.alloc()
                fw.op(act, lambda b, sg=sg, g_ps=g_ps: b.activation(out=sg.h[:, :], in_=g_ps.h[:, :], func=AF.Silu),
                      reads=[g_ps.t], writes=[sg.t])
                fw.op(dve, lambda b, sg=sg, u_ps=u_ps, fc=fc: b.tensor_tensor(
                    out=actb[:, fc, :], in0=sg.h[:, :], in1=u_ps.h[:, :], op=ALU.mult),
                    reads=[sg.t, u_ps.t], writes=[t_act[fc]])
        for half in range(2):
            ring = psB if half == 0 else psA
            accs = [ring.alloc() for _ in range(4)]
            for (fc0, fc1) in ((0, 8), (8, 16), (16, 22)):
                slot = load_piece(kd, half, fc0 * 512, (fc1 - fc0) * 512)
                for fc in range(fc0, fc1):
                    def emit(b, fc=fc, fc0=fc0, slot=slot, accs=accs):
                        ins = None
                        for q in range(4):
                            ins = b.matmul(accs[q].h[:, :],
                                           lhsT=slot.h[:, (fc - fc0) * 512 + q * 128:(fc - fc0) * 512 + q * 128 + 128],
                                           rhs=actb[:, fc, :], start=(fc == 0), stop=(fc == NFC - 1))
                        return ins
                    fw.op(pe, emit, reads=[slot.t, t_act[fc]], writes=[a.t for a in accs])
            for q in range(4):
                dco = half * 4 + q
                fw.op(dve, lambda b, q=q, dco=dco, accs=accs: b.scalar_tensor_tensor(
                    out=x[:, dco, :], in0=accs[q].h[:, :], scalar=0.5, in1=x[:, dco, :],
                    op0=ALU.mult, op1=ALU.add), reads=[accs[q].t], writes=[t_x[xb][dco]])

    def group_norm_tail(src_ap, src_tiles, ps_src, inv_n, bdmat, gain_ap, dst_ap, dst_tile, gain_tile):
        sq = sqring.alloc()
        fw.op(act, lambda b: b.activation(out=sq.h[:, :], in_=src_ap, func=AF.Square), reads=src_tiles, writes=[sq.t])
        ss = psA.alloc()
        fw.op(pe, lambda b: b.matmul(ss.h[:, :], lhsT=bdmat, rhs=sq.h[:, :], start=True, stop=True),
              reads=[sq.t, t_cbf], writes=[ss.t])
        r = rstd_from(ss, inv_n)
        fw.op(dve, lambda b: b.scalar_tensor_tensor(out=dst_ap, in0=src_ap, scalar=gain_ap, in1=r.h[:, :],
                                                     op0=ALU.mult, op1=ALU.mult),
              reads=list(src_tiles) + [r.t, gain_tile], writes=[dst_tile])

    def mixer(xb, j):
        x = xbuf[xb]
        full_norm(xb, P_GM)
        for which in range(2):
            slot = load_piece("win", which)
            for hd in range(4):
                ps = psA.alloc()
                pairs = [(slot.h[:, dc * 512 + hd * 128:dc * 512 + hd * 128 + 128], hb[:, dc, :]) for dc in range(8)]
                fw.op(pe, mm_group(ps.h[:, :], pairs), reads=[slot.t] + t_h, writes=[ps.t])
                if which == 0:
                    group_norm_tail(ps.h[:, :], [ps.t], ps, 1.0 / 64, bd_bf[:, :], drv[:, 0:1],
                                    qT[:, hd, :], t_q[hd], t_drv)
                else:
                    group_norm_tail(ps.h[:, :], [ps.t], ps, 1.0 / 64, bd_bf[:, :], prm[:, P_GK:P_GK + 1],
                                    kT[:, hd, j * TT:(j + 1) * TT], t_k[hd][j], t_prm)
        slot = load_piece("win", 2)
        for tb in range(4):
            ps = psA.alloc()
            pairs = [(hb[:, dc, tb * 128:(tb + 1) * 128], slot.h[:, dc * 512:(dc + 1) * 512]) for dc in range(8)]
            fw.op(pe, mm_group(ps.h[:, :], pairs), reads=[slot.t] + t_h, writes=[ps.t])
            kb = j * 4 + tb
            fw.op(act, lambda b, ps=ps, kb=kb: b.activation(out=Vb[:, kb, :], in_=ps.h[:, :], func=AF.Copy),
                  reads=[ps.t], writes=[t_v[kb]])
        sl_b = load_piece("win", 3)
        sl_c = load_piece("win", 4)
        sl_h = load_piece("win", 5)
        for cc in range(4):
            pss = []
            for sl in (sl_b, sl_c, sl_h):
                ps = psA.alloc()
                pairs = [(sl.h[:, dc * 512 + cc * 128:dc * 512 + cc * 128 + 128], hb[:, dc, :]) for dc in range(8)]
                fw.op(pe, mm_group(ps.h[:, :], pairs), reads=[sl.t] + t_h, writes=[ps.t])
                pss.append(ps)
            gb_ps, gc_ps, hc_ps = pss
            hcs = scr.alloc()
            fw.op(act, lambda b, hcs=hcs, hc_ps=hc_ps: b.activation(out=hcs.h[:, :], in_=hc_ps.h[:, :], func=AF.Copy),
                  reads=[hc_ps.t], writes=[hcs.t])
            u = ubuf[cc]
            fw.op(dve, lambda b, u=u, hcs=hcs, gc_ps=gc_ps: b.tensor_tensor(
                out=u[:, 2:TT + 2], in0=hcs.h[:, :], in1=gc_ps.h[:, :], op=ALU.mult),
                reads=[hcs.t, gc_ps.t], writes=[t_u[cc]])
            y = scr.alloc()
            cw = lambda k, cc=cc: prm[:, P_CW + cc * 3 + k:P_CW + cc * 3 + k + 1]
            fw.op(dve, lambda b, u=u, y=y, cw=cw: b.tensor_scalar(out=y.h[:, :], in0=u[:, 2:TT + 2], scalar1=cw(2),
                                                                   scalar2=None, op0=ALU.mult),
                  reads=[t_u[cc], t_prm], writes=[y.t])
            fw.op(dve, lambda b, u=u, y=y, cw=cw: b.scalar_tensor_tensor(
                out=y.h[:, :], in0=u[:, 1:TT + 1], scalar=cw(1), in1=y.h[:, :], op0=ALU.mult, op1=ALU.add),
                reads=[t_u[cc], t_prm], writes=[y.t])
            fw.op(dve, lambda b, u=u, y=y, cw=cw: b.scalar_tensor_tensor(
                out=y.h[:, :], in0=u[:, 0:TT], scalar=cw(0), in1=y.h[:, :], op0=ALU.mult, op1=ALU.add),
                reads=[t_u[cc], t_prm], writes=[y.t])
            fw.op(dve, lambda b, u=u: b.tensor_copy(out=u[:, 0:2], in_=u[:, TT:TT + 2]), writes=[t_u[cc]])
            fw.op(dve, lambda b, y=y, gb_ps=gb_ps: b.tensor_tensor(out=y.h[:, :], in0=y.h[:, :], in1=gb_ps.h[:, :],
                                                                    op=ALU.mult), reads=[gb_ps.t], writes=[y.t])
            group_norm_tail(y.h[:, :], [y.t], None, 1.0 / 64, bd_bf[:, :], prm[:, P_GCN + cc:P_GCN + cc + 1],
                            mixed[:, 4 + cc, :], t_mx[4 + cc], t_prm)
        nkb = 4 * j + 4
        pending_tail = None
        for hd in range(4):
            acc = [psB.alloc(), psB.alloc()]
            zz = [psB.alloc(), psB.alloc()]
            blocks = [(m, kb) for kb in range(nkb) for m in range(2)]
            LOOK = 2
            inflight = []
            for idx in range(len(blocks) + LOOK):
                if idx < len(blocks):
                    m, kb = blocks[idx]
                    rel = kb - 4 * j
                    koff = max(0, rel) * 128
                    s_ps = psA.alloc()
                    fw.op(pe, lambda b, s_ps=s_ps, m=m, kb=kb, koff=koff, hd=hd: b.matmul(
                        s_ps.h[:, koff:TT], lhsT=kT[m * 64:(m + 1) * 64, hd, kb * 128:(kb + 1) * 128],
                        rhs=qT[m * 64:(m + 1) * 64, hd, koff:TT], start=True, stop=True),
                        reads=[t_k[hd][kb // 4], t_q[hd]], writes=[s_ps.t])
                    p = pring.alloc()
                    bcol = 128 + hd * NREL + (rel + NREL - 4)
                    fw.op(act, lambda b, p=p, s_ps=s_ps, koff=koff, bcol=bcol: b.activation(
                        out=p.h[:, koff:TT], in_=s_ps.h[:, koff:TT], func=AF.Exp, bias=cst[:, bcol:bcol + 1], scale=1.0),
                        reads=[s_ps.t, t_cst], writes=[p.t])
                    if rel >= 0:
                        fw.op(pool, lambda b, p=p, koff=koff: b.tensor_tensor(
                            out=p.h[:, koff:koff + 128], in0=p.h[:, koff:koff + 128], in1=tri_bf[:, :], op=ALU.mult),
                            reads=[t_cbf], writes=[p.t])
                    inflight.append((m, kb, koff, p))
                if idx >= LOOK:
                    m, kb, koff, p = inflight.pop(0)

                    def emit(b, m=m, kb=kb, koff=koff, p=p, hd=hd, acc=acc, zz=zz):
                        b.matmul(acc[m].h[:, koff:TT], lhsT=Vb[:, kb, hd * 128:(hd + 1) * 128], rhs=p.h[:, koff:TT],
                                 start=(kb == 0), stop=(kb == nkb - 1))
                        return b.matmul(zz[m].h[:, koff:TT], lhsT=ones_bf[:, :], rhs=p.h[:, koff:TT],
                                        start=(kb == 0), stop=(kb == nkb - 1))
                    fw.op(pe, emit, reads=[p.t, t_v[kb], t_cbf], writes=[acc[m].t, zz[m].t])
            ts = []
            for m in range(2):
                rz = scr.alloc()
                fw.op(dve, lambda b, rz=rz, m=m, zz=zz: b.reciprocal(out=rz.h[:, :], in_=zz[m].h[:, :]),
                      reads=[zz[m].t], writes=[rz.t])
                tm = scr.alloc()
                fw.op(dve, lambda b, rz=rz, tm=tm, m=m, acc=acc: b.tensor_tensor(
                    out=tm.h[:, :], in0=acc[m].h[:, :], in1=rz.h[:, :], op=ALU.mult),
                    reads=[acc[m].t, rz.t], writes=[tm.t])
                ts.append(tm)
            a = ts[0]
            fw.op(dve, lambda b, ts=ts: b.scalar_tensor_tensor(
                out=ts[0].h[:, :], in0=ts[1].h[:, :], scalar=drv[:, 2:3], in1=ts[0].h[:, :], op0=ALU.mult, op1=ALU.add),
                reads=[ts[1].t, t_drv], writes=[ts[0].t])
            group_norm_tail(a.h[:, :], [a.t], None, 1.0 / 128, ones_bf[:, :], drv[:, 1:2],
                            mixed[:, hd, :], t_mx[hd], t_drv)
        for half in range(2):
            slot = load_piece("wout", half)
            for q in range(4):
                dco = half * 4 + q
                ps = psA.alloc()
                pairs = [(slot.h[:, c * 512 + q * 128:c * 512 + q * 128 + 128], mixed[:, c, :]) for c in range(8)]
                fw.op(pe, mm_group(ps.h[:, :], pairs), reads=[slot.t] + t_mx, writes=[ps.t])
                fw.op(dve, lambda b, ps=ps, dco=dco: b.tensor_tensor(out=x[:, dco, :], in0=ps.h[:, :], in1=x[:, dco, :],
                                                                      op=ALU.add), reads=[ps.t], writes=[t_x[xb][dco]])

    def load_x(j):
        xb = j % 2
        fw.dma(sp, xbuf[xb][:, :, :], xT[:, j * TT:(j + 1) * TT].rearrange("(dc p) s -> p dc s", p=128), ds_x[xb],
               writes=t_x[xb])

    finals = []
    load_x(0)
    for j in range(NCH):
        xb = j % 2
        if j + 1 < NCH:
            load_x(j + 1)
        ffn(xb, P_G1, "wgu1", "wd1")
        mixer(xb, j)
        ffn(xb, P_G2, "wgu2", "wd2")
        full_norm(xb, P_GF, out_h=False)
        ev = fw.dma(pool, outT[:, j * TT:(j + 1) * TT].rearrange("(dc p) s -> p dc s", p=128), xbuf[xb][:, :, :],
                    ds_o[xb], reads=t_x[xb])
        finals.append(ev)
    fw.finish(finals[-2:])
    return nc


def _consts():
    cst = np.zeros((128, 256), np.float32)
    ki = np.arange(128)
    cst[:, 0:128] = (ki[:, None] <= ki[None, :]).astype(np.float32)
    for h in range(4):
        for r in range(NREL):
            rel = r - (NREL - 4)
            cst[:, 128 + h * NREL + r] = SLOPES[h] * (ki + 128.0 * rel - 256.0)
    return cst


def _layout_weights(inp):
    f32 = lambda a: np.ascontiguousarray(np.asarray(a, dtype=np.float32))
    out = {}
    for i, tag in ((1, "ffn1"), (2, "ffn2")):
        wg = f32(inp[f"{tag}_w_gate"])[0].reshape(8, 128, NFG, 256)
        wu = f32(inp[f"{tag}_w_up"])[0].reshape(8, 128, NFG, 256)
        gu = np.stack([wg, wu], axis=0)
        out[f"wgu{i}"] = np.ascontiguousarray(gu.transpose(3, 2, 0, 1, 4)).reshape(NFG, 128, 4096)
        wd = f32(inp[f"{tag}_w_down"])[0].reshape(NFC, 128, 2, 512)
        out[f"wd{i}"] = np.ascontiguousarray(wd.transpose(2, 1, 0, 3)).reshape(2, 128, NFC * 512)
    win = f32(inp["w_in"])[0].reshape(8, 128, 6, 512)
    out["win"] = np.ascontiguousarray(win.transpose(2, 1, 0, 3)).reshape(6, 128, 4096)
    wo = f32(inp["w_out"])[0].reshape(8, 128, 2, 512)
    out["wout"] = np.ascontiguousarray(wo.transpose(2, 1, 0, 3)).reshape(2, 128, 4096)
    prm = np.zeros((128, NPRM), np.float32)
    for col, key in ((P_G1, "ffn1_norm"), (P_GM, "mix_norm"), (P_G2, "ffn2_norm"), (P_GF, "final_norm")):
        prm[:, col:col + 8] = f32(inp[key])[0].reshape(8, 128).T
    prm[:, P_GQ] = np.tile(f32(inp["q_norm"])[0], 2)
    prm[:, P_GK] = np.tile(f32(inp["k_norm"])[0], 2)
    prm[:, P_GSUB] = f32(inp["attn_subln"])[0]
    prm[:, P_GCN:P_GCN + 4] = f32(inp["conv_norm"])[0].reshape(4, 128).T
    cw = f32(inp["conv_w"])[0]
    for cc in range(4):
        for k in range(3):
            prm[:, P_CW + cc * 3 + k] = cw[k, cc * 128:(cc + 1) * 128]
    for i, key in enumerate(("lambda_q1", "lambda_k1", "lambda_q2", "lambda_k2")):
        prm[:, P_LAM + i * 64:P_LAM + (i + 1) * 64] = f32(inp[key])[0][None, :]
    out["prm"] = prm
    out["cst"] = _consts()
    return out


_NC_CACHE = {}


def _get_nc(nch):
    if nch not in _NC_CACHE:
        _NC_CACHE[nch] = build_nc(nch)
    return _NC_CACHE[nch]


def kernel(**inputs):
    x = np.asarray(inputs["x"], dtype=np.float32)
    B, S, _ = x.shape
    nch = S // TT
    shared = _layout_weights(inputs)
    in_maps = []
    for b in range(B):
        m = dict(shared)
        m["xT"] = np.ascontiguousarray(x[b].T)
        in_maps.append(m)
    nc = build_nc(nch)
    res = run_bass_kernel_spmd(nc, in_maps, core_ids=list(range(B)))
    out = np.stack([np.ascontiguousarray(np.asarray(r["outT"]).T) for r in res.results], axis=0)
    return out.astype(np.float32)
```

```python
import math
import numpy as np
import concourse.bass as bass
import concourse.mybir as mybir
from concourse.bass_utils import run_bass_kernel_spmd

F32 = mybir.dt.float32
BF16 = mybir.dt.bfloat16
ALU = mybir.AluOpType
AF = mybir.ActivationFunctionType
AX = mybir.AxisListType

D = 1024
DFF = 2816
NFC = 22
NFG = 11
TT = 512
EPS = 1e-6
LAM_INIT = 0.8 - 0.6 * math.exp(-0.3 * 0)
SLOPES = [2.0 ** (-8.0 * (i + 1) / 4) for i in range(4)]
NREL = 32

P_G1, P_GM, P_G2, P_GF = 0, 8, 16, 24
P_GQ, P_GK, P_GSUB = 32, 33, 34
P_GCN = 35
P_CW = 39
P_LAM = 51
NPRM = P_LAM + 256


class Tl:
    __slots__ = ("name", "w", "r", "excl")

    def __init__(self, name, excl=False):
        self.name = name
        self.w = None
        self.r = []
        self.excl = excl


class DSem:
    def __init__(self, nc, name):
        self.sem = nc.alloc_semaphore(name)
        self.key = name
        self.count = 0


class Eng:
    def __init__(self, nc, name, b):
        self.name = name
        self.b = b
        self.sem = nc.alloc_semaphore("s_" + name)
        self.key = "s_" + name
        self.n = 0
        self.seen = {}
        self.q = []

    def wait(self, ev):
        key, sem, val = ev
        if self.seen.get(key, 0) >= val:
            return
        self.seen[key] = val
        self.q.append(lambda b, sem=sem, val=val: b.wait_ge(sem, val))


class FW:
    def __init__(self, nc):
        self.nc = nc
        self.pe = Eng(nc, "pe", nc.tensor)
        self.act = Eng(nc, "act", nc.scalar)
        self.dve = Eng(nc, "dve", nc.vector)
        self.pool = Eng(nc, "pool", nc.gpsimd)
        self.sp = Eng(nc, "sp", nc.sync)
        self.nds = 0

    def dsem(self, name=None):
        self.nds += 1
        return DSem(self.nc, name or f"d{self.nds}")

    def _deps(self, eng, reads, writes):
        deps = []
        for t in reads:
            if t.w is not None:
                deps.append(t.w)
            if t.excl:
                deps.extend(t.r)
        for t in writes:
            if t.w is not None:
                deps.append(t.w)
            deps.extend(t.r)
        if eng is self.pe:
            deps = [d for d in deps if d[0] != self.pe.key]
        return deps

    def op(self, eng, emit, reads=(), writes=()):
        for d in self._deps(eng, reads, writes):
            eng.wait(d)
        eng.n += 1
        sem = eng.sem
        eng.q.append(lambda b, emit=emit, sem=sem: emit(b).then_inc(sem, 1))
        ev = (eng.key, eng.sem, eng.n)
        for t in reads:
            t.r.append(ev)
        for t in writes:
            t.w = ev
            t.r = []
        return ev

    def dma(self, eng, out, in_, ds, reads=(), writes=()):
        for d in self._deps(eng, reads, writes):
            eng.wait(d)
        ds.count += 16
        sem = ds.sem
        eng.q.append(lambda b, out=out, in_=in_, sem=sem: b.dma_start(out=out, in_=in_).then_inc(sem, 16))
        ev = (ds.key, ds.sem, ds.count)
        for t in reads:
            t.r.append(ev)
        for t in writes:
            t.w = ev
            t.r = []
        return ev

    def finish(self, final_events=()):
        for ev in final_events:
            self.sp.wait(ev)
        with self.nc.Block() as block:
            @block.sync
            def _(e):
                for f in self.sp.q:
                    f(e)

            @block.tensor
            def _(e):
                for f in self.pe.q:
                    f(e)

            @block.scalar
            def _(e):
                for f in self.act.q:
                    f(e)

            @block.vector
            def _(e):
                for f in self.dve.q:
                    f(e)

            @block.gpsimd
            def _(e):
                for f in self.pool.q:
                    f(e)


class Buf:
    __slots__ = ("h", "t", "ds")

    def __init__(self, h, t, ds=None):
        self.h = h
        self.t = t
        self.ds = ds


class BankView:
    def __init__(self, t, m):
        self.t = t
        self.m = m

    def __getitem__(self, idx):
        p, c = idx
        return self.t[p, self.m, c]


class Ring:
    def __init__(self, bufs):
        self.bufs = bufs
        self.i = 0

    def alloc(self):
        b = self.bufs[self.i % len(self.bufs)]
        self.i += 1
        return b


def build_nc(NCH, dbg=False):
    S = NCH * TT
    NKB = S // 128
    nc = bass.Bass("TRN2", target_bir_lowering=False)
    fw = FW(nc)
    pe, act, dve, pool, sp = fw.pe, fw.act, fw.dve, fw.pool, fw.sp

    xT = nc.dram_tensor("xT", [D, S], F32, kind="ExternalInput").ap()
    outT = nc.dram_tensor("outT", [D, S], F32, kind="ExternalOutput").ap()
    prm_d = nc.dram_tensor("prm", [128, NPRM], F32, kind="ExternalInput").ap()
    cst_d = nc.dram_tensor("cst", [128, 256], F32, kind="ExternalInput").ap()
    w32 = {}
    wbf = {}
    wshape = {"wgu1": [NFG, 128, 4096], "wd1": [2, 128, NFC * 512], "win": [6, 128, 4096],
              "wout": [2, 128, 4096], "wgu2": [NFG, 128, 4096], "wd2": [2, 128, NFC * 512]}
    for k, shp in wshape.items():
        w32[k] = nc.dram_tensor(k, shp, F32, kind="ExternalInput").ap()
        wbf[k] = nc.dram_tensor(k + "_bf", shp, BF16, kind="Internal").ap()
    wtl = {k: Tl(f"wt_{k}") for k in wshape}
    ds_wb = {k: fw.dsem(f"dwb_{k}") for k in wshape}

    def sb(name, shape, dt):
        return nc.alloc_sbuf_tensor(name, shape, dt)

    xbuf = [sb(f"x{i}", [128, 8, TT], F32) for i in range(2)]
    t_x = [[Tl(f"x{i}_{dc}") for dc in range(8)] for i in range(2)]
    ds_x = [fw.dsem(f"dx{i}") for i in range(2)]
    ds_o = [fw.dsem(f"do{i}") for i in range(2)]
    hb = sb("h", [128, 8, TT], BF16)
    t_h = [Tl(f"h{dc}") for dc in range(8)]
    actb = sb("act", [128, NFC, TT], BF16)
    t_act = [Tl(f"act{fc}") for fc in range(NFC)]
    qz = sb("qz", [128, 4, 2, TT], BF16)
    t_q = [Tl(f"q{h}") for h in range(4)]
    kT = sb("kT", [128, 4, S], BF16)
    t_k = [[Tl(f"k{h}_{j}") for j in range(NCH)] for h in range(4)]
    Vb = sb("V", [128, NKB, 512], BF16)
    t_v = [Tl(f"v{kb}") for kb in range(NKB)]
    mixed = sb("mixed", [128, 8, TT], BF16)
    t_mx = [Tl(f"mx{c}") for c in range(8)]
    ubuf = [sb(f"u{cc}", [128, TT + 2], F32) for cc in range(4)]
    t_u = [Tl(f"u{cc}") for cc in range(4)]
    wring = Ring([Buf(sb(f"wr{i}", [128, 4096], BF16), Tl(f"wr{i}"), fw.dsem(f"dwr{i}")) for i in range(4)])
    pring = Ring([Buf(sb(f"pr{i}", [128, 2, TT], BF16), Tl(f"pr{i}")) for i in range(3)])
    sqring = Ring([Buf(sb(f"sq{i}", [128, TT], BF16), Tl(f"sq{i}")) for i in range(2)])
    scr = Ring([Buf(sb(f"sc{i}", [128, TT], F32), Tl(f"sc{i}")) for i in range(6)])
    prm = sb("prm_s", [128, NPRM], F32)
    t_prm = Tl("prm")
    cst = sb("cst_s", [128, 256], F32)
    t_cst = Tl("cst")
    drv = sb("drv", [128, 8], F32)
    t_drv = Tl("drv")
    t_dummy = Tl("dummy")
    lamt = sb("lamt", [128, 2, 64], F32)
    lame = sb("lame", [128, 4], F32)
    t_lam = Tl("lam")
    ones_bf = sb("ones_bf", [128, 128], BF16)
    bd_bf = sb("bd_bf", [128, 128], BF16)
    tri2 = sb("tri2", [128, 2, 128], BF16)
    t_cbf = Tl("cbf")

    pa = [nc.alloc_psum_tensor(f"pa{i}", [128, 2, TT], F32) for i in range(2)]
    pb = [nc.alloc_psum_tensor(f"pb{i}", [128, TT], F32) for i in range(4)]
    bankA = [Buf(BankView(pa[i // 2], i % 2), Tl(f"psA{i}", excl=True)) for i in range(4)]
    bankB = [Buf(pb[i], Tl(f"psB{i}", excl=True)) for i in range(4)]
    psA = Ring(bankA)
    psB = Ring(bankB)
    pairA = Ring([(pa[0], bankA[0].t, bankA[1].t), (pa[1], bankA[2].t, bankA[3].t)])

    fw.dma(sp, prm[:, :], prm_d, fw.dsem("dprm"), writes=[t_prm])
    fw.dma(sp, cst[:, :], cst_d, fw.dsem("dcst"), writes=[t_cst])
    fw.op(dve, lambda b: b.memset(ones_bf[:, :], 1.0), writes=[t_cbf])
    fw.op(dve, lambda b: b.memset(bd_bf[:, :], 0.0), writes=[t_cbf])
    fw.op(dve, lambda b: b.memset(bd_bf[0:64, 0:64], 1.0), writes=[t_cbf])
    fw.op(dve, lambda b: b.memset(bd_bf[64:128, 64:128], 1.0), writes=[t_cbf])
    for m in range(2):
        fw.op(dve, lambda b, m=m: b.tensor_copy(out=tri2[:, m, :], in_=cst[:, 0:128]), reads=[t_cst], writes=[t_cbf])
    for cc in range(4):
        fw.op(dve, lambda b, cc=cc: b.memset(ubuf[cc][:, 0:2], 0.0), writes=[t_u[cc]])
    for hd in range(4):
        fw.op(dve, lambda b, hd=hd: b.memset(qz[:, hd, :, :], 0.0), writes=[t_q[hd]])
    fw.op(dve, lambda b: b.tensor_scalar(out=drv[:, 0:1], in0=prm[:, P_GQ:P_GQ + 1], scalar1=0.125, scalar2=None,
                                          op0=ALU.mult), reads=[t_prm], writes=[t_drv])
    fw.op(dve, lambda b: b.tensor_scalar(out=drv[:, 1:2], in0=prm[:, P_GSUB:P_GSUB + 1], scalar1=1.0 - LAM_INIT,
                                          scalar2=None, op0=ALU.mult), reads=[t_prm], writes=[t_drv])
    fw.op(dve, lambda b: b.memset(drv[:, 3:4], EPS), writes=[t_drv])
    fw.op(dve, lambda b: b.tensor_tensor(out=lamt[:, 0, :], in0=prm[:, P_LAM:P_LAM + 64],
                                          in1=prm[:, P_LAM + 64:P_LAM + 128], op=ALU.mult), reads=[t_prm], writes=[t_lam])
    fw.op(dve, lambda b: b.tensor_tensor(out=lamt[:, 1, :], in0=prm[:, P_LAM + 128:P_LAM + 192],
                                          in1=prm[:, P_LAM + 192:P_LAM + 256], op=ALU.mult), reads=[t_prm], writes=[t_lam])
    fw.op(dve, lambda b: b.tensor_reduce(out=lame[:, 0:2], in_=lamt[:, :, :], axis=AX.X, op=ALU.add),
          reads=[t_lam], writes=[t_lam])
    fw.op(act, lambda b: b.activation(out=lame[:, 2:4], in_=lame[:, 0:2], func=AF.Exp), reads=[t_lam], writes=[t_lam])
    fw.op(dve, lambda b: b.tensor_tensor(out=lame[:, 0:1], in0=lame[:, 3:4], in1=lame[:, 2:3], op=ALU.subtract),
          reads=[t_lam], writes=[t_lam])
    fw.op(dve, lambda b: b.tensor_scalar(out=drv[:, 2:3], in0=lame[:, 0:1], scalar1=-LAM_INIT, scalar2=None,
                                          op0=ALU.add), reads=[t_lam], writes=[t_drv])
    eps_col = drv[:, 3:4]

    state = {"j": 0}

    def load_piece(key, idx, col0=0, ncol=4096):
        slot = wring.alloc()
        if state["j"] == 0:
            fw.dma(pool, slot.h[:, 0:ncol], w32[key][idx][:, col0:col0 + ncol], slot.ds, writes=[slot.t])
            ev = fw.dma(sp, wbf[key][idx][:, col0:col0 + ncol], slot.h[:, 0:ncol], ds_wb[key], reads=[slot.t])
            wtl[key].w = ev
        else:
            fw.dma(sp, slot.h[:, 0:ncol], wbf[key][idx][:, col0:col0 + ncol], slot.ds,
                   reads=[wtl[key]], writes=[slot.t])
        return slot

    def mm_group(out_ap, pairs, start=True, stop=True):
        def emit(b):
            ins = None
            n = len(pairs)
            for i, (l, r) in enumerate(pairs):
                ins = b.matmul(out_ap, lhsT=l, rhs=r, start=(start and i == 0), stop=(stop and i == n - 1))
            return ins
        return emit

    def proj_group(ps, pairs, common_reads, per_reads=None):
        if per_reads is None:
            fw.op(pe, mm_group(ps.h[:, :], pairs), reads=common_reads, writes=[ps.t])
        else:
            n = len(pairs)
            for i, (l, r) in enumerate(pairs):
                fw.op(pe, lambda b, l=l, r=r, i=i: b.matmul(ps.h[:, :], lhsT=l, rhs=r, start=(i == 0), stop=(i == n - 1)),
                      reads=list(common_reads) + list(per_reads[i]), writes=[ps.t])

    def preload_ln_table():
        fw.op(act, lambda b: b.activation(out=drv[:, 4:5], in_=drv[:, 3:4], func=AF.Ln), reads=[t_drv], writes=[t_dummy])

    def rstd_from(ps, inv_n):
        r = scr.alloc()
        fw.op(act, lambda b: b.activation(out=r.h[:, :], in_=ps.h[:, :], func=AF.Ln, bias=eps_col, scale=inv_n),
              reads=[ps.t, t_drv], writes=[r.t])
        fw.op(act, lambda b: b.activation(out=r.h[:, :], in_=r.h[:, :], func=AF.Exp, scale=-0.5),
              reads=[r.t], writes=[r.t])
        return r

    def norm_stats(xb):
        x = xbuf[xb]
        ps = psA.alloc()
        for dc in range(8):
            sq = sqring.alloc()
            fw.op(act, lambda b, sq=sq, dc=dc: b.activation(out=sq.h[:, :], in_=x[:, dc, :], func=AF.Square),
                  reads=[t_x[xb][dc]], writes=[sq.t])
            fw.op(pe, lambda b, sq=sq, dc=dc: b.matmul(ps.h[:, :], lhsT=ones_bf[:, :], rhs=sq.h[:, :],
                                                       start=(dc == 0), stop=(dc == 7)),
                  reads=[sq.t, t_cbf], writes=[ps.t])
        return rstd_from(ps, 1.0 / D)

    def norm_to_h(xb, gcol):
        x = xbuf[xb]
        r = norm_stats(xb)
        for dc in range(8):
            fw.op(dve, lambda b, dc=dc: b.scalar_tensor_tensor(
                out=hb[:, dc, :], in0=x[:, dc, :], scalar=prm[:, gcol + dc:gcol + dc + 1], in1=r.h[:, :],
                op0=ALU.mult, op1=ALU.mult), reads=[t_x[xb][dc], r.t, t_prm], writes=[t_h[dc]])

    def final_norm_store(j):
        xb = j % 2
        x = xbuf[xb]
        r = norm_stats(xb)
        for dc in range(8):
            fw.op(dve, lambda b, dc=dc: b.scalar_tensor_tensor(
                out=x[:, dc, :], in0=x[:, dc, :], scalar=prm[:, P_GF + dc:P_GF + dc + 1], in1=r.h[:, :],
                op0=ALU.mult, op1=ALU.mult), reads=[r.t, t_prm], writes=[t_x[xb][dc]])
        return fw.dma(pool, outT[:, j * TT:(j + 1) * TT].rearrange("(dc p) s -> p dc s", p=128), x[:, :, :],
                      ds_o[xb], reads=t_x[xb])

    def ffn_up(kgu, hook=None):
        for fg in range(NFG):
            slot = load_piece(kgu, fg)
            for f2 in range(2):
                fc = fg * 2 + f2
                g_ps = psA.alloc()
                u_ps = psA.alloc()
                prs = []
                for which in range(2):
                    prs.append([(slot.h[:, (which * 8 + dc) * 256 + f2 * 128:(which * 8 + dc) * 256 + f2 * 128 + 128],
                                 hb[:, dc, :]) for dc in range(8)])
                if fc == 0:
                    for dc in range(8):
                        for which, ps in ((0, g_ps), (1, u_ps)):
                            l, r = prs[which][dc]
                            fw.op(pe, lambda b, l=l, r=r, ps=ps, dc=dc: b.matmul(ps.h[:, :], lhsT=l, rhs=r,
                                                                                 start=(dc == 0), stop=(dc == 7)),
                                  reads=[slot.t, t_h[dc]], writes=[ps.t])
                else:
                    for which, ps in ((0, g_ps), (1, u_ps)):
                        fw.op(pe, mm_group(ps.h[:, :], prs[which]), reads=[slot.t] + t_h, writes=[ps.t])
                sg = scr.alloc()
                fw.op(act, lambda b, sg=sg, g_ps=g_ps: b.activation(out=sg.h[:, :], in_=g_ps.h[:, :], func=AF.Silu),
                      reads=[g_ps.t], writes=[sg.t])
                fw.op(dve, lambda b, sg=sg, u_ps=u_ps, fc=fc: b.tensor_tensor(
                    out=actb[:, fc, :], in0=sg.h[:, :], in1=u_ps.h[:, :], op=ALU.mult),
                    reads=[sg.t, u_ps.t], writes=[t_act[fc]])
            if fg == 1 and hook is not None:
                hook()

    def ffn_down(kd, xb):
        x = xbuf[xb]
        preload_ln_table()
        for half in range(2):
            ring = psB if half == 0 else psA
            accs = [ring.alloc() for _ in range(4)]
            for (fc0, fc1) in ((0, 8), (8, 16), (16, 22)):
                slot = load_piece(kd, half, fc0 * 512, (fc1 - fc0) * 512)
                for fc in range(fc0, fc1):
                    def emit(b, fc=fc, fc0=fc0, slot=slot, accs=accs):
                        ins = None
                        for q in range(4):
                            ins = b.matmul(accs[q].h[:, :],
                                           lhsT=slot.h[:, (fc - fc0) * 512 + q * 128:(fc - fc0) * 512 + q * 128 + 128],
                                           rhs=actb[:, fc, :], start=(fc == 0), stop=(fc == NFC - 1))
                        return ins
                    fw.op(pe, emit, reads=[slot.t, t_act[fc]], writes=[a.t for a in accs])
            for q in range(4):
                dco = half * 4 + q
                fw.op(dve, lambda b, q=q, dco=dco, accs=accs: b.scalar_tensor_tensor(
                    out=x[:, dco, :], in0=accs[q].h[:, :], scalar=0.5, in1=x[:, dco, :],
                    op0=ALU.mult, op1=ALU.add), reads=[accs[q].t], writes=[t_x[xb][dco]])

    def gn_front(src_ap, src_tiles):
        sq = sqring.alloc()
        fw.op(act, lambda b: b.activation(out=sq.h[:, :], in_=src_ap, func=AF.Square), reads=src_tiles, writes=[sq.t])
        return sq

    def gn_back(sq, inv_n, bdmat, ring, finals):
        ss = ring.alloc()
        fw.op(pe, lambda b: b.matmul(ss.h[:, :], lhsT=bdmat, rhs=sq.h[:, :], start=True, stop=True),
              reads=[sq.t, t_cbf], writes=[ss.t])
        r = rstd_from(ss, inv_n)
        for (dst_ap, src_ap, gain_ap, psl, reads, dst_tile) in finals:
            fw.op(dve, lambda b, dst_ap=dst_ap, src_ap=src_ap, gain_ap=gain_ap, psl=psl: b.scalar_tensor_tensor(
                out=dst_ap, in0=src_ap, scalar=gain_ap, in1=r.h[psl, :], op0=ALU.mult, op1=ALU.mult),
                reads=list(reads) + [r.t, t_prm, t_drv], writes=[dst_tile])

    ALLP = slice(0, 128)

    def mixer(xb, j):
        x = xbuf[xb]
        norm_to_h(xb, P_GM)
        items = [(which, hd) for which in range(2) for hd in range(4)]
        slots = {}
        pend = None
        for i, (which, hd) in enumerate(items):
            if which not in slots:
                slots[which] = load_piece("win", which)
            slot = slots[which]
            ring = psA if i % 2 == 0 else psB
            ps = ring.alloc()
            pairs = [(slot.h[:, dc * 512 + hd * 128:dc * 512 + hd * 128 + 128], hb[:, dc, :]) for dc in range(8)]
            if i == 0:
                proj_group(ps, pairs, [slot.t], per_reads=[[t_h[dc]] for dc in range(8)])
            else:
                proj_group(ps, pairs, [slot.t] + t_h)
            sq = gn_front(ps.h[:, :], [ps.t])
            if which == 0:
                finals = [(qz[0:64, hd, 0, :], ps.h[0:64, :], drv[0:64, 0:1], slice(0, 64), [ps.t], t_q[hd]),
                          (qz[64:128, hd, 1, :], ps.h[64:128, :], drv[64:128, 0:1], slice(64, 128), [ps.t], t_q[hd])]
            else:
                finals = [(kT[:, hd, j * TT:(j + 1) * TT], ps.h[:, :], prm[:, P_GK:P_GK + 1], ALLP, [ps.t], t_k[hd][j])]
            if pend is not None:
                gn_back(*pend)
            pend = (sq, 1.0 / 64, bd_bf[:, :], ring, finals)
        slot = load_piece("win", 2)
        for tb in range(4):
            ps = psA.alloc()
            pairs = [(hb[:, dc, tb * 128:(tb + 1) * 128], slot.h[:, dc * 512:(dc + 1) * 512]) for dc in range(8)]
            proj_group(ps, pairs, [slot.t] + t_h)
            if tb == 0:
                gn_back(*pend)
                pend = None
            kb = j * 4 + tb
            fw.op(act, lambda b, ps=ps, kb=kb: b.activation(out=Vb[:, kb, :], in_=ps.h[:, :], func=AF.Copy),
                  reads=[ps.t], writes=[t_v[kb]])
        sl_b = load_piece("win", 3)
        sl_c = load_piece("win", 4)
        sl_h = load_piece("win", 5)
        pend = None
        for cc in range(4):
            ring = psA if cc % 2 == 0 else psB
            pss = []
            for sl in (sl_b, sl_c, sl_h):
                ps = ring.alloc()
                pairs = [(sl.h[:, dc * 512 + cc * 128:dc * 512 + cc * 128 + 128], hb[:, dc, :]) for dc in range(8)]
                proj_group(ps, pairs, [sl.t] + t_h)
                pss.append(ps)
            if pend is not None:
                gn_back(*pend)
                pend = None
            gb_ps, gc_ps, hc_ps = pss
            hcs = scr.alloc()
            fw.op(act, lambda b, hcs=hcs, hc_ps=hc_ps: b.activation(out=hcs.h[:, :], in_=hc_ps.h[:, :], func=AF.Copy),
                  reads=[hc_ps.t], writes=[hcs.t])
            u = ubuf[cc]
            fw.op(dve, lambda b, u=u, hcs=hcs, gc_ps=gc_ps: b.tensor_tensor(
                out=u[:, 2:TT + 2], in0=hcs.h[:, :], in1=gc_ps.h[:, :], op=ALU.mult),
                reads=[hcs.t, gc_ps.t], writes=[t_u[cc]])
            y = scr.alloc()
            cw = lambda k, cc=cc: prm[:, P_CW + cc * 3 + k:P_CW + cc * 3 + k + 1]
            fw.op(dve, lambda b, u=u, y=y, cw=cw: b.tensor_scalar(out=y.h[:, :], in0=u[:, 2:TT + 2], scalar1=cw(2),
                                                                   scalar2=None, op0=ALU.mult),
                  reads=[t_u[cc], t_prm], writes=[y.t])
            fw.op(dve, lambda b, u=u, y=y, cw=cw: b.scalar_tensor_tensor(
                out=y.h[:, :], in0=u[:, 1:TT + 1], scalar=cw(1), in1=y.h[:, :], op0=ALU.mult, op1=ALU.add),
                reads=[t_u[cc], t_prm], writes=[y.t])
            fw.op(dve, lambda b, u=u, y=y, cw=cw: b.scalar_tensor_tensor(
                out=y.h[:, :], in0=u[:, 0:TT], scalar=cw(0), in1=y.h[:, :], op0=ALU.mult, op1=ALU.add),
                reads=[t_u[cc], t_prm], writes=[y.t])
            fw.op(dve, lambda b, u=u: b.tensor_copy(out=u[:, 0:2], in_=u[:, TT:TT + 2]), writes=[t_u[cc]])
            fw.op(dve, lambda b, y=y, gb_ps=gb_ps: b.tensor_tensor(out=y.h[:, :], in0=y.h[:, :], in1=gb_ps.h[:, :],
                                                                    op=ALU.mult), reads=[gb_ps.t], writes=[y.t])
            sq = gn_front(y.h[:, :], [y.t])
            pend = (sq, 1.0 / 64, bd_bf[:, :], ring,
                    [(mixed[:, 4 + cc, :], y.h[:, :], prm[:, P_GCN + cc:P_GCN + cc + 1], ALLP, [y.t], t_mx[4 + cc])])
        gn_back(*pend)
        pend = None
        nkb = 4 * j + 4

        def att_p1(hd, acc, zz):
            ls, cs = [], []
            for m in range(2):
                l = scr.alloc()
                fw.op(act, lambda b, l=l, m=m: b.activation(out=l.h[:, :], in_=zz[m].h[:, :], func=AF.Copy),
                      reads=[zz[m].t], writes=[l.t])
                ls.append(l)
            for m in range(2):
                c = scr.alloc()
                fw.op(dve, lambda b, c=c, m=m: b.tensor_copy(out=c.h[:, :], in_=acc[m].h[:, :]),
                      reads=[acc[m].t], writes=[c.t])
                cs.append(c)
            return (hd, ls, cs)

        def att_p2(st):
            hd, ls, cs = st
            for m in range(2):
                fw.op(dve, lambda b, l=ls[m]: b.reciprocal(out=l.h[:, :], in_=l.h[:, :]), reads=[ls[m].t], writes=[ls[m].t])
            for m in range(2):
                fw.op(dve, lambda b, c=cs[m], l=ls[m]: b.tensor_tensor(out=c.h[:, :], in0=c.h[:, :], in1=l.h[:, :],
                                                                        op=ALU.mult), reads=[ls[m].t], writes=[cs[m].t])
            fw.op(dve, lambda b: b.scalar_tensor_tensor(
                out=cs[0].h[:, :], in0=cs[1].h[:, :], scalar=drv[:, 2:3], in1=cs[0].h[:, :], op0=ALU.mult, op1=ALU.add),
                reads=[cs[1].t, t_drv], writes=[cs[0].t])
            a = cs[0]
            sq = sqring.alloc()
            fw.op(dve, lambda b: b.tensor_tensor(out=sq.h[:, :], in0=a.h[:, :], in1=a.h[:, :], op=ALU.mult),
                  reads=[a.t], writes=[sq.t])
            return (hd, a, sq)

        def att_p3(st, ring):
            hd, a, sq = st
            gn_back(sq, 1.0 / 128, ones_bf[:, :], ring,
                    [(mixed[:, hd, :], a.h[:, :], drv[:, 1:2], ALLP, [a.t], t_mx[hd])])

        st1 = None
        st2 = None
        for hd in range(4):
            acc = [psB.alloc(), psB.alloc()]
            zz = [psB.alloc(), psB.alloc()]
            prev = None
            for kb in range(nkb + 1):
                if kb < nkb:
                    rel = kb - 4 * j
                    koff = max(0, rel) * 128
                    s3, ta, tb_ = pairA.alloc()

                    def emit_s(b, s3=s3, kb=kb, koff=koff, hd=hd):
                        ins = None
                        for m in range(2):
                            ins = b.matmul(s3[:, m, koff:TT], lhsT=kT[:, hd, kb * 128:(kb + 1) * 128],
                                           rhs=qz[:, hd, m, koff:TT], start=True, stop=True)
                        return ins
                    fw.op(pe, emit_s, reads=[t_k[hd][kb // 4], t_q[hd]], writes=[ta, tb_])
                    p = pring.alloc()
                    bcol = 128 + hd * NREL + (rel + NREL - 4)
                    fw.op(act, lambda b, p=p, s3=s3, koff=koff, bcol=bcol: b.activation(
                        out=p.h[:, :, koff:TT], in_=s3[:, :, koff:TT], func=AF.Exp, bias=cst[:, bcol:bcol + 1], scale=1.0),
                        reads=[ta, tb_, t_cst], writes=[p.t])
                    if rel >= 0:
                        fw.op(dve, lambda b, p=p, koff=koff: b.tensor_tensor(
                            out=p.h[:, :, koff:koff + 128], in0=p.h[:, :, koff:koff + 128], in1=tri2[:, :, :], op=ALU.mult),
                            reads=[t_cbf], writes=[p.t])
                    cur = (kb, koff, p)
                else:
                    cur = None
                if kb == 0 and st1 is not None:
                    st2 = att_p2(st1)
                    st1 = None
                if prev is not None:
                    pkb, pkoff, pp = prev

                    def emit_pv(b, kb=pkb, koff=pkoff, p=pp, hd=hd, acc=acc, zz=zz):
                        ins = None
                        for m in range(2):
                            b.matmul(acc[m].h[:, koff:TT], lhsT=Vb[:, kb, hd * 128:(hd + 1) * 128], rhs=p.h[:, m, koff:TT],
                                     start=(kb == 0), stop=(kb == nkb - 1))
                            ins = b.matmul(zz[m].h[:, koff:TT], lhsT=ones_bf[:, :], rhs=p.h[:, m, koff:TT],
                                           start=(kb == 0), stop=(kb == nkb - 1))
                        return ins
                    fw.op(pe, emit_pv, reads=[pp.t, t_v[pkb], t_cbf], writes=[acc[0].t, acc[1].t, zz[0].t, zz[1].t])
                prev = cur
                if kb == nkb - 1 and st2 is not None:
                    att_p3(st2, psA)
                    st2 = None
            st1 = att_p1(hd, acc, zz)
        st2 = att_p2(st1)
        sl_o = [load_piece("wout", 0), load_piece("wout", 1)]
        corder = [4, 5, 6, 7, 0, 1, 2]
        obanks = [psA.alloc() for _ in range(4)] + [psB.alloc() for _ in range(3)]
        for dco in range(7):
            slot = sl_o[dco // 4]
            q = dco % 4
            ps = obanks[dco]
            pairs = [(slot.h[:, c * 512 + q * 128:c * 512 + q * 128 + 128], mixed[:, c, :]) for c in corder]
            fw.op(pe, mm_group(ps.h[:, :], pairs, start=True, stop=False),
                  reads=[slot.t] + [t_mx[c] for c in corder], writes=[ps.t])
        att_p3(st2, psB)
        for dco in range(7):
            slot = sl_o[dco // 4]
            q = dco % 4
            ps = obanks[dco]
            fw.op(pe, mm_group(ps.h[:, :], [(slot.h[:, 3 * 512 + q * 128:3 * 512 + q * 128 + 128], mixed[:, 3, :])],
                               start=False, stop=True), reads=[slot.t, t_mx[3]], writes=[ps.t])
            fw.op(dve, lambda b, ps=ps, dco=dco: b.tensor_tensor(out=x[:, dco, :], in0=ps.h[:, :], in1=x[:, dco, :],
                                                                  op=ALU.add), reads=[ps.t], writes=[t_x[xb][dco]])
        ps = psA.alloc()
        pairs = [(sl_o[1].h[:, c * 512 + 3 * 128:c * 512 + 3 * 128 + 128], mixed[:, c, :]) for c in range(8)]
        proj_group(ps, pairs, [sl_o[1].t] + t_mx)
        fw.op(dve, lambda b, ps=ps: b.tensor_tensor(out=x[:, 7, :], in0=ps.h[:, :], in1=x[:, 7, :], op=ALU.add),
              reads=[ps.t], writes=[t_x[xb][7]])

    def load_x(j):
        xb = j % 2
        fw.dma(sp, xbuf[xb][:, :, :], xT[:, j * TT:(j + 1) * TT].rearrange("(dc p) s -> p dc s", p=128), ds_x[xb],
               writes=t_x[xb])

    finals = []
    load_x(0)
    norm_to_h(0, P_G1)
    for j in range(NCH):
        state["j"] = j
        xb = j % 2
        hook = None
        if j > 0:
            hook = lambda j=j: finals.append(final_norm_store(j - 1))
        ffn_up("wgu1", hook)
        if j + 1 < NCH:
            load_x(j + 1)
        ffn_down("wd1", xb)
        mixer(xb, j)
        norm_to_h(xb, P_G2)
        ffn_up("wgu2")
        if j + 1 < NCH:
            norm_to_h((j + 1) % 2, P_G1)
        ffn_down("wd2", xb)
    finals.append(final_norm_store(NCH - 1))
    fw.finish(finals[-2:])
    return nc


def _consts():
    cst = np.zeros((128, 256), np.float32)
    ki = np.arange(128)
    cst[:, 0:128] = (ki[:, None] <= ki[None, :]).astype(np.float32)
    for h in range(4):
        for r in range(NREL):
            rel = r - (NREL - 4)
            cst[:, 128 + h * NREL + r] = SLOPES[h] * (ki + 128.0 * rel - 256.0)
    return cst


def _layout_weights(inp):
    f32 = lambda a: np.ascontiguousarray(np.asarray(a, dtype=np.float32))
    out = {}
    for i, tag in ((1, "ffn1"), (2, "ffn2")):
        wg = f32(inp[f"{tag}_w_gate"])[0].reshape(8, 128, NFG, 256)
        wu = f32(inp[f"{tag}_w_up"])[0].reshape(8, 128, NFG, 256)
        gu = np.stack([wg, wu], axis=0)
        out[f"wgu{i}"] = np.ascontiguousarray(gu.transpose(3, 2, 0, 1, 4)).reshape(NFG, 128, 4096)
        wd = f32(inp[f"{tag}_w_down"])[0].reshape(NFC, 128, 2, 512)
        out[f"wd{i}"] = np.ascontiguousarray(wd.transpose(2, 1, 0, 3)).reshape(2, 128, NFC * 512)
    win = f32(inp["w_in"])[0].reshape(8, 128, 6, 512)
    out["win"] = np.ascontiguousarray(win.transpose(2, 1, 0, 3)).reshape(6, 128, 4096)
    wo = f32(inp["w_out"])[0].reshape(8, 128, 2, 512)
    out["wout"] = np.ascontiguousarray(wo.transpose(2, 1, 0, 3)).reshape(2, 128, 4096)
    prm = np.zeros((128, NPRM), np.float32)
    for col, key in ((P_G1, "ffn1_norm"), (P_GM, "mix_norm"), (P_G2, "ffn2_norm"), (P_GF, "final_norm")):
        prm[:, col:col + 8] = f32(inp[key])[0].reshape(8, 128).T
    prm[:, P_GQ] = np.tile(f32(inp["q_norm"])[0], 2)
    prm[:, P_GK] = np.tile(f32(inp["k_norm"])[0], 2)
    prm[:, P_GSUB] = f32(inp["attn_subln"])[0]
    prm[:, P_GCN:P_GCN + 4] = f32(inp["conv_norm"])[0].reshape(4, 128).T
    cw = f32(inp["conv_w"])[0]
    for cc in range(4):
        for k in range(3):
            prm[:, P_CW + cc * 3 + k] = cw[k, cc * 128:(cc + 1) * 128]
    for i, key in enumerate(("lambda_q1", "lambda_k1", "lambda_q2", "lambda_k2")):
        prm[:, P_LAM + i * 64:P_LAM + (i + 1) * 64] = f32(inp[key])[0][None, :]
    out["prm"] = prm
    out["cst"] = _consts()
    return out


_NC_CACHE = {}


def _get_nc(nch):
    if nch not in _NC_CACHE:
        _NC_CACHE[nch] = build_nc(nch)
    return _NC_CACHE[nch]


def kernel(**inputs):
    x = np.asarray(inputs["x"], dtype=np.float32)
    B, S, _ = x.shape
    nch = S // TT
    shared = _layout_weights(inputs)
    in_maps = []
    for b in range(B):
        m = dict(shared)
        m["xT"] = np.ascontiguousarray(x[b].T)
        in_maps.append(m)
    nc = build_nc(nch)
    res = run_bass_kernel_spmd(nc, in_maps, core_ids=list(range(B)))
    out = np.stack([np.ascontiguousarray(np.asarray(r["outT"]).T) for r in res.results], axis=0)
    return out.astype(np.float32)
```

```python
import math
import numpy as np
import concourse.bass as bass
import concourse.mybir as mybir
from concourse.bass_utils import run_bass_kernel_spmd

F32 = mybir.dt.float32
BF16 = mybir.dt.bfloat16
ALU = mybir.AluOpType
AF = mybir.ActivationFunctionType
AX = mybir.AxisListType

D = 1024
DFF = 2816
NFC = 22
NFG = 11
TT = 512
EPS = 1e-6
LAM_INIT = 0.8 - 0.6 * math.exp(-0.3 * 0)
SLOPES = [2.0 ** (-8.0 * (i + 1) / 4) for i in range(4)]
NREL = 32

P_G1, P_GM, P_G2, P_GF = 0, 8, 16, 24
P_GQ, P_GK, P_GSUB = 32, 33, 34
P_GCN = 35
P_CW = 39
P_LAM = 51
NPRM = P_LAM + 256


class Tl:
    __slots__ = ("name", "w", "r", "excl")

    def __init__(self, name, excl=False):
        self.name = name
        self.w = None
        self.r = []
        self.excl = excl


class DSem:
    def __init__(self, nc, name):
        self.sem = nc.alloc_semaphore(name)
        self.key = name
        self.count = 0


class Eng:
    def __init__(self, nc, name, b):
        self.name = name
        self.b = b
        self.sem = nc.alloc_semaphore("s_" + name)
        self.key = "s_" + name
        self.n = 0
        self.seen = {}
        self.q = []

    def wait(self, ev):
        key, sem, val = ev
        if self.seen.get(key, 0) >= val:
            return
        self.seen[key] = val
        self.q.append(lambda b, sem=sem, val=val: b.wait_ge(sem, val))


class FW:
    def __init__(self, nc):
        self.nc = nc
        self.pe = Eng(nc, "pe", nc.tensor)
        self.act = Eng(nc, "act", nc.scalar)
        self.dve = Eng(nc, "dve", nc.vector)
        self.pool = Eng(nc, "pool", nc.gpsimd)
        self.sp = Eng(nc, "sp", nc.sync)
        self.nds = 0

    def dsem(self, name=None):
        self.nds += 1
        return DSem(self.nc, name or f"d{self.nds}")

    def _deps(self, eng, reads, writes):
        deps = []
        for t in reads:
            if t.w is not None:
                deps.append(t.w)
            if t.excl:
                deps.extend(t.r)
        for t in writes:
            if t.w is not None:
                deps.append(t.w)
            deps.extend(t.r)
        if eng is self.pe:
            deps = [d for d in deps if d[0] != self.pe.key]
        return deps

    def op(self, eng, emit, reads=(), writes=()):
        for d in self._deps(eng, reads, writes):
            eng.wait(d)
        eng.n += 1
        sem = eng.sem
        eng.q.append(lambda b, emit=emit, sem=sem: emit(b).then_inc(sem, 1))
        ev = (eng.key, eng.sem, eng.n)
        for t in reads:
            t.r.append(ev)
        for t in writes:
            t.w = ev
            t.r = []
        return ev

    def dma(self, eng, out, in_, ds, reads=(), writes=()):
        for d in self._deps(eng, reads, writes):
            eng.wait(d)
        ds.count += 16
        sem = ds.sem
        eng.q.append(lambda b, out=out, in_=in_, sem=sem: b.dma_start(out=out, in_=in_).then_inc(sem, 16))
        ev = (ds.key, ds.sem, ds.count)
        for t in reads:
            t.r.append(ev)
        for t in writes:
            t.w = ev
            t.r = []
        return ev

    def finish(self, final_events=()):
        for ev in final_events:
            self.sp.wait(ev)
        with self.nc.Block() as block:
            @block.sync
            def _(e):
                for f in self.sp.q:
                    f(e)

            @block.tensor
            def _(e):
                for f in self.pe.q:
                    f(e)

            @block.scalar
            def _(e):
                for f in self.act.q:
                    f(e)

            @block.vector
            def _(e):
                for f in self.dve.q:
                    f(e)

            @block.gpsimd
            def _(e):
                for f in self.pool.q:
                    f(e)


class Buf:
    __slots__ = ("h", "t", "ds")

    def __init__(self, h, t, ds=None):
        self.h = h
        self.t = t
        self.ds = ds


class BankView:
    def __init__(self, t, m):
        self.t = t
        self.m = m

    def __getitem__(self, idx):
        p, c = idx
        return self.t[p, self.m, c]


class Ring:
    def __init__(self, bufs):
        self.bufs = bufs
        self.i = 0

    def alloc(self):
        b = self.bufs[self.i % len(self.bufs)]
        self.i += 1
        return b


def build_nc(NCH, dbg=False):
    S = NCH * TT
    NKB = S // 128
    nc = bass.Bass("TRN2", target_bir_lowering=False)
    fw = FW(nc)
    pe, act, dve, pool, sp = fw.pe, fw.act, fw.dve, fw.pool, fw.sp

    xT = nc.dram_tensor("xT", [D, S], F32, kind="ExternalInput").ap()
    outT = nc.dram_tensor("outT", [D, S], F32, kind="ExternalOutput").ap()
    prm_d = nc.dram_tensor("prm", [128, NPRM], F32, kind="ExternalInput").ap()
    cst_d = nc.dram_tensor("cst", [128, 256], F32, kind="ExternalInput").ap()
    w32 = {}
    wbf = {}
    wshape = {"wgu1": [NFG, 128, 4096], "wd1": [2, 128, NFC * 512], "win": [6, 128, 4096],
              "wout": [2, 128, 4096], "wgu2": [NFG, 128, 4096], "wd2": [2, 128, NFC * 512]}
    for k, shp in wshape.items():
        w32[k] = nc.dram_tensor(k, shp, F32, kind="ExternalInput").ap()
        wbf[k] = nc.dram_tensor(k + "_bf", shp, BF16, kind="Internal").ap()
    wtl = {k: Tl(f"wt_{k}") for k in wshape}
    ds_wb = {k: fw.dsem(f"dwb_{k}") for k in wshape}

    def sb(name, shape, dt):
        return nc.alloc_sbuf_tensor(name, shape, dt)

    xbuf = [sb(f"x{i}", [128, 8, TT], F32) for i in range(2)]
    t_x = [[Tl(f"x{i}_{dc}") for dc in range(8)] for i in range(2)]
    ds_x = [fw.dsem(f"dx{i}") for i in range(2)]
    ds_o = [fw.dsem(f"do{i}") for i in range(2)]
    hA = (sb("hA", [128, 8, TT], BF16), [Tl(f"hA{dc}") for dc in range(8)])
    hB = (sb("hB", [128, 8, TT], BF16), [Tl(f"hB{dc}") for dc in range(8)])
    actb = sb("act", [128, NFC, TT], BF16)
    t_act = [Tl(f"act{fc}") for fc in range(NFC)]
    qz = sb("qz", [128, 4, 2, TT], BF16)
    t_q = [Tl(f"q{h}") for h in range(4)]
    kT = sb("kT", [128, 4, S], BF16)
    t_k = [[Tl(f"k{h}_{j}") for j in range(NCH)] for h in range(4)]
    Vb = sb("V", [128, NKB, 512], BF16)
    t_v = [Tl(f"v{kb}") for kb in range(NKB)]
    mixed = sb("mixed", [128, 8, TT], BF16)
    t_mx = [Tl(f"mx{c}") for c in range(8)]
    ucar = sb("ucar", [128, 4, 2], F32)
    t_u = [Tl(f"u{cc}") for cc in range(4)]
    wring = Ring([Buf(sb(f"wr{i}", [128, 4096], BF16), Tl(f"wr{i}"), fw.dsem(f"dwr{i}")) for i in range(4)])
    pring = Ring([Buf(sb(f"pr{i}", [128, 2, TT], BF16), Tl(f"pr{i}")) for i in range(3)])
    sqring = Ring([Buf(sb(f"sq{i}", [128, TT], BF16), Tl(f"sq{i}")) for i in range(2)])
    sqn = Ring([Buf(sb(f"sqn{i}", [128, TT], BF16), Tl(f"sqn{i}")) for i in range(2)])
    scr = Ring([Buf(sb(f"sc{i}", [128, TT + 2], F32), Tl(f"sc{i}"), fw.dsem(f"dsc{i}")) for i in range(6)])
    prm = sb("prm_s", [128, P_LAM], F32)
    t_prm = Tl("prm")
    cst = sb("cst_s", [128, 256], F32)
    t_cst = Tl("cst")
    drv = sb("drv", [128, 8], F32)
    t_drv = Tl("drv")
    t_dummy = Tl("dummy")
    lamt = sb("lamt", [128, 2, 64], F32)
    lame = sb("lame", [128, 4], F32)
    t_lam = Tl("lam")
    ones_bf = sb("ones_bf", [128, 128], BF16)
    bd_bf = sb("bd_bf", [128, 128], BF16)
    tri2 = sb("tri2", [128, 2, 128], BF16)
    t_cbf = Tl("cbf")

    pa = [nc.alloc_psum_tensor(f"pa{i}", [128, 2, TT], F32) for i in range(2)]
    pb = [nc.alloc_psum_tensor(f"pb{i}", [128, TT], F32) for i in range(4)]
    bankA = [Buf(BankView(pa[i // 2], i % 2), Tl(f"psA{i}", excl=True)) for i in range(4)]
    bankB = [Buf(pb[i], Tl(f"psB{i}", excl=True)) for i in range(4)]
    psA = Ring(bankA)
    psB = Ring(bankB)
    pairA = Ring([(pa[0], bankA[0].t, bankA[1].t), (pa[1], bankA[2].t, bankA[3].t)])

    fw.dma(sp, prm[:, :], prm_d[:, 0:P_LAM], fw.dsem("dprm"), writes=[t_prm])
    lamb = scr.alloc()
    fw.dma(sp, lamb.h[:, 0:256], prm_d[:, P_LAM:P_LAM + 256], fw.dsem("dlam"), writes=[lamb.t])
    fw.dma(sp, cst[:, :], cst_d, fw.dsem("dcst"), writes=[t_cst])
    fw.op(dve, lambda b: b.memset(ones_bf[:, :], 1.0), writes=[t_cbf])
    fw.op(dve, lambda b: b.memset(bd_bf[:, :], 0.0), writes=[t_cbf])
    fw.op(dve, lambda b: b.memset(bd_bf[0:64, 0:64], 1.0), writes=[t_cbf])
    fw.op(dve, lambda b: b.memset(bd_bf[64:128, 64:128], 1.0), writes=[t_cbf])
    for m in range(2):
        fw.op(dve, lambda b, m=m: b.tensor_copy(out=tri2[:, m, :], in_=cst[:, 0:128]), reads=[t_cst], writes=[t_cbf])
    for cc in range(4):
        fw.op(dve, lambda b, cc=cc: b.memset(ucar[:, cc, :], 0.0), writes=[t_u[cc]])
    for hd in range(4):
        fw.op(dve, lambda b, hd=hd: b.memset(qz[:, hd, :, :], 0.0), writes=[t_q[hd]])
    fw.op(dve, lambda b: b.tensor_scalar(out=drv[:, 0:1], in0=prm[:, P_GQ:P_GQ + 1], scalar1=0.125, scalar2=None,
                                          op0=ALU.mult), reads=[t_prm], writes=[t_drv])
    fw.op(dve, lambda b: b.tensor_scalar(out=drv[:, 1:2], in0=prm[:, P_GSUB:P_GSUB + 1], scalar1=1.0 - LAM_INIT,
                                          scalar2=None, op0=ALU.mult), reads=[t_prm], writes=[t_drv])
    fw.op(dve, lambda b: b.memset(drv[:, 3:4], EPS), writes=[t_drv])
    fw.op(dve, lambda b: b.tensor_tensor(out=lamt[:, 0, :], in0=lamb.h[:, 0:64],
                                          in1=lamb.h[:, 64:128], op=ALU.mult), reads=[lamb.t], writes=[t_lam])
    fw.op(dve, lambda b: b.tensor_tensor(out=lamt[:, 1, :], in0=lamb.h[:, 128:192],
                                          in1=lamb.h[:, 192:256], op=ALU.mult), reads=[lamb.t], writes=[t_lam])
    fw.op(dve, lambda b: b.tensor_reduce(out=lame[:, 0:2], in_=lamt[:, :, :], axis=AX.X, op=ALU.add),
          reads=[t_lam], writes=[t_lam])
    fw.op(act, lambda b: b.activation(out=lame[:, 2:4], in_=lame[:, 0:2], func=AF.Exp), reads=[t_lam], writes=[t_lam])
    fw.op(dve, lambda b: b.tensor_tensor(out=lame[:, 0:1], in0=lame[:, 3:4], in1=lame[:, 2:3], op=ALU.subtract),
          reads=[t_lam], writes=[t_lam])
    fw.op(dve, lambda b: b.tensor_scalar(out=drv[:, 2:3], in0=lame[:, 0:1], scalar1=-LAM_INIT, scalar2=None,
                                          op0=ALU.add), reads=[t_lam], writes=[t_drv])
    eps_col = drv[:, 3:4]

    seen_pieces = set()

    def load_piece(key, idx, col0=0, ncol=4096):
        slot = wring.alloc()
        pk = (key, idx, col0)
        if pk not in seen_pieces:
            seen_pieces.add(pk)
            fw.dma(pool, slot.h[:, 0:ncol], w32[key][idx][:, col0:col0 + ncol], slot.ds, writes=[slot.t])
            ev = fw.dma(sp, wbf[key][idx][:, col0:col0 + ncol], slot.h[:, 0:ncol], ds_wb[key], reads=[slot.t])
            wtl[key].w = ev
        else:
            fw.dma(sp, slot.h[:, 0:ncol], wbf[key][idx][:, col0:col0 + ncol], slot.ds,
                   reads=[wtl[key]], writes=[slot.t])
        return slot

    def mm_group(out_ap, pairs, start=True, stop=True):
        def emit(b):
            ins = None
            n = len(pairs)
            for i, (l, r) in enumerate(pairs):
                ins = b.matmul(out_ap, lhsT=l, rhs=r, start=(start and i == 0), stop=(stop and i == n - 1))
            return ins
        return emit

    def proj_group(ps, pairs, common_reads, per_reads=None):
        if per_reads is None:
            fw.op(pe, mm_group(ps.h[:, 0:TT], pairs), reads=common_reads, writes=[ps.t])
        else:
            n = len(pairs)
            for i, (l, r) in enumerate(pairs):
                fw.op(pe, lambda b, l=l, r=r, i=i: b.matmul(ps.h[:, 0:TT], lhsT=l, rhs=r, start=(i == 0), stop=(i == n - 1)),
                      reads=list(common_reads) + list(per_reads[i]), writes=[ps.t])

    def preload_ln_table():
        fw.op(act, lambda b: b.activation(out=drv[:, 4:5], in_=drv[:, 3:4], func=AF.Ln), reads=[t_drv], writes=[t_dummy])

    def rstd_from(ps, inv_n):
        r = scr.alloc()
        fw.op(act, lambda b: b.activation(out=r.h[:, 0:TT], in_=ps.h[:, 0:TT], func=AF.Ln, bias=eps_col, scale=inv_n),
              reads=[ps.t, t_drv], writes=[r.t])
        fw.op(act, lambda b: b.activation(out=r.h[:, 0:TT], in_=r.h[:, 0:TT], func=AF.Exp, scale=-0.5),
              reads=[r.t], writes=[r.t])
        return r

    def norm_stats(xb):
        x = xbuf[xb]
        ps = psA.alloc()
        for dc in range(8):
            sq = sqn.alloc()
            fw.op(act, lambda b, sq=sq, dc=dc: b.activation(out=sq.h[:, 0:TT], in_=x[:, dc, :], func=AF.Square),
                  reads=[t_x[xb][dc]], writes=[sq.t])
            fw.op(pe, lambda b, sq=sq, dc=dc: b.matmul(ps.h[:, 0:TT], lhsT=ones_bf[:, :], rhs=sq.h[:, 0:TT],
                                                       start=(dc == 0), stop=(dc == 7)),
                  reads=[sq.t, t_cbf], writes=[ps.t])
        return rstd_from(ps, 1.0 / D)

    def norm_to_h(xb, gcol, hbuf):
        hb, t_h = hbuf
        x = xbuf[xb]
        r = norm_stats(xb)
        for dc in range(8):
            fw.op(dve, lambda b, dc=dc: b.scalar_tensor_tensor(
                out=hb[:, dc, :], in0=x[:, dc, :], scalar=prm[:, gcol + dc:gcol + dc + 1], in1=r.h[:, 0:TT],
                op0=ALU.mult, op1=ALU.mult), reads=[t_x[xb][dc], r.t, t_prm], writes=[t_h[dc]])

    def final_norm_store(j):
        xb = j % 2
        x = xbuf[xb]
        r = norm_stats(xb)
        evs = []
        outs = [scr.alloc() for _ in range(5)]
        for dc in range(8):
            o = outs[dc % 5]
            fw.op(dve, lambda b, dc=dc, o=o: b.scalar_tensor_tensor(
                out=o.h[:, 0:TT], in0=x[:, dc, :], scalar=prm[:, P_GF + dc:P_GF + dc + 1], in1=r.h[:, 0:TT],
                op0=ALU.mult, op1=ALU.mult), reads=[t_x[xb][dc], r.t, t_prm], writes=[o.t])
            evs.append(fw.dma(pool, outT[dc * 128:(dc + 1) * 128, j * TT:(j + 1) * TT], o.h[:, 0:TT], o.ds, reads=[o.t]))
        return evs

    def ffn_up(kgu, hbuf, hook=None):
        hb, t_h = hbuf
        for fg in range(NFG):
            slot = load_piece(kgu, fg)
            for f2 in range(2):
                fc = fg * 2 + f2
                g_ps = psA.alloc()
                u_ps = psA.alloc()
                prs = []
                for which in range(2):
                    prs.append([(slot.h[:, (which * 8 + dc) * 256 + f2 * 128:(which * 8 + dc) * 256 + f2 * 128 + 128],
                                 hb[:, dc, :]) for dc in range(8)])
                if fc == 0:
                    for dc in range(8):
                        for which, ps in ((0, g_ps), (1, u_ps)):
                            l, r = prs[which][dc]
                            fw.op(pe, lambda b, l=l, r=r, ps=ps, dc=dc: b.matmul(ps.h[:, 0:TT], lhsT=l, rhs=r,
                                                                                 start=(dc == 0), stop=(dc == 7)),
                                  reads=[slot.t, t_h[dc]], writes=[ps.t])
                else:
                    for which, ps in ((0, g_ps), (1, u_ps)):
                        fw.op(pe, mm_group(ps.h[:, 0:TT], prs[which]), reads=[slot.t] + t_h, writes=[ps.t])
                sg = scr.alloc()
                fw.op(act, lambda b, sg=sg, g_ps=g_ps: b.activation(out=sg.h[:, 0:TT], in_=g_ps.h[:, 0:TT], func=AF.Silu),
                      reads=[g_ps.t], writes=[sg.t])
                fw.op(dve, lambda b, sg=sg, u_ps=u_ps, fc=fc: b.tensor_tensor(
                    out=actb[:, fc, :], in0=sg.h[:, 0:TT], in1=u_ps.h[:, 0:TT], op=ALU.mult),
                    reads=[sg.t, u_ps.t], writes=[t_act[fc]])
            if fg == 1 and hook is not None:
                hook()

    def ffn_down(kd, xb, hook=None):
        x = xbuf[xb]
        preload_ln_table()
        for half in range(2):
            ring = psB if half == 0 else psA
            accs = [ring.alloc() for _ in range(4)]
            for (fc0, fc1) in ((0, 8), (8, 16), (16, 22)):
                slot = load_piece(kd, half, fc0 * 512, (fc1 - fc0) * 512)
                for fc in range(fc0, fc1):
                    def emit(b, fc=fc, fc0=fc0, slot=slot, accs=accs):
                        ins = None
                        for q in range(4):
                            ins = b.matmul(accs[q].h[:, 0:TT],
                                           lhsT=slot.h[:, (fc - fc0) * 512 + q * 128:(fc - fc0) * 512 + q * 128 + 128],
                                           rhs=actb[:, fc, :], start=(fc == 0), stop=(fc == NFC - 1))
                        return ins
                    fw.op(pe, emit, reads=[slot.t, t_act[fc]], writes=[a.t for a in accs])
                if half == 0 and fc0 == 0 and hook is not None:
                    hook()
            for q in range(4):
                dco = half * 4 + q
                fw.op(dve, lambda b, q=q, dco=dco, accs=accs: b.scalar_tensor_tensor(
                    out=x[:, dco, :], in0=accs[q].h[:, 0:TT], scalar=0.5, in1=x[:, dco, :],
                    op0=ALU.mult, op1=ALU.add), reads=[accs[q].t], writes=[t_x[xb][dco]])

    def gn_front(src_ap, src_tiles):
        sq = sqring.alloc()
        fw.op(act, lambda b: b.activation(out=sq.h[:, 0:TT], in_=src_ap, func=AF.Square), reads=src_tiles, writes=[sq.t])
        return sq

    def gn_back(sq, inv_n, bdmat, ring, finals):
        ss = ring.alloc()
        fw.op(pe, lambda b: b.matmul(ss.h[:, 0:TT], lhsT=bdmat, rhs=sq.h[:, 0:TT], start=True, stop=True),
              reads=[sq.t, t_cbf], writes=[ss.t])
        r = rstd_from(ss, inv_n)
        for (dst_ap, src_ap, gain_ap, psl, reads, dst_tile) in finals:
            fw.op(dve, lambda b, dst_ap=dst_ap, src_ap=src_ap, gain_ap=gain_ap, psl=psl: b.scalar_tensor_tensor(
                out=dst_ap, in0=src_ap, scalar=gain_ap, in1=r.h[psl, 0:TT], op0=ALU.mult, op1=ALU.mult),
                reads=list(reads) + [r.t, t_prm, t_drv], writes=[dst_tile])

    ALLP = slice(0, 128)

    def mixer_part1(xb, j, hook_a=None, hook_b=None):
        hb, t_h = hB
        x = xbuf[xb]
        items = [(which, hd) for which in range(2) for hd in range(4)]
        slots = {}
        pend = None
        for i, (which, hd) in enumerate(items):
            if which not in slots:
                slots[which] = load_piece("win", which)
            slot = slots[which]
            ring = psA if i % 2 == 0 else psB
            ps = ring.alloc()
            pairs = [(slot.h[:, dc * 512 + hd * 128:dc * 512 + hd * 128 + 128], hb[:, dc, :]) for dc in range(8)]
            if i == 0:
                proj_group(ps, pairs, [slot.t], per_reads=[[t_h[dc]] for dc in range(8)])
            else:
                proj_group(ps, pairs, [slot.t] + t_h)
            sq = gn_front(ps.h[:, 0:TT], [ps.t])
            if which == 0:
                finals = [(qz[0:64, hd, 0, :], ps.h[0:64, 0:TT], drv[0:64, 0:1], slice(0, 64), [ps.t], t_q[hd]),
                          (qz[64:128, hd, 1, :], ps.h[64:128, 0:TT], drv[64:128, 0:1], slice(64, 128), [ps.t], t_q[hd])]
            else:
                finals = [(kT[:, hd, j * TT:(j + 1) * TT], ps.h[:, 0:TT], prm[:, P_GK:P_GK + 1], ALLP, [ps.t], t_k[hd][j])]
            if pend is not None:
                gn_back(*pend)
            pend = (sq, 1.0 / 64, bd_bf[:, :], ring, finals)
            if i == 2 and hook_a is not None:
                hook_a()
        slot = load_piece("win", 2)
        for tb in range(4):
            ps = psA.alloc()
            pairs = [(hb[:, dc, tb * 128:(tb + 1) * 128], slot.h[:, dc * 512:(dc + 1) * 512]) for dc in range(8)]
            proj_group(ps, pairs, [slot.t] + t_h)
            if tb == 0:
                gn_back(*pend)
                pend = None
            kb = j * 4 + tb
            fw.op(act, lambda b, ps=ps, kb=kb: b.activation(out=Vb[:, kb, :], in_=ps.h[:, 0:TT], func=AF.Copy),
                  reads=[ps.t], writes=[t_v[kb]])
        sl_b = load_piece("win", 3)
        sl_c = load_piece("win", 4)
        sl_h = load_piece("win", 5)
        pend = None
        for cc in range(4):
            ring = psA if cc % 2 == 0 else psB
            pss = []
            for sl in (sl_b, sl_c, sl_h):
                ps = ring.alloc()
                pairs = [(sl.h[:, dc * 512 + cc * 128:dc * 512 + cc * 128 + 128], hb[:, dc, :]) for dc in range(8)]
                proj_group(ps, pairs, [sl.t] + t_h)
                pss.append(ps)
            if pend is not None:
                gn_back(*pend)
                pend = None
            gb_ps, gc_ps, hc_ps = pss
            hcs = scr.alloc()
            fw.op(act, lambda b, hcs=hcs, hc_ps=hc_ps: b.activation(out=hcs.h[:, 0:TT], in_=hc_ps.h[:, 0:TT], func=AF.Copy),
                  reads=[hc_ps.t], writes=[hcs.t])
            ub = scr.alloc()
            u = ub.h
            fw.op(dve, lambda b, u=u, cc=cc: b.tensor_copy(out=u[:, 0:2], in_=ucar[:, cc, :]), reads=[t_u[cc]], writes=[ub.t])
            fw.op(dve, lambda b, u=u, hcs=hcs, gc_ps=gc_ps: b.tensor_tensor(
                out=u[:, 2:TT + 2], in0=hcs.h[:, 0:TT], in1=gc_ps.h[:, 0:TT], op=ALU.mult),
                reads=[hcs.t, gc_ps.t], writes=[ub.t])
            y = scr.alloc()
            cw = lambda k, cc=cc: prm[:, P_CW + cc * 3 + k:P_CW + cc * 3 + k + 1]
            fw.op(dve, lambda b, u=u, y=y, cw=cw: b.tensor_scalar(out=y.h[:, 0:TT], in0=u[:, 2:TT + 2], scalar1=cw(2),
                                                                   scalar2=None, op0=ALU.mult),
                  reads=[ub.t, t_prm], writes=[y.t])
            fw.op(dve, lambda b, u=u, y=y, cw=cw: b.scalar_tensor_tensor(
                out=y.h[:, 0:TT], in0=u[:, 1:TT + 1], scalar=cw(1), in1=y.h[:, 0:TT], op0=ALU.mult, op1=ALU.add),
                reads=[ub.t, t_prm], writes=[y.t])
            fw.op(dve, lambda b, u=u, y=y, cw=cw: b.scalar_tensor_tensor(
                out=y.h[:, 0:TT], in0=u[:, 0:TT], scalar=cw(0), in1=y.h[:, 0:TT], op0=ALU.mult, op1=ALU.add),
                reads=[ub.t, t_prm], writes=[y.t])
            fw.op(dve, lambda b, u=u, cc=cc: b.tensor_copy(out=ucar[:, cc, :], in_=u[:, TT:TT + 2]),
                  reads=[ub.t], writes=[t_u[cc]])
            fw.op(dve, lambda b, y=y, gb_ps=gb_ps: b.tensor_tensor(out=y.h[:, 0:TT], in0=y.h[:, 0:TT], in1=gb_ps.h[:, 0:TT],
                                                                    op=ALU.mult), reads=[gb_ps.t], writes=[y.t])
            sq = gn_front(y.h[:, 0:TT], [y.t])
            pend = (sq, 1.0 / 64, bd_bf[:, :], ring,
                    [(mixed[:, 4 + cc, :], y.h[:, 0:TT], prm[:, P_GCN + cc:P_GCN + cc + 1], ALLP, [y.t], t_mx[4 + cc])])
        if hook_b is not None:
            hook_b()
        gn_back(*pend)
        pend = None
        nkb = 4 * j + 4
        LAG = 2

        def att_p1(hd, acc, zz):
            ls, cs = [], []
            for m in range(2):
                l = scr.alloc()
                fw.op(act, lambda b, l=l, m=m: b.activation(out=l.h[:, 0:TT], in_=zz[m].h[:, 0:TT], func=AF.Copy),
                      reads=[zz[m].t], writes=[l.t])
                ls.append(l)
            for m in (1, 0):
                c = scr.alloc()
                fw.op(dve, lambda b, c=c, m=m: b.tensor_copy(out=c.h[:, 0:TT], in_=acc[m].h[:, 0:TT]),
                      reads=[acc[m].t], writes=[c.t])
                cs.insert(0, c)
            return (hd, ls, cs)

        def att_p2(st):
            hd, ls, cs = st
            for m in range(2):
                fw.op(dve, lambda b, l=ls[m]: b.reciprocal(out=l.h[:, 0:TT], in_=l.h[:, 0:TT]), reads=[ls[m].t], writes=[ls[m].t])
            for m in range(2):
                fw.op(dve, lambda b, c=cs[m], l=ls[m]: b.tensor_tensor(out=c.h[:, 0:TT], in0=c.h[:, 0:TT], in1=l.h[:, 0:TT],
                                                                        op=ALU.mult), reads=[ls[m].t], writes=[cs[m].t])
            fw.op(dve, lambda b: b.scalar_tensor_tensor(
                out=cs[0].h[:, 0:TT], in0=cs[1].h[:, 0:TT], scalar=drv[:, 2:3], in1=cs[0].h[:, 0:TT], op0=ALU.mult, op1=ALU.add),
                reads=[cs[1].t, t_drv], writes=[cs[0].t])
            a = cs[0]
            sq = sqring.alloc()
            fw.op(dve, lambda b: b.tensor_tensor(out=sq.h[:, 0:TT], in0=a.h[:, 0:TT], in1=a.h[:, 0:TT], op=ALU.mult),
                  reads=[a.t], writes=[sq.t])
            return (hd, a, sq)

        def att_p3(st, ring):
            hd, a, sq = st
            gn_back(sq, 1.0 / 128, ones_bf[:, :], ring,
                    [(mixed[:, hd, :], a.h[:, 0:TT], drv[:, 1:2], ALLP, [a.t], t_mx[hd])])

        st1 = None
        st2 = None
        for hd in range(4):
            acc = [psB.alloc(), psB.alloc()]
            zz = [psB.alloc(), psB.alloc()]
            inflight = []
            for kb in range(nkb + LAG):
                if kb < nkb:
                    rel = kb - 4 * j
                    koff = max(0, rel) * 128
                    s3, ta, tb_ = pairA.alloc()

                    def emit_s(b, s3=s3, kb=kb, koff=koff, hd=hd):
                        ins = None
                        for m in range(2):
                            ins = b.matmul(s3[:, m, koff:TT], lhsT=kT[:, hd, kb * 128:(kb + 1) * 128],
                                           rhs=qz[:, hd, m, koff:TT], start=True, stop=True)
                        return ins
                    fw.op(pe, emit_s, reads=[t_k[hd][kb // 4], t_q[hd]], writes=[ta, tb_])
                    p = pring.alloc()
                    bcol = 128 + hd * NREL + (rel + NREL - 4)
                    fw.op(act, lambda b, p=p, s3=s3, koff=koff, bcol=bcol: b.activation(
                        out=p.h[:, :, koff:TT], in_=s3[:, :, koff:TT], func=AF.Exp, bias=cst[:, bcol:bcol + 1], scale=1.0),
                        reads=[ta, tb_, t_cst], writes=[p.t])
                    if rel >= 0:
                        fw.op(dve, lambda b, p=p, koff=koff: b.tensor_tensor(
                            out=p.h[:, :, koff:koff + 128], in0=p.h[:, :, koff:koff + 128], in1=tri2[:, :, :], op=ALU.mult),
                            reads=[t_cbf], writes=[p.t])
                    inflight.append((kb, koff, p))
                if kb == 0 and st1 is not None:
                    st2 = att_p2(st1)
                    st1 = None
                if kb >= LAG:
                    pkb, pkoff, pp = inflight.pop(0)

                    def emit_pv(b, kb=pkb, koff=pkoff, p=pp, hd=hd, acc=acc, zz=zz):
                        ins = None
                        for m in range(2):
                            b.matmul(acc[m].h[:, koff:TT], lhsT=Vb[:, kb, hd * 128:(hd + 1) * 128], rhs=p.h[:, m, koff:TT],
                                     start=(kb == 0), stop=(kb == nkb - 1))
                            ins = b.matmul(zz[m].h[:, koff:TT], lhsT=ones_bf[:, :], rhs=p.h[:, m, koff:TT],
                                           start=(kb == 0), stop=(kb == nkb - 1))
                        return ins
                    fw.op(pe, emit_pv, reads=[pp.t, t_v[pkb], t_cbf], writes=[acc[0].t, acc[1].t, zz[0].t, zz[1].t])
                if kb == nkb - 1 and st2 is not None:
                    att_p3(st2, psA)
                    st2 = None
            st1 = att_p1(hd, acc, zz)
        st2 = att_p2(st1)
        return lambda ring: att_p3(st2, ring)

    def mixer_part2(xb, tail):
        x = xbuf[xb]
        tail(psA)
        for half in range(2):
            slot = load_piece("wout", half)
            for q in range(4):
                dco = half * 4 + q
                ps = psA.alloc()
                pairs = [(slot.h[:, c * 512 + q * 128:c * 512 + q * 128 + 128], mixed[:, c, :]) for c in range(8)]
                proj_group(ps, pairs, [slot.t] + t_mx)
                fw.op(dve, lambda b, ps=ps, dco=dco: b.tensor_tensor(out=x[:, dco, :], in0=ps.h[:, 0:TT], in1=x[:, dco, :],
                                                                      op=ALU.add), reads=[ps.t], writes=[t_x[xb][dco]])

    def load_x(j):
        xb = j % 2
        fw.dma(pool, xbuf[xb][:, :, :], xT[:, j * TT:(j + 1) * TT].rearrange("(dc p) s -> p dc s", p=128), ds_x[xb],
               writes=t_x[xb])

    finals = []
    load_x(0)
    norm_to_h(0, P_G1, hA)
    ffn_up("wgu1", hA)
    if NCH > 1:
        load_x(1)
    ffn_down("wd1", 0)
    norm_to_h(0, P_GM, hB)
    for j in range(NCH):
        xb = j % 2
        nxt = j + 1 < NCH
        hook_a = None
        if j > 0:
            def hook_a(j=j, nxt=nxt):
                finals.extend(final_norm_store(j - 1))
                if nxt:
                    load_x(j + 1)
        hook_b = (lambda j=j: norm_to_h((j + 1) % 2, P_G1, hA)) if nxt else None
        tail = mixer_part1(xb, j, hook_a, hook_b)
        if nxt:
            ffn_up("wgu1", hA, hook=lambda tail=tail: tail(psB))
            mixer_part2(xb, lambda ring: None)
            ffn_down("wd1", (j + 1) % 2, hook=lambda xb=xb: norm_to_h(xb, P_G2, hB))
        else:
            mixer_part2(xb, tail)
            norm_to_h(xb, P_G2, hB)
        ffn_up("wgu2", hB)
        ffn_down("wd2", xb, hook=(lambda j=j: norm_to_h((j + 1) % 2, P_GM, hB)) if nxt else None)
    finals.extend(final_norm_store(NCH - 1))
    fw.finish(finals)
    return nc


def _consts():
    cst = np.zeros((128, 256), np.float32)
    ki = np.arange(128)
    cst[:, 0:128] = (ki[:, None] <= ki[None, :]).astype(np.float32)
    for h in range(4):
        for r in range(NREL):
            rel = r - (NREL - 4)
            cst[:, 128 + h * NREL + r] = SLOPES[h] * (ki + 128.0 * rel - 256.0)
    return cst


def _layout_weights(inp):
    f32 = lambda a: np.ascontiguousarray(np.asarray(a, dtype=np.float32))
    out = {}
    for i, tag in ((1, "ffn1"), (2, "ffn2")):
        wg = f32(inp[f"{tag}_w_gate"])[0].reshape(8, 128, NFG, 256)
        wu = f32(inp[f"{tag}_w_up"])[0].reshape(8, 128, NFG, 256)
        gu = np.stack([wg, wu], axis=0)
        out[f"wgu{i}"] = np.ascontiguousarray(gu.transpose(3, 2, 0, 1, 4)).reshape(NFG, 128, 4096)
        wd = f32(inp[f"{tag}_w_down"])[0].reshape(NFC, 128, 2, 512)
        out[f"wd{i}"] = np.ascontiguousarray(wd.transpose(2, 1, 0, 3)).reshape(2, 128, NFC * 512)
    win = f32(inp["w_in"])[0].reshape(8, 128, 6, 512)
    out["win"] = np.ascontiguousarray(win.transpose(2, 1, 0, 3)).reshape(6, 128, 4096)
    wo = f32(inp["w_out"])[0].reshape(8, 128, 2, 512)
    out["wout"] = np.ascontiguousarray(wo.transpose(2, 1, 0, 3)).reshape(2, 128, 4096)
    prm = np.zeros((128, NPRM), np.float32)
    for col, key in ((P_G1, "ffn1_norm"), (P_GM, "mix_norm"), (P_G2, "ffn2_norm"), (P_GF, "final_norm")):
        prm[:, col:col + 8] = f32(inp[key])[0].reshape(8, 128).T
    prm[:, P_GQ] = np.tile(f32(inp["q_norm"])[0], 2)
    prm[:, P_GK] = np.tile(f32(inp["k_norm"])[0], 2)
    prm[:, P_GSUB] = f32(inp["attn_subln"])[0]
    prm[:, P_GCN:P_GCN + 4] = f32(inp["conv_norm"])[0].reshape(4, 128).T
    cw = f32(inp["conv_w"])[0]
    for cc in range(4):
        for k in range(3):
            prm[:, P_CW + cc * 3 + k] = cw[k, cc * 128:(cc + 1) * 128]
    for i, key in enumerate(("lambda_q1", "lambda_k1", "lambda_q2", "lambda_k2")):
        prm[:, P_LAM + i * 64:P_LAM + (i + 1) * 64] = f32(inp[key])[0][None, :]
    out["prm"] = prm
    out["cst"] = _consts()
    return out


_NC_CACHE = {}


def _get_nc(nch):
    if nch not in _NC_CACHE:
        _NC_CACHE[nch] = build_nc(nch)
    return _NC_CACHE[nch]


def kernel(**inputs):
    x = np.asarray(inputs["x"], dtype=np.float32)
    B, S, _ = x.shape
    nch = S // TT
    shared = _layout_weights(inputs)
    in_maps = []
    for b in range(B):
        m = dict(shared)
        m["xT"] = np.ascontiguousarray(x[b].T)
        in_maps.append(m)
    nc = build_nc(nch)
    res = run_bass_kernel_spmd(nc, in_maps, core_ids=list(range(B)))
    out = np.stack([np.ascontiguousarray(np.asarray(r["outT"]).T) for r in res.results], axis=0)
    return out.astype(np.float32)
```

```python
import math
import numpy as np
import concourse.bass as bass
import concourse.mybir as mybir
from concourse.bass_utils import run_bass_kernel_spmd

F32 = mybir.dt.float32
BF16 = mybir.dt.bfloat16
ALU = mybir.AluOpType
AF = mybir.ActivationFunctionType
AX = mybir.AxisListType

D = 1024
DFF = 2816
NFC = 22
NFG = 11
TT = 512
EPS = 1e-6
LAM_INIT = 0.8 - 0.6 * math.exp(-0.3 * 0)
SLOPES = [2.0 ** (-8.0 * (i + 1) / 4) for i in range(4)]
NREL = 32

P_G1, P_GM, P_G2, P_GF = 0, 8, 16, 24
P_GQ, P_GK, P_GSUB = 32, 33, 34
P_GCN = 35
P_CW = 39
P_LAM = 51
NPRM = P_LAM + 256


class Tl:
    __slots__ = ("name", "w", "r", "excl")

    def __init__(self, name, excl=False):
        self.name = name
        self.w = None
        self.r = []
        self.excl = excl


class DSem:
    def __init__(self, nc, name):
        self.sem = nc.alloc_semaphore(name)
        self.key = name
        self.count = 0


class Eng:
    def __init__(self, nc, name, b):
        self.name = name
        self.b = b
        self.sem = nc.alloc_semaphore("s_" + name)
        self.key = "s_" + name
        self.n = 0
        self.seen = {}
        self.q = []

    def wait(self, ev):
        key, sem, val = ev
        if self.seen.get(key, 0) >= val:
            return
        self.seen[key] = val
        self.q.append(lambda b, sem=sem, val=val: b.wait_ge(sem, val))


class FW:
    def __init__(self, nc):
        self.nc = nc
        self.pe = Eng(nc, "pe", nc.tensor)
        self.act = Eng(nc, "act", nc.scalar)
        self.dve = Eng(nc, "dve", nc.vector)
        self.pool = Eng(nc, "pool", nc.gpsimd)
        self.sp = Eng(nc, "sp", nc.sync)
        self.nds = 0

    def dsem(self, name=None):
        self.nds += 1
        return DSem(self.nc, name or f"d{self.nds}")

    def _deps(self, eng, reads, writes):
        deps = []
        for t in reads:
            if t.w is not None:
                deps.append(t.w)
            if t.excl:
                deps.extend(t.r)
        for t in writes:
            if t.w is not None:
                deps.append(t.w)
            deps.extend(t.r)
        if eng is self.pe:
            deps = [d for d in deps if d[0] != self.pe.key]
        return deps

    def op(self, eng, emit, reads=(), writes=()):
        for d in self._deps(eng, reads, writes):
            eng.wait(d)
        eng.n += 1
        sem = eng.sem
        eng.q.append(lambda b, emit=emit, sem=sem: emit(b).then_inc(sem, 1))
        ev = (eng.key, eng.sem, eng.n)
        for t in reads:
            t.r.append(ev)
        for t in writes:
            t.w = ev
            t.r = []
        return ev

    def dma(self, eng, out, in_, ds, reads=(), writes=()):
        for d in self._deps(eng, reads, writes):
            eng.wait(d)
        ds.count += 16
        sem = ds.sem
        eng.q.append(lambda b, out=out, in_=in_, sem=sem: b.dma_start(out=out, in_=in_).then_inc(sem, 16))
        ev = (ds.key, ds.sem, ds.count)
        for t in reads:
            t.r.append(ev)
        for t in writes:
            t.w = ev
            t.r = []
        return ev

    def finish(self, final_events=()):
        for ev in final_events:
            self.sp.wait(ev)
        with self.nc.Block() as block:
            @block.sync
            def _(e):
                for f in self.sp.q:
                    f(e)

            @block.tensor
            def _(e):
                for f in self.pe.q:
                    f(e)

            @block.scalar
            def _(e):
                for f in self.act.q:
                    f(e)

            @block.vector
            def _(e):
                for f in self.dve.q:
                    f(e)

            @block.gpsimd
            def _(e):
                for f in self.pool.q:
                    f(e)


class Buf:
    __slots__ = ("h", "t", "ds")

    def __init__(self, h, t, ds=None):
        self.h = h
        self.t = t
        self.ds = ds


class BankView:
    def __init__(self, t, m):
        self.t = t
        self.m = m

    def __getitem__(self, idx):
        p, c = idx
        return self.t[p, self.m, c]


class Ring:
    def __init__(self, bufs):
        self.bufs = bufs
        self.i = 0

    def alloc(self):
        b = self.bufs[self.i % len(self.bufs)]
        self.i += 1
        return b


def build_nc(NCH, dbg=False):
    S = NCH * TT
    NKB = S // 128
    nc = bass.Bass("TRN2", target_bir_lowering=False)
    fw = FW(nc)
    pe, act, dve, pool, sp = fw.pe, fw.act, fw.dve, fw.pool, fw.sp

    xT = nc.dram_tensor("xT", [D, S], F32, kind="ExternalInput").ap()
    outT = nc.dram_tensor("outT", [D, S], F32, kind="ExternalOutput").ap()
    prm_d = nc.dram_tensor("prm", [128, NPRM], F32, kind="ExternalInput").ap()
    cst_d = nc.dram_tensor("cst", [128, 256], F32, kind="ExternalInput").ap()
    w32 = {}
    wbf = {}
    wshape = {"wgu1": [NFG, 128, 4096], "wd1": [2, 128, NFC * 512], "win": [6, 128, 4096],
              "wout": [2, 128, 4096], "wgu2": [NFG, 128, 4096], "wd2": [2, 128, NFC * 512]}
    for k, shp in wshape.items():
        w32[k] = nc.dram_tensor(k, shp, F32, kind="ExternalInput").ap()
        wbf[k] = nc.dram_tensor(k + "_bf", shp, BF16, kind="Internal").ap()
    wtl = {k: Tl(f"wt_{k}") for k in wshape}
    ds_wb = {k: fw.dsem(f"dwb_{k}") for k in wshape}

    def sb(name, shape, dt):
        return nc.alloc_sbuf_tensor(name, shape, dt)

    xbuf = [sb(f"x{i}", [128, 8, TT], F32) for i in range(2)]
    t_x = [[Tl(f"x{i}_{dc}") for dc in range(8)] for i in range(2)]
    ds_x = [fw.dsem(f"dx{i}") for i in range(2)]
    ds_o = [fw.dsem(f"do{i}") for i in range(2)]
    hA = (sb("hA", [128, 8, TT], BF16), [Tl(f"hA{dc}") for dc in range(8)])
    hB = (sb("hB", [128, 8, TT], BF16), [Tl(f"hB{dc}") for dc in range(8)])
    actb = sb("act", [128, NFC, TT], BF16)
    t_act = [Tl(f"act{fc}") for fc in range(NFC)]
    qz = sb("qz", [128, 4, 2, TT], BF16)
    t_q = [Tl(f"q{h}") for h in range(4)]
    kT = sb("kT", [128, 4, S], BF16)
    t_k = [[Tl(f"k{h}_{j}") for j in range(NCH)] for h in range(4)]
    Vb = sb("V", [128, NKB, 512], BF16)
    t_v = [Tl(f"v{kb}") for kb in range(NKB)]
    mixed = sb("mixed", [128, 8, TT], BF16)
    t_mx = [Tl(f"mx{c}") for c in range(8)]
    ucar = sb("ucar", [128, 4, 2], F32)
    t_u = [Tl(f"u{cc}") for cc in range(4)]
    wring = Ring([Buf(sb(f"wr{i}", [128, 4096], BF16), Tl(f"wr{i}"), fw.dsem(f"dwr{i}")) for i in range(4)])
    pring = Ring([Buf(sb(f"pr{i}", [128, 2, TT], BF16), Tl(f"pr{i}")) for i in range(3)])
    sqring = Ring([Buf(sb(f"sq{i}", [128, TT], BF16), Tl(f"sq{i}")) for i in range(2)])
    sqn = Ring([Buf(sb(f"sqn{i}", [128, TT], BF16), Tl(f"sqn{i}")) for i in range(2)])
    scr = Ring([Buf(sb(f"sc{i}", [128, TT + 2], F32), Tl(f"sc{i}"), fw.dsem(f"dsc{i}")) for i in range(6)])
    prm = sb("prm_s", [128, P_LAM], F32)
    t_prm = Tl("prm")
    cst = sb("cst_s", [128, 256], F32)
    t_cst = Tl("cst")
    drv = sb("drv", [128, 8], F32)
    t_drv = Tl("drv")
    t_dummy = Tl("dummy")
    lamt = sb("lamt", [128, 2, 64], F32)
    lame = sb("lame", [128, 4], F32)
    t_lam = Tl("lam")
    ones_bf = sb("ones_bf", [128, 128], BF16)
    bd_bf = sb("bd_bf", [128, 128], BF16)
    tri2 = sb("tri2", [128, 2, 128], BF16)
    t_cbf = Tl("cbf")

    pa = [nc.alloc_psum_tensor(f"pa{i}", [128, 2, TT], F32) for i in range(2)]
    pb = [nc.alloc_psum_tensor(f"pb{i}", [128, TT], F32) for i in range(4)]
    bankA = [Buf(BankView(pa[i // 2], i % 2), Tl(f"psA{i}", excl=True)) for i in range(4)]
    bankB = [Buf(pb[i], Tl(f"psB{i}", excl=True)) for i in range(4)]
    psA = Ring(bankA)
    psB = Ring(bankB)
    pairA = Ring([(pa[0], bankA[0].t, bankA[1].t), (pa[1], bankA[2].t, bankA[3].t)])

    fw.dma(sp, prm[:, :], prm_d[:, 0:P_LAM], fw.dsem("dprm"), writes=[t_prm])
    lamb = scr.alloc()
    fw.dma(sp, lamb.h[:, 0:256], prm_d[:, P_LAM:P_LAM + 256], fw.dsem("dlam"), writes=[lamb.t])
    fw.dma(sp, cst[:, :], cst_d, fw.dsem("dcst"), writes=[t_cst])
    fw.op(dve, lambda b: b.memset(ones_bf[:, :], 1.0), writes=[t_cbf])
    fw.op(dve, lambda b: b.memset(bd_bf[:, :], 0.0), writes=[t_cbf])
    fw.op(dve, lambda b: b.memset(bd_bf[0:64, 0:64], 1.0), writes=[t_cbf])
    fw.op(dve, lambda b: b.memset(bd_bf[64:128, 64:128], 1.0), writes=[t_cbf])
    for m in range(2):
        fw.op(dve, lambda b, m=m: b.tensor_copy(out=tri2[:, m, :], in_=cst[:, 0:128]), reads=[t_cst], writes=[t_cbf])
    for cc in range(4):
        fw.op(dve, lambda b, cc=cc: b.memset(ucar[:, cc, :], 0.0), writes=[t_u[cc]])
    for hd in range(4):
        fw.op(dve, lambda b, hd=hd: b.memset(qz[:, hd, :, :], 0.0), writes=[t_q[hd]])
    fw.op(dve, lambda b: b.tensor_scalar(out=drv[:, 0:1], in0=prm[:, P_GQ:P_GQ + 1], scalar1=0.125, scalar2=None,
                                          op0=ALU.mult), reads=[t_prm], writes=[t_drv])
    fw.op(dve, lambda b: b.tensor_scalar(out=drv[:, 1:2], in0=prm[:, P_GSUB:P_GSUB + 1], scalar1=1.0 - LAM_INIT,
                                          scalar2=None, op0=ALU.mult), reads=[t_prm], writes=[t_drv])
    fw.op(dve, lambda b: b.memset(drv[:, 3:4], EPS), writes=[t_drv])
    fw.op(dve, lambda b: b.tensor_tensor(out=lamt[:, 0, :], in0=lamb.h[:, 0:64],
                                          in1=lamb.h[:, 64:128], op=ALU.mult), reads=[lamb.t], writes=[t_lam])
    fw.op(dve, lambda b: b.tensor_tensor(out=lamt[:, 1, :], in0=lamb.h[:, 128:192],
                                          in1=lamb.h[:, 192:256], op=ALU.mult), reads=[lamb.t], writes=[t_lam])
    fw.op(dve, lambda b: b.tensor_reduce(out=lame[:, 0:2], in_=lamt[:, :, :], axis=AX.X, op=ALU.add),
          reads=[t_lam], writes=[t_lam])
    fw.op(act, lambda b: b.activation(out=lame[:, 2:4], in_=lame[:, 0:2], func=AF.Exp), reads=[t_lam], writes=[t_lam])
    fw.op(dve, lambda b: b.tensor_tensor(out=lame[:, 0:1], in0=lame[:, 3:4], in1=lame[:, 2:3], op=ALU.subtract),
          reads=[t_lam], writes=[t_lam])
    fw.op(dve, lambda b: b.tensor_scalar(out=drv[:, 2:3], in0=lame[:, 0:1], scalar1=-LAM_INIT, scalar2=None,
                                          op0=ALU.add), reads=[t_lam], writes=[t_drv])
    eps_col = drv[:, 3:4]

    seen_pieces = set()

    def load_piece(key, idx, col0=0, ncol=4096):
        slot = wring.alloc()
        pk = (key, idx, col0)
        if pk not in seen_pieces:
            seen_pieces.add(pk)
            fw.dma(pool, slot.h[:, 0:ncol], w32[key][idx][:, col0:col0 + ncol], slot.ds, writes=[slot.t])
            ev = fw.dma(sp, wbf[key][idx][:, col0:col0 + ncol], slot.h[:, 0:ncol], ds_wb[key], reads=[slot.t])
            wtl[key].w = ev
        else:
            fw.dma(sp, slot.h[:, 0:ncol], wbf[key][idx][:, col0:col0 + ncol], slot.ds,
                   reads=[wtl[key]], writes=[slot.t])
        return slot

    def mm_group(out_ap, pairs, start=True, stop=True):
        def emit(b):
            ins = None
            n = len(pairs)
            for i, (l, r) in enumerate(pairs):
                ins = b.matmul(out_ap, lhsT=l, rhs=r, start=(start and i == 0), stop=(stop and i == n - 1))
            return ins
        return emit

    def proj_group(ps, pairs, common_reads, per_reads=None):
        if per_reads is None:
            fw.op(pe, mm_group(ps.h[:, 0:TT], pairs), reads=common_reads, writes=[ps.t])
        else:
            n = len(pairs)
            for i, (l, r) in enumerate(pairs):
                fw.op(pe, lambda b, l=l, r=r, i=i: b.matmul(ps.h[:, 0:TT], lhsT=l, rhs=r, start=(i == 0), stop=(i == n - 1)),
                      reads=list(common_reads) + list(per_reads[i]), writes=[ps.t])

    def preload_ln_table():
        fw.op(act, lambda b: b.activation(out=drv[:, 4:5], in_=drv[:, 3:4], func=AF.Ln), reads=[t_drv], writes=[t_dummy])

    def rstd_from(ps, inv_n):
        r = scr.alloc()
        fw.op(act, lambda b: b.activation(out=r.h[:, 0:TT], in_=ps.h[:, 0:TT], func=AF.Ln, bias=eps_col, scale=inv_n),
              reads=[ps.t, t_drv], writes=[r.t])
        fw.op(act, lambda b: b.activation(out=r.h[:, 0:TT], in_=r.h[:, 0:TT], func=AF.Exp, scale=-0.5),
              reads=[r.t], writes=[r.t])
        return r

    def norm_stats(xb):
        x = xbuf[xb]
        ps = psA.alloc()
        for dc in range(8):
            sq = sqn.alloc()
            fw.op(act, lambda b, sq=sq, dc=dc: b.activation(out=sq.h[:, 0:TT], in_=x[:, dc, :], func=AF.Square),
                  reads=[t_x[xb][dc]], writes=[sq.t])
            fw.op(pe, lambda b, sq=sq, dc=dc: b.matmul(ps.h[:, 0:TT], lhsT=ones_bf[:, :], rhs=sq.h[:, 0:TT],
                                                       start=(dc == 0), stop=(dc == 7)),
                  reads=[sq.t, t_cbf], writes=[ps.t])
        return rstd_from(ps, 1.0 / D)

    def norm_to_h(xb, gcol, hbuf):
        hb, t_h = hbuf
        x = xbuf[xb]
        r = norm_stats(xb)
        for dc in range(8):
            fw.op(dve, lambda b, dc=dc: b.scalar_tensor_tensor(
                out=hb[:, dc, :], in0=x[:, dc, :], scalar=prm[:, gcol + dc:gcol + dc + 1], in1=r.h[:, 0:TT],
                op0=ALU.mult, op1=ALU.mult), reads=[t_x[xb][dc], r.t, t_prm], writes=[t_h[dc]])

    def final_norm_store(j):
        xb = j % 2
        x = xbuf[xb]
        r = norm_stats(xb)
        for dc in range(8):
            fw.op(dve, lambda b, dc=dc: b.scalar_tensor_tensor(
                out=x[:, dc, :], in0=x[:, dc, :], scalar=prm[:, P_GF + dc:P_GF + dc + 1], in1=r.h[:, 0:TT],
                op0=ALU.mult, op1=ALU.mult), reads=[r.t, t_prm], writes=[t_x[xb][dc]])
        return [fw.dma(pool, outT[:, j * TT:(j + 1) * TT].rearrange("(dc p) s -> p dc s", p=128), x[:, :, :],
                       ds_o[xb], reads=t_x[xb])]

    def ffn_up(kgu, hbuf, hook=None):
        hb, t_h = hbuf
        for fg in range(NFG):
            slot = load_piece(kgu, fg)
            for f2 in range(2):
                fc = fg * 2 + f2
                g_ps = psA.alloc()
                u_ps = psA.alloc()
                prs = []
                for which in range(2):
                    prs.append([(slot.h[:, (which * 8 + dc) * 256 + f2 * 128:(which * 8 + dc) * 256 + f2 * 128 + 128],
                                 hb[:, dc, :]) for dc in range(8)])
                if fc == 0:
                    for dc in range(8):
                        for which, ps in ((0, g_ps), (1, u_ps)):
                            l, r = prs[which][dc]
                            fw.op(pe, lambda b, l=l, r=r, ps=ps, dc=dc: b.matmul(ps.h[:, 0:TT], lhsT=l, rhs=r,
                                                                                 start=(dc == 0), stop=(dc == 7)),
                                  reads=[slot.t, t_h[dc]], writes=[ps.t])
                else:
                    for which, ps in ((0, g_ps), (1, u_ps)):
                        fw.op(pe, mm_group(ps.h[:, 0:TT], prs[which]), reads=[slot.t] + t_h, writes=[ps.t])
                sg = scr.alloc()
                fw.op(act, lambda b, sg=sg, g_ps=g_ps: b.activation(out=sg.h[:, 0:TT], in_=g_ps.h[:, 0:TT], func=AF.Silu),
                      reads=[g_ps.t], writes=[sg.t])
                fw.op(dve, lambda b, sg=sg, u_ps=u_ps, fc=fc: b.tensor_tensor(
                    out=actb[:, fc, :], in0=sg.h[:, 0:TT], in1=u_ps.h[:, 0:TT], op=ALU.mult),
                    reads=[sg.t, u_ps.t], writes=[t_act[fc]])
            if fg == 1 and hook is not None:
                hook()

    def ffn_down(kd, xb, hook=None):
        x = xbuf[xb]
        preload_ln_table()
        for half in range(2):
            ring = psB if half == 0 else psA
            accs = [ring.alloc() for _ in range(4)]
            for (fc0, fc1) in ((0, 8), (8, 16), (16, 22)):
                slot = load_piece(kd, half, fc0 * 512, (fc1 - fc0) * 512)
                for fc in range(fc0, fc1):
                    def emit(b, fc=fc, fc0=fc0, slot=slot, accs=accs):
                        ins = None
                        for q in range(4):
                            ins = b.matmul(accs[q].h[:, 0:TT],
                                           lhsT=slot.h[:, (fc - fc0) * 512 + q * 128:(fc - fc0) * 512 + q * 128 + 128],
                                           rhs=actb[:, fc, :], start=(fc == 0), stop=(fc == NFC - 1))
                        return ins
                    fw.op(pe, emit, reads=[slot.t, t_act[fc]], writes=[a.t for a in accs])
                if half == 0 and fc0 == 0 and hook is not None:
                    hook()
            for q in range(4):
                dco = half * 4 + q
                fw.op(dve, lambda b, q=q, dco=dco, accs=accs: b.scalar_tensor_tensor(
                    out=x[:, dco, :], in0=accs[q].h[:, 0:TT], scalar=0.5, in1=x[:, dco, :],
                    op0=ALU.mult, op1=ALU.add), reads=[accs[q].t], writes=[t_x[xb][dco]])

    def gn_front(src_ap, src_tiles):
        sq = sqring.alloc()
        fw.op(act, lambda b: b.activation(out=sq.h[:, 0:TT], in_=src_ap, func=AF.Square), reads=src_tiles, writes=[sq.t])
        return sq

    def gn_back(sq, inv_n, bdmat, ring, finals):
        ss = ring.alloc()
        fw.op(pe, lambda b: b.matmul(ss.h[:, 0:TT], lhsT=bdmat, rhs=sq.h[:, 0:TT], start=True, stop=True),
              reads=[sq.t, t_cbf], writes=[ss.t])
        r = rstd_from(ss, inv_n)
        for (dst_ap, src_ap, gain_ap, psl, reads, dst_tile) in finals:
            fw.op(dve, lambda b, dst_ap=dst_ap, src_ap=src_ap, gain_ap=gain_ap, psl=psl: b.scalar_tensor_tensor(
                out=dst_ap, in0=src_ap, scalar=gain_ap, in1=r.h[psl, 0:TT], op0=ALU.mult, op1=ALU.mult),
                reads=list(reads) + [r.t, t_prm, t_drv], writes=[dst_tile])

    ALLP = slice(0, 128)

    def mixer_part1(xb, j, hook_a=None, hook_b=None):
        hb, t_h = hB
        x = xbuf[xb]
        items = [(which, hd) for which in range(2) for hd in range(4)]
        slots = {}
        pend = None
        for i, (which, hd) in enumerate(items):
            if which not in slots:
                slots[which] = load_piece("win", which)
            slot = slots[which]
            ring = psA if i % 2 == 0 else psB
            ps = ring.alloc()
            pairs = [(slot.h[:, dc * 512 + hd * 128:dc * 512 + hd * 128 + 128], hb[:, dc, :]) for dc in range(8)]
            if i == 0:
                proj_group(ps, pairs, [slot.t], per_reads=[[t_h[dc]] for dc in range(8)])
            else:
                proj_group(ps, pairs, [slot.t] + t_h)
            sq = gn_front(ps.h[:, 0:TT], [ps.t])
            if which == 0:
                finals = [(qz[0:64, hd, 0, :], ps.h[0:64, 0:TT], drv[0:64, 0:1], slice(0, 64), [ps.t], t_q[hd]),
                          (qz[64:128, hd, 1, :], ps.h[64:128, 0:TT], drv[64:128, 0:1], slice(64, 128), [ps.t], t_q[hd])]
            else:
                finals = [(kT[:, hd, j * TT:(j + 1) * TT], ps.h[:, 0:TT], prm[:, P_GK:P_GK + 1], ALLP, [ps.t], t_k[hd][j])]
            if pend is not None:
                gn_back(*pend)
            pend = (sq, 1.0 / 64, bd_bf[:, :], ring, finals)
            if i == 2 and hook_a is not None:
                hook_a()
        slot = load_piece("win", 2)
        for tb in range(4):
            ps = psA.alloc()
            pairs = [(hb[:, dc, tb * 128:(tb + 1) * 128], slot.h[:, dc * 512:(dc + 1) * 512]) for dc in range(8)]
            proj_group(ps, pairs, [slot.t] + t_h)
            if tb == 0:
                gn_back(*pend)
                pend = None
            kb = j * 4 + tb
            fw.op(act, lambda b, ps=ps, kb=kb: b.activation(out=Vb[:, kb, :], in_=ps.h[:, 0:TT], func=AF.Copy),
                  reads=[ps.t], writes=[t_v[kb]])
        sl_b = load_piece("win", 3)
        sl_c = load_piece("win", 4)
        sl_h = load_piece("win", 5)
        pend = None
        for cc in range(4):
            ring = psA if cc % 2 == 0 else psB
            pss = []
            for sl in (sl_b, sl_c, sl_h):
                ps = ring.alloc()
                pairs = [(sl.h[:, dc * 512 + cc * 128:dc * 512 + cc * 128 + 128], hb[:, dc, :]) for dc in range(8)]
                proj_group(ps, pairs, [sl.t] + t_h)
                pss.append(ps)
            if pend is not None:
                gn_back(*pend)
                pend = None
            gb_ps, gc_ps, hc_ps = pss
            hcs = scr.alloc()
            fw.op(act, lambda b, hcs=hcs, hc_ps=hc_ps: b.activation(out=hcs.h[:, 0:TT], in_=hc_ps.h[:, 0:TT], func=AF.Copy),
                  reads=[hc_ps.t], writes=[hcs.t])
            ub = scr.alloc()
            u = ub.h
            fw.op(dve, lambda b, u=u, cc=cc: b.tensor_copy(out=u[:, 0:2], in_=ucar[:, cc, :]), reads=[t_u[cc]], writes=[ub.t])
            fw.op(dve, lambda b, u=u, hcs=hcs, gc_ps=gc_ps: b.tensor_tensor(
                out=u[:, 2:TT + 2], in0=hcs.h[:, 0:TT], in1=gc_ps.h[:, 0:TT], op=ALU.mult),
                reads=[hcs.t, gc_ps.t], writes=[ub.t])
            y = scr.alloc()
            cw = lambda k, cc=cc: prm[:, P_CW + cc * 3 + k:P_CW + cc * 3 + k + 1]
            fw.op(dve, lambda b, u=u, y=y, cw=cw: b.tensor_scalar(out=y.h[:, 0:TT], in0=u[:, 2:TT + 2], scalar1=cw(2),
                                                                   scalar2=None, op0=ALU.mult),
                  reads=[ub.t, t_prm], writes=[y.t])
            fw.op(dve, lambda b, u=u, y=y, cw=cw: b.scalar_tensor_tensor(
                out=y.h[:, 0:TT], in0=u[:, 1:TT + 1], scalar=cw(1), in1=y.h[:, 0:TT], op0=ALU.mult, op1=ALU.add),
                reads=[ub.t, t_prm], writes=[y.t])
            fw.op(dve, lambda b, u=u, y=y, cw=cw: b.scalar_tensor_tensor(
                out=y.h[:, 0:TT], in0=u[:, 0:TT], scalar=cw(0), in1=y.h[:, 0:TT], op0=ALU.mult, op1=ALU.add),
                reads=[ub.t, t_prm], writes=[y.t])
            fw.op(dve, lambda b, u=u, cc=cc: b.tensor_copy(out=ucar[:, cc, :], in_=u[:, TT:TT + 2]),
                  reads=[ub.t], writes=[t_u[cc]])
            fw.op(dve, lambda b, y=y, gb_ps=gb_ps: b.tensor_tensor(out=y.h[:, 0:TT], in0=y.h[:, 0:TT], in1=gb_ps.h[:, 0:TT],
                                                                    op=ALU.mult), reads=[gb_ps.t], writes=[y.t])
            sq = gn_front(y.h[:, 0:TT], [y.t])
            pend = (sq, 1.0 / 64, bd_bf[:, :], ring,
                    [(mixed[:, 4 + cc, :], y.h[:, 0:TT], prm[:, P_GCN + cc:P_GCN + cc + 1], ALLP, [y.t], t_mx[4 + cc])])
        if hook_b is not None:
            hook_b()
        gn_back(*pend)
        pend = None
        nkb = 4 * j + 4
        LAG = 2

        def att_p1(hd, acc, zz):
            ls, cs = [], []
            for m in range(2):
                l = scr.alloc()
                fw.op(act, lambda b, l=l, m=m: b.activation(out=l.h[:, 0:TT], in_=zz[m].h[:, 0:TT], func=AF.Copy),
                      reads=[zz[m].t], writes=[l.t])
                ls.append(l)
            for m in (1, 0):
                c = scr.alloc()
                fw.op(dve, lambda b, c=c, m=m: b.tensor_copy(out=c.h[:, 0:TT], in_=acc[m].h[:, 0:TT]),
                      reads=[acc[m].t], writes=[c.t])
                cs.insert(0, c)
            return (hd, ls, cs)

        def att_p2(st):
            hd, ls, cs = st
            for m in range(2):
                fw.op(dve, lambda b, l=ls[m]: b.reciprocal(out=l.h[:, 0:TT], in_=l.h[:, 0:TT]), reads=[ls[m].t], writes=[ls[m].t])
            for m in range(2):
                fw.op(dve, lambda b, c=cs[m], l=ls[m]: b.tensor_tensor(out=c.h[:, 0:TT], in0=c.h[:, 0:TT], in1=l.h[:, 0:TT],
                                                                        op=ALU.mult), reads=[ls[m].t], writes=[cs[m].t])
            fw.op(dve, lambda b: b.scalar_tensor_tensor(
                out=cs[0].h[:, 0:TT], in0=cs[1].h[:, 0:TT], scalar=drv[:, 2:3], in1=cs[0].h[:, 0:TT], op0=ALU.mult, op1=ALU.add),
                reads=[cs[1].t, t_drv], writes=[cs[0].t])
            a = cs[0]
            sq = sqring.alloc()
            fw.op(dve, lambda b: b.tensor_tensor(out=sq.h[:, 0:TT], in0=a.h[:, 0:TT], in1=a.h[:, 0:TT], op=ALU.mult),
                  reads=[a.t], writes=[sq.t])
            return (hd, a, sq)

        def att_p3(st, ring):
            hd, a, sq = st
            gn_back(sq, 1.0 / 128, ones_bf[:, :], ring,
                    [(mixed[:, hd, :], a.h[:, 0:TT], drv[:, 1:2], ALLP, [a.t], t_mx[hd])])

        st1 = None
        st2 = None
        for hd in range(4):
            acc = [psB.alloc(), psB.alloc()]
            zz = [psB.alloc(), psB.alloc()]
            inflight = []
            for kb in range(nkb + LAG):
                if kb < nkb:
                    rel = kb - 4 * j
                    koff = max(0, rel) * 128
                    s3, ta, tb_ = pairA.alloc()

                    def emit_s(b, s3=s3, kb=kb, koff=koff, hd=hd):
                        ins = None
                        for m in range(2):
                            ins = b.matmul(s3[:, m, koff:TT], lhsT=kT[:, hd, kb * 128:(kb + 1) * 128],
                                           rhs=qz[:, hd, m, koff:TT], start=True, stop=True)
                        return ins
                    fw.op(pe, emit_s, reads=[t_k[hd][kb // 4], t_q[hd]], writes=[ta, tb_])
                    p = pring.alloc()
                    bcol = 128 + hd * NREL + (rel + NREL - 4)
                    fw.op(act, lambda b, p=p, s3=s3, koff=koff, bcol=bcol: b.activation(
                        out=p.h[:, :, koff:TT], in_=s3[:, :, koff:TT], func=AF.Exp, bias=cst[:, bcol:bcol + 1], scale=1.0),
                        reads=[ta, tb_, t_cst], writes=[p.t])
                    if rel >= 0:
                        fw.op(dve, lambda b, p=p, koff=koff: b.tensor_tensor(
                            out=p.h[:, :, koff:koff + 128], in0=p.h[:, :, koff:koff + 128], in1=tri2[:, :, :], op=ALU.mult),
                            reads=[t_cbf], writes=[p.t])
                    inflight.append((kb, koff, p))
                if kb == 0 and st1 is not None:
                    st2 = att_p2(st1)
                    st1 = None
                if kb >= LAG:
                    pkb, pkoff, pp = inflight.pop(0)

                    def emit_pv(b, kb=pkb, koff=pkoff, p=pp, hd=hd, acc=acc, zz=zz):
                        ins = None
                        for m in range(2):
                            b.matmul(acc[m].h[:, koff:TT], lhsT=Vb[:, kb, hd * 128:(hd + 1) * 128], rhs=p.h[:, m, koff:TT],
                                     start=(kb == 0), stop=(kb == nkb - 1))
                            ins = b.matmul(zz[m].h[:, koff:TT], lhsT=ones_bf[:, :], rhs=p.h[:, m, koff:TT],
                                           start=(kb == 0), stop=(kb == nkb - 1))
                        return ins
                    fw.op(pe, emit_pv, reads=[pp.t, t_v[pkb], t_cbf], writes=[acc[0].t, acc[1].t, zz[0].t, zz[1].t])
                if kb == nkb - 1 and st2 is not None:
                    att_p3(st2, psA)
                    st2 = None
            st1 = att_p1(hd, acc, zz)
        st2 = att_p2(st1)
        return lambda ring: att_p3(st2, ring)

    def mixer_part2(xb, tail):
        x = xbuf[xb]
        tail(psA)
        for half in range(2):
            slot = load_piece("wout", half)
            for q in range(4):
                dco = half * 4 + q
                ps = psA.alloc()
                pairs = [(slot.h[:, c * 512 + q * 128:c * 512 + q * 128 + 128], mixed[:, c, :]) for c in range(8)]
                proj_group(ps, pairs, [slot.t] + t_mx)
                fw.op(dve, lambda b, ps=ps, dco=dco: b.tensor_tensor(out=x[:, dco, :], in0=ps.h[:, 0:TT], in1=x[:, dco, :],
                                                                      op=ALU.add), reads=[ps.t], writes=[t_x[xb][dco]])

    def load_x(j):
        xb = j % 2
        fw.dma(pool, xbuf[xb][:, :, :], xT[:, j * TT:(j + 1) * TT].rearrange("(dc p) s -> p dc s", p=128), ds_x[xb],
               writes=t_x[xb])

    finals = []
    load_x(0)
    norm_to_h(0, P_G1, hA)
    ffn_up("wgu1", hA)
    if NCH > 1:
        load_x(1)
    ffn_down("wd1", 0)
    norm_to_h(0, P_GM, hB)
    for j in range(NCH):
        xb = j % 2
        nxt = j + 1 < NCH
        hook_a = None
        if j > 0:
            def hook_a(j=j, nxt=nxt):
                finals.extend(final_norm_store(j - 1))
                if nxt:
                    load_x(j + 1)
        hook_b = (lambda j=j: norm_to_h((j + 1) % 2, P_G1, hA)) if nxt else None
        tail = mixer_part1(xb, j, hook_a, hook_b)
        if nxt:
            ffn_up("wgu1", hA, hook=lambda tail=tail: tail(psB))
            mixer_part2(xb, lambda ring: None)
            ffn_down("wd1", (j + 1) % 2, hook=lambda xb=xb: norm_to_h(xb, P_G2, hB))
        else:
            mixer_part2(xb, tail)
            norm_to_h(xb, P_G2, hB)
        ffn_up("wgu2", hB)
        ffn_down("wd2", xb, hook=(lambda j=j: norm_to_h((j + 1) % 2, P_GM, hB)) if nxt else None)
    finals.extend(final_norm_store(NCH - 1))
    fw.finish(finals)
    return nc


def _consts():
    cst = np.zeros((128, 256), np.float32)
    ki = np.arange(128)
    cst[:, 0:128] = (ki[:, None] <= ki[None, :]).astype(np.float32)
    for h in range(4):
        for r in range(NREL):
            rel = r - (NREL - 4)
            cst[:, 128 + h * NREL + r] = SLOPES[h] * (ki + 128.0 * rel - 256.0)
    return cst


def _layout_weights(inp):
    f32 = lambda a: np.ascontiguousarray(np.asarray(a, dtype=np.float32))
    out = {}
    for i, tag in ((1, "ffn1"), (2, "ffn2")):
        wg = f32(inp[f"{tag}_w_gate"])[0].reshape(8, 128, NFG, 256)
        wu = f32(inp[f"{tag}_w_up"])[0].reshape(8, 128, NFG, 256)
        gu = np.stack([wg, wu], axis=0)
        out[f"wgu{i}"] = np.ascontiguousarray(gu.transpose(3, 2, 0, 1, 4)).reshape(NFG, 128, 4096)
        wd = f32(inp[f"{tag}_w_down"])[0].reshape(NFC, 128, 2, 512)
        out[f"wd{i}"] = np.ascontiguousarray(wd.transpose(2, 1, 0, 3)).reshape(2, 128, NFC * 512)
    win = f32(inp["w_in"])[0].reshape(8, 128, 6, 512)
    out["win"] = np.ascontiguousarray(win.transpose(2, 1, 0, 3)).reshape(6, 128, 4096)
    wo = f32(inp["w_out"])[0].reshape(8, 128, 2, 512)
    out["wout"] = np.ascontiguousarray(wo.transpose(2, 1, 0, 3)).reshape(2, 128, 4096)
    prm = np.zeros((128, NPRM), np.float32)
    for col, key in ((P_G1, "ffn1_norm"), (P_GM, "mix_norm"), (P_G2, "ffn2_norm"), (P_GF, "final_norm")):
        prm[:, col:col + 8] = f32(inp[key])[0].reshape(8, 128).T
    prm[:, P_GQ] = np.tile(f32(inp["q_norm"])[0], 2)
    prm[:, P_GK] = np.tile(f32(inp["k_norm"])[0], 2)
    prm[:, P_GSUB] = f32(inp["attn_subln"])[0]
    prm[:, P_GCN:P_GCN + 4] = f32(inp["conv_norm"])[0].reshape(4, 128).T
    cw = f32(inp["conv_w"])[0]
    for cc in range(4):
        for k in range(3):
            prm[:, P_CW + cc * 3 + k] = cw[k, cc * 128:(cc + 1) * 128]
    for i, key in enumerate(("lambda_q1", "lambda_k1", "lambda_q2", "lambda_k2")):
        prm[:, P_LAM + i * 64:P_LAM + (i + 1) * 64] = f32(inp[key])[0][None, :]
    out["prm"] = prm
    out["cst"] = _consts()
    return out


_NC_CACHE = {}


def _get_nc(nch):
    if nch not in _NC_CACHE:
        _NC_CACHE[nch] = build_nc(nch)
    return _NC_CACHE[nch]


def kernel(**inputs):
    x = np.asarray(inputs["x"], dtype=np.float32)
    B, S, _ = x.shape
    nch = S // TT
    shared = _layout_weights(inputs)
    in_maps = []
    for b in range(B):
        m = dict(shared)
        m["xT"] = np.ascontiguousarray(x[b].T)
        in_maps.append(m)
    nc = build_nc(nch)
    res = run_bass_kernel_spmd(nc, in_maps, core_ids=list(range(B)))
    out = np.stack([np.ascontiguousarray(np.asarray(r["outT"]).T) for r in res.results], axis=0)
    return out.astype(np.float32)
```

```python
import math
import numpy as np
import concourse.bass as bass
import concourse.mybir as mybir
from concourse.bass_utils import run_bass_kernel_spmd

F32 = mybir.dt.float32
BF16 = mybir.dt.bfloat16
ALU = mybir.AluOpType
AF = mybir.ActivationFunctionType
AX = mybir.AxisListType

D = 1024
DFF = 2816
NFC = 22
NFG = 11
TT = 512
EPS = 1e-6
LAM_INIT = 0.8 - 0.6 * math.exp(-0.3 * 0)
SLOPES = [2.0 ** (-8.0 * (i + 1) / 4) for i in range(4)]
NREL = 32

P_G1, P_GM, P_G2, P_GF = 0, 8, 16, 24
P_GQ, P_GK, P_GSUB = 32, 33, 34
P_GCN = 35
P_CW = 39
P_LAM = 51
NPRM = P_LAM + 256


class Tl:
    __slots__ = ("name", "w", "r", "excl")

    def __init__(self, name, excl=False):
        self.name = name
        self.w = None
        self.r = []
        self.excl = excl


class DSem:
    def __init__(self, nc, name):
        self.sem = nc.alloc_semaphore(name)
        self.key = name
        self.count = 0


class Eng:
    def __init__(self, nc, name, b):
        self.name = name
        self.b = b
        self.sem = nc.alloc_semaphore("s_" + name)
        self.key = "s_" + name
        self.n = 0
        self.seen = {}
        self.q = []

    def wait(self, ev):
        key, sem, val = ev
        if self.seen.get(key, 0) >= val:
            return
        self.seen[key] = val
        self.q.append(lambda b, sem=sem, val=val: b.wait_ge(sem, val))


class FW:
    def __init__(self, nc):
        self.nc = nc
        self.pe = Eng(nc, "pe", nc.tensor)
        self.act = Eng(nc, "act", nc.scalar)
        self.dve = Eng(nc, "dve", nc.vector)
        self.pool = Eng(nc, "pool", nc.gpsimd)
        self.sp = Eng(nc, "sp", nc.sync)
        self.nds = 0

    def dsem(self, name=None):
        self.nds += 1
        return DSem(self.nc, name or f"d{self.nds}")

    def _deps(self, eng, reads, writes):
        deps = []
        for t in reads:
            if t.w is not None:
                deps.append(t.w)
            if t.excl:
                deps.extend(t.r)
        for t in writes:
            if t.w is not None:
                deps.append(t.w)
            deps.extend(t.r)
        if eng is self.pe:
            deps = [d for d in deps if d[0] != self.pe.key]
        return deps

    def op(self, eng, emit, reads=(), writes=()):
        for d in self._deps(eng, reads, writes):
            eng.wait(d)
        eng.n += 1
        sem = eng.sem
        eng.q.append(lambda b, emit=emit, sem=sem: emit(b).then_inc(sem, 1))
        ev = (eng.key, eng.sem, eng.n)
        for t in reads:
            t.r.append(ev)
        for t in writes:
            t.w = ev
            t.r = []
        return ev

    def dma(self, eng, out, in_, ds, reads=(), writes=()):
        for d in self._deps(eng, reads, writes):
            eng.wait(d)
        ds.count += 16
        sem = ds.sem
        eng.q.append(lambda b, out=out, in_=in_, sem=sem: b.dma_start(out=out, in_=in_).then_inc(sem, 16))
        ev = (ds.key, ds.sem, ds.count)
        for t in reads:
            t.r.append(ev)
        for t in writes:
            t.w = ev
            t.r = []
        return ev

    def finish(self, final_events=()):
        for ev in final_events:
            self.sp.wait(ev)
        with self.nc.Block() as block:
            @block.sync
            def _(e):
                for f in self.sp.q:
                    f(e)

            @block.tensor
            def _(e):
                for f in self.pe.q:
                    f(e)

            @block.scalar
            def _(e):
                for f in self.act.q:
                    f(e)

            @block.vector
            def _(e):
                for f in self.dve.q:
                    f(e)

            @block.gpsimd
            def _(e):
                for f in self.pool.q:
                    f(e)


class Buf:
    __slots__ = ("h", "t", "ds")

    def __init__(self, h, t, ds=None):
        self.h = h
        self.t = t
        self.ds = ds


class BankView:
    def __init__(self, t, m):
        self.t = t
        self.m = m

    def __getitem__(self, idx):
        p, c = idx
        return self.t[p, self.m, c]


class Ring:
    def __init__(self, bufs):
        self.bufs = bufs
        self.i = 0

    def alloc(self):
        b = self.bufs[self.i % len(self.bufs)]
        self.i += 1
        return b


def build_nc(NCH, dbg=False):
    S = NCH * TT
    NKB = S // 128
    nc = bass.Bass("TRN2", target_bir_lowering=False)
    fw = FW(nc)
    pe, act, dve, pool, sp = fw.pe, fw.act, fw.dve, fw.pool, fw.sp

    xT = nc.dram_tensor("xT", [D, S], F32, kind="ExternalInput").ap()
    outT = nc.dram_tensor("outT", [D, S], F32, kind="ExternalOutput").ap()
    prm_d = nc.dram_tensor("prm", [128, NPRM], F32, kind="ExternalInput").ap()
    cst_d = nc.dram_tensor("cst", [128, 256], F32, kind="ExternalInput").ap()
    w32 = {}
    wbf = {}
    wshape = {"wgu1": [NFG, 128, 4096], "wd1": [2, 128, NFC * 512], "win": [6, 128, 4096],
              "wout": [2, 128, 4096], "wgu2": [NFG, 128, 4096], "wd2": [2, 128, NFC * 512]}
    for k, shp in wshape.items():
        w32[k] = nc.dram_tensor(k, shp, F32, kind="ExternalInput").ap()
        wbf[k] = nc.dram_tensor(k + "_bf", shp, BF16, kind="Internal").ap()
    wtl = {k: Tl(f"wt_{k}") for k in wshape}
    ds_wb = {k: fw.dsem(f"dwb_{k}") for k in wshape}

    def sb(name, shape, dt):
        return nc.alloc_sbuf_tensor(name, shape, dt)

    xbuf = [sb(f"x{i}", [128, 8, TT], F32) for i in range(2)]
    t_x = [[Tl(f"x{i}_{dc}") for dc in range(8)] for i in range(2)]
    ds_x = [fw.dsem(f"dx{i}") for i in range(2)]
    ds_o = [fw.dsem(f"do{i}") for i in range(2)]
    hA = (sb("hA", [128, 8, TT], BF16), [Tl(f"hA{dc}") for dc in range(8)])
    hB = (sb("hB", [128, 8, TT], BF16), [Tl(f"hB{dc}") for dc in range(8)])
    actb = sb("act", [128, NFC, TT], BF16)
    t_act = [Tl(f"act{fc}") for fc in range(NFC)]
    qz = sb("qz", [128, 4, 2, TT], BF16)
    t_q = [Tl(f"q{h}") for h in range(4)]
    kT = sb("kT", [128, 4, S], BF16)
    t_k = [[Tl(f"k{h}_{j}") for j in range(NCH)] for h in range(4)]
    Vb = sb("V", [128, NKB, 512], BF16)
    t_v = [Tl(f"v{kb}") for kb in range(NKB)]
    mixed = sb("mixed", [128, 8, TT], BF16)
    t_mx = [Tl(f"mx{c}") for c in range(8)]
    ucar = sb("ucar", [128, 4, 2], F32)
    t_u = [Tl(f"u{cc}") for cc in range(4)]
    wring = Ring([Buf(sb(f"wr{i}", [128, 4096], BF16), Tl(f"wr{i}"), fw.dsem(f"dwr{i}")) for i in range(4)])
    pring = Ring([Buf(sb(f"pr{i}", [128, 2, TT], BF16), Tl(f"pr{i}")) for i in range(3)])
    sqring = Ring([Buf(sb(f"sq{i}", [128, TT], BF16), Tl(f"sq{i}")) for i in range(2)])
    sqn = Ring([Buf(sb(f"sqn{i}", [128, TT], BF16), Tl(f"sqn{i}")) for i in range(2)])
    scr = Ring([Buf(sb(f"sc{i}", [128, TT + 2], F32), Tl(f"sc{i}"), fw.dsem(f"dsc{i}")) for i in range(6)])
    prm = sb("prm_s", [128, P_LAM], F32)
    t_prm = Tl("prm")
    cst = sb("cst_s", [128, 256], F32)
    t_cst = Tl("cst")
    drv = sb("drv", [128, 8], F32)
    t_drv = Tl("drv")
    t_dummy = Tl("dummy")
    lamt = sb("lamt", [128, 2, 64], F32)
    lame = sb("lame", [128, 4], F32)
    t_lam = Tl("lam")
    ones_bf = sb("ones_bf", [128, 128], BF16)
    bd_bf = sb("bd_bf", [128, 128], BF16)
    tri2 = sb("tri2", [128, 2, 128], BF16)
    t_cbf = Tl("cbf")

    pa = [nc.alloc_psum_tensor(f"pa{i}", [128, 2, TT], F32) for i in range(2)]
    pb = [nc.alloc_psum_tensor(f"pb{i}", [128, TT], F32) for i in range(4)]
    bankA = [Buf(BankView(pa[i // 2], i % 2), Tl(f"psA{i}", excl=True)) for i in range(4)]
    bankB = [Buf(pb[i], Tl(f"psB{i}", excl=True)) for i in range(4)]
    psA = Ring(bankA)
    psB = Ring(bankB)
    pairA = Ring([(pa[0], bankA[0].t, bankA[1].t), (pa[1], bankA[2].t, bankA[3].t)])

    fw.dma(sp, prm[:, :], prm_d[:, 0:P_LAM], fw.dsem("dprm"), writes=[t_prm])
    lamb = scr.alloc()
    fw.dma(sp, lamb.h[:, 0:256], prm_d[:, P_LAM:P_LAM + 256], fw.dsem("dlam"), writes=[lamb.t])
    fw.dma(sp, cst[:, :], cst_d, fw.dsem("dcst"), writes=[t_cst])
    fw.op(dve, lambda b: b.memset(ones_bf[:, :], 1.0), writes=[t_cbf])
    fw.op(dve, lambda b: b.memset(bd_bf[:, :], 0.0), writes=[t_cbf])
    fw.op(dve, lambda b: b.memset(bd_bf[0:64, 0:64], 1.0), writes=[t_cbf])
    fw.op(dve, lambda b: b.memset(bd_bf[64:128, 64:128], 1.0), writes=[t_cbf])
    for m in range(2):
        fw.op(dve, lambda b, m=m: b.tensor_copy(out=tri2[:, m, :], in_=cst[:, 0:128]), reads=[t_cst], writes=[t_cbf])
    for cc in range(4):
        fw.op(dve, lambda b, cc=cc: b.memset(ucar[:, cc, :], 0.0), writes=[t_u[cc]])
    for hd in range(4):
        fw.op(dve, lambda b, hd=hd: b.memset(qz[:, hd, :, :], 0.0), writes=[t_q[hd]])
    fw.op(dve, lambda b: b.tensor_scalar(out=drv[:, 0:1], in0=prm[:, P_GQ:P_GQ + 1], scalar1=0.125, scalar2=None,
                                          op0=ALU.mult), reads=[t_prm], writes=[t_drv])
    fw.op(dve, lambda b: b.tensor_scalar(out=drv[:, 1:2], in0=prm[:, P_GSUB:P_GSUB + 1], scalar1=1.0 - LAM_INIT,
                                          scalar2=None, op0=ALU.mult), reads=[t_prm], writes=[t_drv])
    fw.op(dve, lambda b: b.memset(drv[:, 3:4], EPS), writes=[t_drv])
    fw.op(dve, lambda b: b.tensor_tensor(out=lamt[:, 0, :], in0=lamb.h[:, 0:64],
                                          in1=lamb.h[:, 64:128], op=ALU.mult), reads=[lamb.t], writes=[t_lam])
    fw.op(dve, lambda b: b.tensor_tensor(out=lamt[:, 1, :], in0=lamb.h[:, 128:192],
                                          in1=lamb.h[:, 192:256], op=ALU.mult), reads=[lamb.t], writes=[t_lam])
    fw.op(dve, lambda b: b.tensor_reduce(out=lame[:, 0:2], in_=lamt[:, :, :], axis=AX.X, op=ALU.add),
          reads=[t_lam], writes=[t_lam])
    fw.op(act, lambda b: b.activation(out=lame[:, 2:4], in_=lame[:, 0:2], func=AF.Exp), reads=[t_lam], writes=[t_lam])
    fw.op(dve, lambda b: b.tensor_tensor(out=lame[:, 0:1], in0=lame[:, 3:4], in1=lame[:, 2:3], op=ALU.subtract),
          reads=[t_lam], writes=[t_lam])
    fw.op(dve, lambda b: b.tensor_scalar(out=drv[:, 2:3], in0=lame[:, 0:1], scalar1=-LAM_INIT, scalar2=None,
                                          op0=ALU.add), reads=[t_lam], writes=[t_drv])
    eps_col = drv[:, 3:4]

    seen_pieces = set()

    def load_piece(key, idx, col0=0, ncol=4096):
        slot = wring.alloc()
        pk = (key, idx, col0)
        if pk not in seen_pieces:
            seen_pieces.add(pk)
            fw.dma(pool, slot.h[:, 0:ncol], w32[key][idx][:, col0:col0 + ncol], slot.ds, writes=[slot.t])
            ev = fw.dma(sp, wbf[key][idx][:, col0:col0 + ncol], slot.h[:, 0:ncol], ds_wb[key], reads=[slot.t])
            wtl[key].w = ev
        else:
            fw.dma(sp, slot.h[:, 0:ncol], wbf[key][idx][:, col0:col0 + ncol], slot.ds,
                   reads=[wtl[key]], writes=[slot.t])
        return slot

    def mm_group(out_ap, pairs, start=True, stop=True):
        def emit(b):
            ins = None
            n = len(pairs)
            for i, (l, r) in enumerate(pairs):
                ins = b.matmul(out_ap, lhsT=l, rhs=r, start=(start and i == 0), stop=(stop and i == n - 1))
            return ins
        return emit

    def proj_group(ps, pairs, common_reads, per_reads=None):
        if per_reads is None:
            fw.op(pe, mm_group(ps.h[:, 0:TT], pairs), reads=common_reads, writes=[ps.t])
        else:
            n = len(pairs)
            for i, (l, r) in enumerate(pairs):
                fw.op(pe, lambda b, l=l, r=r, i=i: b.matmul(ps.h[:, 0:TT], lhsT=l, rhs=r, start=(i == 0), stop=(i == n - 1)),
                      reads=list(common_reads) + list(per_reads[i]), writes=[ps.t])

    def preload_ln_table():
        fw.op(act, lambda b: b.activation(out=drv[:, 4:5], in_=drv[:, 3:4], func=AF.Ln), reads=[t_drv], writes=[t_dummy])

    def rstd_from(ps, inv_n):
        r = scr.alloc()
        fw.op(act, lambda b: b.activation(out=r.h[:, 0:TT], in_=ps.h[:, 0:TT], func=AF.Ln, bias=eps_col, scale=inv_n),
              reads=[ps.t, t_drv], writes=[r.t])
        fw.op(act, lambda b: b.activation(out=r.h[:, 0:TT], in_=r.h[:, 0:TT], func=AF.Exp, scale=-0.5),
              reads=[r.t], writes=[r.t])
        return r

    def norm_stats(xb):
        x = xbuf[xb]
        ps = psA.alloc()
        for dc in range(8):
            sq = sqn.alloc()
            fw.op(act, lambda b, sq=sq, dc=dc: b.activation(out=sq.h[:, 0:TT], in_=x[:, dc, :], func=AF.Square),
                  reads=[t_x[xb][dc]], writes=[sq.t])
            fw.op(pe, lambda b, sq=sq, dc=dc: b.matmul(ps.h[:, 0:TT], lhsT=ones_bf[:, :], rhs=sq.h[:, 0:TT],
                                                       start=(dc == 0), stop=(dc == 7)),
                  reads=[sq.t, t_cbf], writes=[ps.t])
        return rstd_from(ps, 1.0 / D)

    def norm_to_h(xb, gcol, hbuf):
        hb, t_h = hbuf
        x = xbuf[xb]
        r = norm_stats(xb)
        for dc in range(8):
            fw.op(dve, lambda b, dc=dc: b.scalar_tensor_tensor(
                out=hb[:, dc, :], in0=x[:, dc, :], scalar=prm[:, gcol + dc:gcol + dc + 1], in1=r.h[:, 0:TT],
                op0=ALU.mult, op1=ALU.mult), reads=[t_x[xb][dc], r.t, t_prm], writes=[t_h[dc]])

    def final_norm_store(j):
        xb = j % 2
        x = xbuf[xb]
        r = norm_stats(xb)
        for dc in range(8):
            fw.op(dve, lambda b, dc=dc: b.scalar_tensor_tensor(
                out=x[:, dc, :], in0=x[:, dc, :], scalar=prm[:, P_GF + dc:P_GF + dc + 1], in1=r.h[:, 0:TT],
                op0=ALU.mult, op1=ALU.mult), reads=[r.t, t_prm], writes=[t_x[xb][dc]])
        return [fw.dma(pool, outT[:, j * TT:(j + 1) * TT].rearrange("(dc p) s -> p dc s", p=128), x[:, :, :],
                       ds_o[xb], reads=t_x[xb])]

    def ffn_up(kgu, hbuf, hook=None):
        hb, t_h = hbuf
        for fg in range(NFG):
            slot = load_piece(kgu, fg)
            for f2 in range(2):
                fc = fg * 2 + f2
                g_ps = psA.alloc()
                u_ps = psA.alloc()
                prs = []
                for which in range(2):
                    prs.append([(slot.h[:, (which * 8 + dc) * 256 + f2 * 128:(which * 8 + dc) * 256 + f2 * 128 + 128],
                                 hb[:, dc, :]) for dc in range(8)])
                if fc == 0:
                    for dc in range(8):
                        for which, ps in ((0, g_ps), (1, u_ps)):
                            l, r = prs[which][dc]
                            fw.op(pe, lambda b, l=l, r=r, ps=ps, dc=dc: b.matmul(ps.h[:, 0:TT], lhsT=l, rhs=r,
                                                                                 start=(dc == 0), stop=(dc == 7)),
                                  reads=[slot.t, t_h[dc]], writes=[ps.t])
                else:
                    for which, ps in ((0, g_ps), (1, u_ps)):
                        fw.op(pe, mm_group(ps.h[:, 0:TT], prs[which]), reads=[slot.t] + t_h, writes=[ps.t])
                sg = scr.alloc()
                fw.op(act, lambda b, sg=sg, g_ps=g_ps: b.activation(out=sg.h[:, 0:TT], in_=g_ps.h[:, 0:TT], func=AF.Silu),
                      reads=[g_ps.t], writes=[sg.t])
                fw.op(dve, lambda b, sg=sg, u_ps=u_ps, fc=fc: b.tensor_tensor(
                    out=actb[:, fc, :], in0=sg.h[:, 0:TT], in1=u_ps.h[:, 0:TT], op=ALU.mult),
                    reads=[sg.t, u_ps.t], writes=[t_act[fc]])
            if fg == 1 and hook is not None:
                hook()

    def ffn_down(kd, xb, hook=None):
        x = xbuf[xb]
        preload_ln_table()
        for half in range(2):
            ring = psB if half == 0 else psA
            accs = [ring.alloc() for _ in range(4)]
            for (fc0, fc1) in ((0, 8), (8, 16), (16, 22)):
                slot = load_piece(kd, half, fc0 * 512, (fc1 - fc0) * 512)
                for fc in range(fc0, fc1):
                    def emit(b, fc=fc, fc0=fc0, slot=slot, accs=accs):
                        ins = None
                        for q in range(4):
                            ins = b.matmul(accs[q].h[:, 0:TT],
                                           lhsT=slot.h[:, (fc - fc0) * 512 + q * 128:(fc - fc0) * 512 + q * 128 + 128],
                                           rhs=actb[:, fc, :], start=(fc == 0), stop=(fc == NFC - 1))
                        return ins
                    fw.op(pe, emit, reads=[slot.t, t_act[fc]], writes=[a.t for a in accs])
                if half == 0 and fc0 == 0 and hook is not None:
                    hook()
            for q in range(4):
                dco = half * 4 + q
                fw.op(dve, lambda b, q=q, dco=dco, accs=accs: b.scalar_tensor_tensor(
                    out=x[:, dco, :], in0=accs[q].h[:, 0:TT], scalar=0.5, in1=x[:, dco, :],
                    op0=ALU.mult, op1=ALU.add), reads=[accs[q].t], writes=[t_x[xb][dco]])

    def gn_front(src_ap, src_tiles):
        sq = sqring.alloc()
        fw.op(act, lambda b: b.activation(out=sq.h[:, 0:TT], in_=src_ap, func=AF.Square), reads=src_tiles, writes=[sq.t])
        return sq

    def gn_back(sq, inv_n, bdmat, ring, finals):
        ss = ring.alloc()
        fw.op(pe, lambda b: b.matmul(ss.h[:, 0:TT], lhsT=bdmat, rhs=sq.h[:, 0:TT], start=True, stop=True),
              reads=[sq.t, t_cbf], writes=[ss.t])
        r = rstd_from(ss, inv_n)
        for (dst_ap, src_ap, gain_ap, psl, reads, dst_tile) in finals:
            fw.op(dve, lambda b, dst_ap=dst_ap, src_ap=src_ap, gain_ap=gain_ap, psl=psl: b.scalar_tensor_tensor(
                out=dst_ap, in0=src_ap, scalar=gain_ap, in1=r.h[psl, 0:TT], op0=ALU.mult, op1=ALU.mult),
                reads=list(reads) + [r.t, t_prm, t_drv], writes=[dst_tile])

    ALLP = slice(0, 128)

    def mixer_part1(xb, j, hook_a=None, hook_b=None):
        hb, t_h = hB
        x = xbuf[xb]
        items = [(which, hd) for which in range(2) for hd in range(4)]
        slots = {}
        pend = None
        for i, (which, hd) in enumerate(items):
            if which not in slots:
                slots[which] = load_piece("win", which)
            slot = slots[which]
            ring = psA if i % 2 == 0 else psB
            ps = ring.alloc()
            pairs = [(slot.h[:, dc * 512 + hd * 128:dc * 512 + hd * 128 + 128], hb[:, dc, :]) for dc in range(8)]
            if i == 0:
                proj_group(ps, pairs, [slot.t], per_reads=[[t_h[dc]] for dc in range(8)])
            else:
                proj_group(ps, pairs, [slot.t] + t_h)
            sq = gn_front(ps.h[:, 0:TT], [ps.t])
            if which == 0:
                finals = [(qz[0:64, hd, 0, :], ps.h[0:64, 0:TT], drv[0:64, 0:1], slice(0, 64), [ps.t], t_q[hd]),
                          (qz[64:128, hd, 1, :], ps.h[64:128, 0:TT], drv[64:128, 0:1], slice(64, 128), [ps.t], t_q[hd])]
            else:
                finals = [(kT[:, hd, j * TT:(j + 1) * TT], ps.h[:, 0:TT], prm[:, P_GK:P_GK + 1], ALLP, [ps.t], t_k[hd][j])]
            if pend is not None:
                gn_back(*pend)
            pend = (sq, 1.0 / 64, bd_bf[:, :], ring, finals)
            if i == 2 and hook_a is not None:
                hook_a()
        slot = load_piece("win", 2)
        for tb in range(4):
            ps = psA.alloc()
            pairs = [(hb[:, dc, tb * 128:(tb + 1) * 128], slot.h[:, dc * 512:(dc + 1) * 512]) for dc in range(8)]
            proj_group(ps, pairs, [slot.t] + t_h)
            if tb == 0:
                gn_back(*pend)
                pend = None
            kb = j * 4 + tb
            fw.op(act, lambda b, ps=ps, kb=kb: b.activation(out=Vb[:, kb, :], in_=ps.h[:, 0:TT], func=AF.Copy),
                  reads=[ps.t], writes=[t_v[kb]])
        sl_b = load_piece("win", 3)
        sl_c = load_piece("win", 4)
        sl_h = load_piece("win", 5)
        pend = None
        for cc in range(4):
            ring = psA if cc % 2 == 0 else psB
            pss = []
            for sl in (sl_b, sl_c, sl_h):
                ps = ring.alloc()
                pairs = [(sl.h[:, dc * 512 + cc * 128:dc * 512 + cc * 128 + 128], hb[:, dc, :]) for dc in range(8)]
                proj_group(ps, pairs, [sl.t] + t_h)
                pss.append(ps)
            if pend is not None:
                gn_back(*pend)
                pend = None
            gb_ps, gc_ps, hc_ps = pss
            hcs = scr.alloc()
            fw.op(act, lambda b, hcs=hcs, hc_ps=hc_ps: b.activation(out=hcs.h[:, 0:TT], in_=hc_ps.h[:, 0:TT], func=AF.Copy),
                  reads=[hc_ps.t], writes=[hcs.t])
            ub = scr.alloc()
            u = ub.h
            fw.op(dve, lambda b, u=u, cc=cc: b.tensor_copy(out=u[:, 0:2], in_=ucar[:, cc, :]), reads=[t_u[cc]], writes=[ub.t])
            fw.op(dve, lambda b, u=u, hcs=hcs, gc_ps=gc_ps: b.tensor_tensor(
                out=u[:, 2:TT + 2], in0=hcs.h[:, 0:TT], in1=gc_ps.h[:, 0:TT], op=ALU.mult),
                reads=[hcs.t, gc_ps.t], writes=[ub.t])
            y = scr.alloc()
            cw = lambda k, cc=cc: prm[:, P_CW + cc * 3 + k:P_CW + cc * 3 + k + 1]
            fw.op(dve, lambda b, u=u, y=y, cw=cw: b.tensor_scalar(out=y.h[:, 0:TT], in0=u[:, 2:TT + 2], scalar1=cw(2),
                                                                   scalar2=None, op0=ALU.mult),
                  reads=[ub.t, t_prm], writes=[y.t])
            fw.op(dve, lambda b, u=u, y=y, cw=cw: b.scalar_tensor_tensor(
                out=y.h[:, 0:TT], in0=u[:, 1:TT + 1], scalar=cw(1), in1=y.h[:, 0:TT], op0=ALU.mult, op1=ALU.add),
                reads=[ub.t, t_prm], writes=[y.t])
            fw.op(dve, lambda b, u=u, y=y, cw=cw: b.scalar_tensor_tensor(
                out=y.h[:, 0:TT], in0=u[:, 0:TT], scalar=cw(0), in1=y.h[:, 0:TT], op0=ALU.mult, op1=ALU.add),
                reads=[ub.t, t_prm], writes=[y.t])
            fw.op(dve, lambda b, u=u, cc=cc: b.tensor_copy(out=ucar[:, cc, :], in_=u[:, TT:TT + 2]),
                  reads=[ub.t], writes=[t_u[cc]])
            fw.op(dve, lambda b, y=y, gb_ps=gb_ps: b.tensor_tensor(out=y.h[:, 0:TT], in0=y.h[:, 0:TT], in1=gb_ps.h[:, 0:TT],
                                                                    op=ALU.mult), reads=[gb_ps.t], writes=[y.t])
            cfin = [(mixed[:, 4 + cc, :], y.h[:, 0:TT], prm[:, P_GCN + cc:P_GCN + cc + 1], ALLP, [y.t], t_mx[4 + cc])]
            if cc == 3 and hook_b is not None:
                hook_b()
            sq = gn_front(y.h[:, 0:TT], [y.t])
            pend = (sq, 1.0 / 64, bd_bf[:, :], ring, cfin)
        gn_back(*pend)
        pend = None
        nkb = 4 * j + 4
        LAG = 2

        def att_p1(hd, acc, zz):
            ls, cs = [], []
            for m in range(2):
                l = scr.alloc()
                fw.op(act, lambda b, l=l, m=m: b.activation(out=l.h[:, 0:TT], in_=zz[m].h[:, 0:TT], func=AF.Copy),
                      reads=[zz[m].t], writes=[l.t])
                ls.append(l)
            for m in (1, 0):
                c = scr.alloc()
                fw.op(dve, lambda b, c=c, m=m: b.tensor_copy(out=c.h[:, 0:TT], in_=acc[m].h[:, 0:TT]),
                      reads=[acc[m].t], writes=[c.t])
                cs.insert(0, c)
            return (hd, ls, cs)

        def att_p2(st):
            hd, ls, cs = st
            for m in range(2):
                fw.op(dve, lambda b, l=ls[m]: b.reciprocal(out=l.h[:, 0:TT], in_=l.h[:, 0:TT]), reads=[ls[m].t], writes=[ls[m].t])
            for m in range(2):
                fw.op(dve, lambda b, c=cs[m], l=ls[m]: b.tensor_tensor(out=c.h[:, 0:TT], in0=c.h[:, 0:TT], in1=l.h[:, 0:TT],
                                                                        op=ALU.mult), reads=[ls[m].t], writes=[cs[m].t])
            fw.op(dve, lambda b: b.scalar_tensor_tensor(
                out=cs[0].h[:, 0:TT], in0=cs[1].h[:, 0:TT], scalar=drv[:, 2:3], in1=cs[0].h[:, 0:TT], op0=ALU.mult, op1=ALU.add),
                reads=[cs[1].t, t_drv], writes=[cs[0].t])
            a = cs[0]
            sq = sqring.alloc()
            fw.op(dve, lambda b: b.tensor_tensor(out=sq.h[:, 0:TT], in0=a.h[:, 0:TT], in1=a.h[:, 0:TT], op=ALU.mult),
                  reads=[a.t], writes=[sq.t])
            return (hd, a, sq)

        def att_p3(st, ring):
            hd, a, sq = st
            gn_back(sq, 1.0 / 128, ones_bf[:, :], ring,
                    [(mixed[:, hd, :], a.h[:, 0:TT], drv[:, 1:2], ALLP, [a.t], t_mx[hd])])

        st1 = None
        st2 = None
        for hd in range(4):
            acc = [psB.alloc(), psB.alloc()]
            zz = [psB.alloc(), psB.alloc()]
            inflight = []
            for kb in range(nkb + LAG):
                if kb < nkb:
                    rel = kb - 4 * j
                    koff = max(0, rel) * 128
                    s3, ta, tb_ = pairA.alloc()

                    def emit_s(b, s3=s3, kb=kb, koff=koff, hd=hd):
                        ins = None
                        for m in range(2):
                            ins = b.matmul(s3[:, m, koff:TT], lhsT=kT[:, hd, kb * 128:(kb + 1) * 128],
                                           rhs=qz[:, hd, m, koff:TT], start=True, stop=True)
                        return ins
                    fw.op(pe, emit_s, reads=[t_k[hd][kb // 4], t_q[hd]], writes=[ta, tb_])
                    p = pring.alloc()
                    bcol = 128 + hd * NREL + (rel + NREL - 4)
                    fw.op(act, lambda b, p=p, s3=s3, koff=koff, bcol=bcol: b.activation(
                        out=p.h[:, :, koff:TT], in_=s3[:, :, koff:TT], func=AF.Exp, bias=cst[:, bcol:bcol + 1], scale=1.0),
                        reads=[ta, tb_, t_cst], writes=[p.t])
                    if rel >= 0:
                        fw.op(dve, lambda b, p=p, koff=koff: b.tensor_tensor(
                            out=p.h[:, :, koff:koff + 128], in0=p.h[:, :, koff:koff + 128], in1=tri2[:, :, :], op=ALU.mult),
                            reads=[t_cbf], writes=[p.t])
                    inflight.append((kb, koff, p))
                if kb == 0 and st1 is not None:
                    st2 = att_p2(st1)
                    st1 = None
                if kb >= LAG:
                    pkb, pkoff, pp = inflight.pop(0)

                    def emit_pv(b, kb=pkb, koff=pkoff, p=pp, hd=hd, acc=acc, zz=zz):
                        ins = None
                        for m in range(2):
                            b.matmul(acc[m].h[:, koff:TT], lhsT=Vb[:, kb, hd * 128:(hd + 1) * 128], rhs=p.h[:, m, koff:TT],
                                     start=(kb == 0), stop=(kb == nkb - 1))
                            ins = b.matmul(zz[m].h[:, koff:TT], lhsT=ones_bf[:, :], rhs=p.h[:, m, koff:TT],
                                           start=(kb == 0), stop=(kb == nkb - 1))
                        return ins
                    fw.op(pe, emit_pv, reads=[pp.t, t_v[pkb], t_cbf], writes=[acc[0].t, acc[1].t, zz[0].t, zz[1].t])
                if kb == nkb - 1 and st2 is not None:
                    att_p3(st2, psA)
                    st2 = None
            st1 = att_p1(hd, acc, zz)
        st2 = att_p2(st1)
        return lambda ring: att_p3(st2, ring)

    def mixer_part2(xb, tail):
        x = xbuf[xb]
        tail(psA)
        for half in range(2):
            slot = load_piece("wout", half)
            for q in range(4):
                dco = half * 4 + q
                ps = psA.alloc()
                pairs = [(slot.h[:, c * 512 + q * 128:c * 512 + q * 128 + 128], mixed[:, c, :]) for c in range(8)]
                proj_group(ps, pairs, [slot.t] + t_mx)
                fw.op(dve, lambda b, ps=ps, dco=dco: b.tensor_tensor(out=x[:, dco, :], in0=ps.h[:, 0:TT], in1=x[:, dco, :],
                                                                      op=ALU.add), reads=[ps.t], writes=[t_x[xb][dco]])

    def load_x(j):
        xb = j % 2
        fw.dma(pool, xbuf[xb][:, :, :], xT[:, j * TT:(j + 1) * TT].rearrange("(dc p) s -> p dc s", p=128), ds_x[xb],
               writes=t_x[xb])

    finals = []
    load_x(0)
    norm_to_h(0, P_G1, hA)
    ffn_up("wgu1", hA)
    if NCH > 1:
        load_x(1)
    ffn_down("wd1", 0)
    norm_to_h(0, P_GM, hB)
    for j in range(NCH):
        xb = j % 2
        nxt = j + 1 < NCH
        hook_a = None
        if j > 0:
            def hook_a(j=j, nxt=nxt):
                finals.extend(final_norm_store(j - 1))
                if nxt:
                    load_x(j + 1)
        hook_b = (lambda j=j: norm_to_h((j + 1) % 2, P_G1, hA)) if nxt else None
        tail = mixer_part1(xb, j, hook_a, hook_b)
        if nxt:
            ffn_up("wgu1", hA, hook=lambda tail=tail: tail(psB))
            mixer_part2(xb, lambda ring: None)
            ffn_down("wd1", (j + 1) % 2, hook=lambda xb=xb: norm_to_h(xb, P_G2, hB))
        else:
            mixer_part2(xb, tail)
            norm_to_h(xb, P_G2, hB)
        ffn_up("wgu2", hB)
        ffn_down("wd2", xb, hook=(lambda j=j: norm_to_h((j + 1) % 2, P_GM, hB)) if nxt else None)
    finals.extend(final_norm_store(NCH - 1))
    fw.finish(finals)
    return nc


def _consts():
    cst = np.zeros((128, 256), np.float32)
    ki = np.arange(128)
    cst[:, 0:128] = (ki[:, None] <= ki[None, :]).astype(np.float32)
    for h in range(4):
        for r in range(NREL):
            rel = r - (NREL - 4)
            cst[:, 128 + h * NREL + r] = SLOPES[h] * (ki + 128.0 * rel - 256.0)
    return cst


def _layout_weights(inp):
    f32 = lambda a: np.ascontiguousarray(np.asarray(a, dtype=np.float32))
    out = {}
    for i, tag in ((1, "ffn1"), (2, "ffn2")):
        wg = f32(inp[f"{tag}_w_gate"])[0].reshape(8, 128, NFG, 256)
        wu = f32(inp[f"{tag}_w_up"])[0].reshape(8, 128, NFG, 256)
        gu = np.stack([wg, wu], axis=0)
        out[f"wgu{i}"] = np.ascontiguousarray(gu.transpose(3, 2, 0, 1, 4)).reshape(NFG, 128, 4096)
        wd = f32(inp[f"{tag}_w_down"])[0].reshape(NFC, 128, 2, 512)
        out[f"wd{i}"] = np.ascontiguousarray(wd.transpose(2, 1, 0, 3)).reshape(2, 128, NFC * 512)
    win = f32(inp["w_in"])[0].reshape(8, 128, 6, 512)
    out["win"] = np.ascontiguousarray(win.transpose(2, 1, 0, 3)).reshape(6, 128, 4096)
    wo = f32(inp["w_out"])[0].reshape(8, 128, 2, 512)
    out["wout"] = np.ascontiguousarray(wo.transpose(2, 1, 0, 3)).reshape(2, 128, 4096)
    prm = np.zeros((128, NPRM), np.float32)
    for col, key in ((P_G1, "ffn1_norm"), (P_GM, "mix_norm"), (P_G2, "ffn2_norm"), (P_GF, "final_norm")):
        prm[:, col:col + 8] = f32(inp[key])[0].reshape(8, 128).T
    prm[:, P_GQ] = np.tile(f32(inp["q_norm"])[0], 2)
    prm[:, P_GK] = np.tile(f32(inp["k_norm"])[0], 2)
    prm[:, P_GSUB] = f32(inp["attn_subln"])[0]
    prm[:, P_GCN:P_GCN + 4] = f32(inp["conv_norm"])[0].reshape(4, 128).T
    cw = f32(inp["conv_w"])[0]
    for cc in range(4):
        for k in range(3):
            prm[:, P_CW + cc * 3 + k] = cw[k, cc * 128:(cc + 1) * 128]
    for i, key in enumerate(("lambda_q1", "lambda_k1", "lambda_q2", "lambda_k2")):
        prm[:, P_LAM + i * 64:P_LAM + (i + 1) * 64] = f32(inp[key])[0][None, :]
    out["prm"] = prm
    out["cst"] = _consts()
    return out


_NC_CACHE = {}


def _get_nc(nch):
    if nch not in _NC_CACHE:
        _NC_CACHE[nch] = build_nc(nch)
    return _NC_CACHE[nch]


def kernel(**inputs):
    x = np.asarray(inputs["x"], dtype=np.float32)
    B, S, _ = x.shape
    nch = S // TT
    shared = _layout_weights(inputs)
    in_maps = []
    for b in range(B):
        m = dict(shared)
        m["xT"] = np.ascontiguousarray(x[b].T)
        in_maps.append(m)
    nc = build_nc(nch)
    res = run_bass_kernel_spmd(nc, in_maps, core_ids=list(range(B)))
    out = np.stack([np.ascontiguousarray(np.asarray(r["outT"]).T) for r in res.results], axis=0)
    return out.astype(np.float32)
```

```python
import math
import numpy as np
import concourse.bass as bass
import concourse.mybir as mybir
from concourse.bass_utils import run_bass_kernel_spmd

F32 = mybir.dt.float32
BF16 = mybir.dt.bfloat16
ALU = mybir.AluOpType
AF = mybir.ActivationFunctionType
AX = mybir.AxisListType

D = 1024
DFF = 2816
NFC = 22
NFG = 11
TT = 512
EPS = 1e-6
LAM_INIT = 0.8 - 0.6 * math.exp(-0.3 * 0)
SLOPES = [2.0 ** (-8.0 * (i + 1) / 4) for i in range(4)]
NREL = 32

P_G1, P_GM, P_G2, P_GF = 0, 8, 16, 24
P_GQ, P_GK, P_GSUB = 32, 33, 34
P_GCN = 35
P_CW = 39
P_LAM = 51
NPRM = P_LAM + 256


class Tl:
    __slots__ = ("name", "w", "r", "excl")

    def __init__(self, name, excl=False):
        self.name = name
        self.w = None
        self.r = []
        self.excl = excl


class DSem:
    def __init__(self, nc, name):
        self.sem = nc.alloc_semaphore(name)
        self.key = name
        self.count = 0


class Eng:
    def __init__(self, nc, name, b):
        self.name = name
        self.b = b
        self.sem = nc.alloc_semaphore("s_" + name)
        self.key = "s_" + name
        self.n = 0
        self.seen = {}
        self.q = []

    def wait(self, ev):
        key, sem, val = ev
        if self.seen.get(key, 0) >= val:
            return
        self.seen[key] = val
        self.q.append(lambda b, sem=sem, val=val: b.wait_ge(sem, val))


class FW:
    def __init__(self, nc):
        self.nc = nc
        self.pe = Eng(nc, "pe", nc.tensor)
        self.act = Eng(nc, "act", nc.scalar)
        self.dve = Eng(nc, "dve", nc.vector)
        self.pool = Eng(nc, "pool", nc.gpsimd)
        self.sp = Eng(nc, "sp", nc.sync)
        self.nds = 0

    def dsem(self, name=None):
        self.nds += 1
        return DSem(self.nc, name or f"d{self.nds}")

    def _deps(self, eng, reads, writes):
        deps = []
        for t in reads:
            if t.w is not None:
                deps.append(t.w)
            if t.excl:
                deps.extend(t.r)
        for t in writes:
            if t.w is not None:
                deps.append(t.w)
            deps.extend(t.r)
        if eng is self.pe:
            deps = [d for d in deps if d[0] != self.pe.key]
        return deps

    def op(self, eng, emit, reads=(), writes=()):
        for d in self._deps(eng, reads, writes):
            eng.wait(d)
        eng.n += 1
        sem = eng.sem
        eng.q.append(lambda b, emit=emit, sem=sem: emit(b).then_inc(sem, 1))
        ev = (eng.key, eng.sem, eng.n)
        for t in reads:
            t.r.append(ev)
        for t in writes:
            t.w = ev
            t.r = []
        return ev

    def dma(self, eng, out, in_, ds, reads=(), writes=()):
        for d in self._deps(eng, reads, writes):
            eng.wait(d)
        ds.count += 16
        sem = ds.sem
        eng.q.append(lambda b, out=out, in_=in_, sem=sem: b.dma_start(out=out, in_=in_).then_inc(sem, 16))
        ev = (ds.key, ds.sem, ds.count)
        for t in reads:
            t.r.append(ev)
        for t in writes:
            t.w = ev
            t.r = []
        return ev

    def finish(self, final_events=()):
        for ev in final_events:
            self.sp.wait(ev)
        with self.nc.Block() as block:
            @block.sync
            def _(e):
                for f in self.sp.q:
                    f(e)

            @block.tensor
            def _(e):
                for f in self.pe.q:
                    f(e)

            @block.scalar
            def _(e):
                for f in self.act.q:
                    f(e)

            @block.vector
            def _(e):
                for f in self.dve.q:
                    f(e)

            @block.gpsimd
            def _(e):
                for f in self.pool.q:
                    f(e)


class Buf:
    __slots__ = ("h", "t", "ds")

    def __init__(self, h, t, ds=None):
        self.h = h
        self.t = t
        self.ds = ds


class BankView:
    def __init__(self, t, m):
        self.t = t
        self.m = m

    def __getitem__(self, idx):
        p, c = idx
        return self.t[p, self.m, c]


class Ring:
    def __init__(self, bufs):
        self.bufs = bufs
        self.i = 0

    def alloc(self):
        b = self.bufs[self.i % len(self.bufs)]
        self.i += 1
        return b


def build_nc(NCH, dbg=False):
    S = NCH * TT
    NKB = S // 128
    nc = bass.Bass("TRN2", target_bir_lowering=False)
    fw = FW(nc)
    pe, act, dve, pool, sp = fw.pe, fw.act, fw.dve, fw.pool, fw.sp

    xT = nc.dram_tensor("xT", [D, S], F32, kind="ExternalInput").ap()
    outT = nc.dram_tensor("outT", [D, S], F32, kind="ExternalOutput").ap()
    prm_d = nc.dram_tensor("prm", [128, NPRM], F32, kind="ExternalInput").ap()
    cst_d = nc.dram_tensor("cst", [128, 256], F32, kind="ExternalInput").ap()
    w32 = {}
    wbf = {}
    wshape = {"wgu1": [NFG, 128, 4096], "wd1": [2, 128, NFC * 512], "win": [6, 128, 4096],
              "wout": [2, 128, 4096], "wgu2": [NFG, 128, 4096], "wd2": [2, 128, NFC * 512]}
    for k, shp in wshape.items():
        w32[k] = nc.dram_tensor(k, shp, F32, kind="ExternalInput").ap()
        wbf[k] = nc.dram_tensor(k + "_bf", shp, BF16, kind="Internal").ap()
    wtl = {}
    ds_wb = {}

    def sb(name, shape, dt):
        return nc.alloc_sbuf_tensor(name, shape, dt)

    xbuf = [sb(f"x{i}", [128, 8, TT], F32) for i in range(2)]
    t_x = [[Tl(f"x{i}_{dc}") for dc in range(8)] for i in range(2)]
    ds_x = [fw.dsem(f"dx{i}") for i in range(2)]
    ds_o = [fw.dsem(f"do{i}") for i in range(2)]
    hA = (sb("hA", [128, 8, TT], BF16), [Tl(f"hA{dc}") for dc in range(8)])
    hB = (sb("hB", [128, 8, TT], BF16), [Tl(f"hB{dc}") for dc in range(8)])
    actb = sb("act", [128, NFC, TT], BF16)
    t_act = [Tl(f"act{fc}") for fc in range(NFC)]
    qz = sb("qz", [128, 4, 2, TT], BF16)
    t_q = [Tl(f"q{h}") for h in range(4)]
    kT = sb("kT", [128, 4, S], BF16)
    t_k = [[Tl(f"k{h}_{j}") for j in range(NCH)] for h in range(4)]
    Vb = sb("V", [128, NKB, 512], BF16)
    t_v = [Tl(f"v{kb}") for kb in range(NKB)]
    mixed = sb("mixed", [128, 8, TT], BF16)
    t_mx = [Tl(f"mx{c}") for c in range(8)]
    ucar = sb("ucar", [128, 4, 2], F32)
    t_u = [Tl(f"u{cc}") for cc in range(4)]
    wring = Ring([Buf(sb(f"wr{i}", [128, 4096], BF16), Tl(f"wr{i}"), fw.dsem(f"dwr{i}")) for i in range(4)])
    pring = Ring([Buf(sb(f"pr{i}", [128, 2, TT], BF16), Tl(f"pr{i}")) for i in range(3)])
    sqring = Ring([Buf(sb(f"sq{i}", [128, TT], BF16), Tl(f"sq{i}")) for i in range(2)])
    sqn = Ring([Buf(sb(f"sqn{i}", [128, TT], BF16), Tl(f"sqn{i}")) for i in range(2)])
    scr = Ring([Buf(sb(f"sc{i}", [128, TT + 2], F32), Tl(f"sc{i}"), fw.dsem(f"dsc{i}")) for i in range(6)])
    prm = sb("prm_s", [128, P_LAM], F32)
    t_prm = Tl("prm")
    cst = sb("cst_s", [128, 256], F32)
    t_cst = Tl("cst")
    drv = sb("drv", [128, 8], F32)
    t_drv = Tl("drv")
    t_dummy = Tl("dummy")
    lamt = sb("lamt", [128, 2, 64], F32)
    lame = sb("lame", [128, 4], F32)
    t_lam = Tl("lam")
    ones_bf = sb("ones_bf", [128, 128], BF16)
    bd_bf = sb("bd_bf", [128, 128], BF16)
    tri2 = sb("tri2", [128, 2, 128], BF16)
    t_cbf = Tl("cbf")

    pa = [nc.alloc_psum_tensor(f"pa{i}", [128, 2, TT], F32) for i in range(2)]
    pb = [nc.alloc_psum_tensor(f"pb{i}", [128, TT], F32) for i in range(4)]
    bankA = [Buf(BankView(pa[i // 2], i % 2), Tl(f"psA{i}", excl=True)) for i in range(4)]
    bankB = [Buf(pb[i], Tl(f"psB{i}", excl=True)) for i in range(4)]
    psA = Ring(bankA)
    psB = Ring(bankB)
    pairA = Ring([(pa[0], bankA[0].t, bankA[1].t), (pa[1], bankA[2].t, bankA[3].t)])

    fw.dma(sp, prm[:, :], prm_d[:, 0:P_LAM], fw.dsem("dprm"), writes=[t_prm])
    lamb = scr.alloc()
    fw.dma(sp, lamb.h[:, 0:256], prm_d[:, P_LAM:P_LAM + 256], fw.dsem("dlam"), writes=[lamb.t])
    fw.dma(sp, cst[:, :], cst_d, fw.dsem("dcst"), writes=[t_cst])
    fw.op(dve, lambda b: b.memset(ones_bf[:, :], 1.0), writes=[t_cbf])
    fw.op(dve, lambda b: b.memset(bd_bf[:, :], 0.0), writes=[t_cbf])
    fw.op(dve, lambda b: b.memset(bd_bf[0:64, 0:64], 1.0), writes=[t_cbf])
    fw.op(dve, lambda b: b.memset(bd_bf[64:128, 64:128], 1.0), writes=[t_cbf])
    for m in range(2):
        fw.op(dve, lambda b, m=m: b.tensor_copy(out=tri2[:, m, :], in_=cst[:, 0:128]), reads=[t_cst], writes=[t_cbf])
    for cc in range(4):
        fw.op(dve, lambda b, cc=cc: b.memset(ucar[:, cc, :], 0.0), writes=[t_u[cc]])
    for hd in range(4):
        fw.op(dve, lambda b, hd=hd: b.memset(qz[:, hd, :, :], 0.0), writes=[t_q[hd]])
    fw.op(dve, lambda b: b.tensor_scalar(out=drv[:, 0:1], in0=prm[:, P_GQ:P_GQ + 1], scalar1=0.125, scalar2=None,
                                          op0=ALU.mult), reads=[t_prm], writes=[t_drv])
    fw.op(dve, lambda b: b.tensor_scalar(out=drv[:, 1:2], in0=prm[:, P_GSUB:P_GSUB + 1], scalar1=1.0 - LAM_INIT,
                                          scalar2=None, op0=ALU.mult), reads=[t_prm], writes=[t_drv])
    fw.op(dve, lambda b: b.memset(drv[:, 3:4], EPS), writes=[t_drv])
    fw.op(dve, lambda b: b.tensor_tensor(out=lamt[:, 0, :], in0=lamb.h[:, 0:64],
                                          in1=lamb.h[:, 64:128], op=ALU.mult), reads=[lamb.t], writes=[t_lam])
    fw.op(dve, lambda b: b.tensor_tensor(out=lamt[:, 1, :], in0=lamb.h[:, 128:192],
                                          in1=lamb.h[:, 192:256], op=ALU.mult), reads=[lamb.t], writes=[t_lam])
    fw.op(dve, lambda b: b.tensor_reduce(out=lame[:, 0:2], in_=lamt[:, :, :], axis=AX.X, op=ALU.add),
          reads=[t_lam], writes=[t_lam])
    fw.op(act, lambda b: b.activation(out=lame[:, 2:4], in_=lame[:, 0:2], func=AF.Exp), reads=[t_lam], writes=[t_lam])
    fw.op(dve, lambda b: b.tensor_tensor(out=lame[:, 0:1], in0=lame[:, 3:4], in1=lame[:, 2:3], op=ALU.subtract),
          reads=[t_lam], writes=[t_lam])
    fw.op(dve, lambda b: b.tensor_scalar(out=drv[:, 2:3], in0=lame[:, 0:1], scalar1=-LAM_INIT, scalar2=None,
                                          op0=ALU.add), reads=[t_lam], writes=[t_drv])
    eps_col = drv[:, 3:4]

    seen_pieces = set()

    def load_piece(key, idx, col0=0, ncol=4096):
        slot = wring.alloc()
        pk = (key, idx, col0)
        if pk not in seen_pieces:
            seen_pieces.add(pk)
            wtl[pk] = Tl(f"wt_{key}_{idx}_{col0}")
            ds_wb[pk] = fw.dsem(f"dwb_{key}_{idx}_{col0}")
            fw.dma(pool, slot.h[:, 0:ncol], w32[key][idx][:, col0:col0 + ncol], slot.ds, writes=[slot.t])
            fw.dma(sp, wbf[key][idx][:, col0:col0 + ncol], slot.h[:, 0:ncol], ds_wb[pk], reads=[slot.t],
                   writes=[wtl[pk]])
        else:
            fw.dma(sp, slot.h[:, 0:ncol], wbf[key][idx][:, col0:col0 + ncol], slot.ds,
                   reads=[wtl[pk]], writes=[slot.t])
        return slot

    def mm_group(out_ap, pairs, start=True, stop=True):
        def emit(b):
            ins = None
            n = len(pairs)
            for i, (l, r) in enumerate(pairs):
                ins = b.matmul(out_ap, lhsT=l, rhs=r, start=(start and i == 0), stop=(stop and i == n - 1))
            return ins
        return emit

    def proj_group(ps, pairs, common_reads, per_reads=None):
        if per_reads is None:
            fw.op(pe, mm_group(ps.h[:, 0:TT], pairs), reads=common_reads, writes=[ps.t])
        else:
            n = len(pairs)
            for i, (l, r) in enumerate(pairs):
                fw.op(pe, lambda b, l=l, r=r, i=i: b.matmul(ps.h[:, 0:TT], lhsT=l, rhs=r, start=(i == 0), stop=(i == n - 1)),
                      reads=list(common_reads) + list(per_reads[i]), writes=[ps.t])

    def preload_ln_table():
        fw.op(act, lambda b: b.activation(out=drv[:, 4:5], in_=drv[:, 3:4], func=AF.Ln), reads=[t_drv], writes=[t_dummy])

    def rstd_from(ps, inv_n):
        r = scr.alloc()
        fw.op(act, lambda b: b.activation(out=r.h[:, 0:TT], in_=ps.h[:, 0:TT], func=AF.Ln, bias=eps_col, scale=inv_n),
              reads=[ps.t, t_drv], writes=[r.t])
        fw.op(act, lambda b: b.activation(out=r.h[:, 0:TT], in_=r.h[:, 0:TT], func=AF.Exp, scale=-0.5),
              reads=[r.t], writes=[r.t])
        return r

    def norm_stats(xb):
        x = xbuf[xb]
        ps = psA.alloc()
        for dc in range(8):
            sq = sqn.alloc()
            fw.op(act, lambda b, sq=sq, dc=dc: b.activation(out=sq.h[:, 0:TT], in_=x[:, dc, :], func=AF.Square),
                  reads=[t_x[xb][dc]], writes=[sq.t])
            fw.op(pe, lambda b, sq=sq, dc=dc: b.matmul(ps.h[:, 0:TT], lhsT=ones_bf[:, :], rhs=sq.h[:, 0:TT],
                                                       start=(dc == 0), stop=(dc == 7)),
                  reads=[sq.t, t_cbf], writes=[ps.t])
        return rstd_from(ps, 1.0 / D)

    def norm_to_h(xb, gcol, hbuf):
        hb, t_h = hbuf
        x = xbuf[xb]
        r = norm_stats(xb)
        for dc in range(8):
            fw.op(dve, lambda b, dc=dc: b.scalar_tensor_tensor(
                out=hb[:, dc, :], in0=x[:, dc, :], scalar=prm[:, gcol + dc:gcol + dc + 1], in1=r.h[:, 0:TT],
                op0=ALU.mult, op1=ALU.mult), reads=[t_x[xb][dc], r.t, t_prm], writes=[t_h[dc]])

    def final_norm_store(j):
        xb = j % 2
        x = xbuf[xb]
        r = norm_stats(xb)
        for dc in range(8):
            fw.op(dve, lambda b, dc=dc: b.scalar_tensor_tensor(
                out=x[:, dc, :], in0=x[:, dc, :], scalar=prm[:, P_GF + dc:P_GF + dc + 1], in1=r.h[:, 0:TT],
                op0=ALU.mult, op1=ALU.mult), reads=[r.t, t_prm], writes=[t_x[xb][dc]])
        return [fw.dma(pool, outT[:, j * TT:(j + 1) * TT].rearrange("(dc p) s -> p dc s", p=128), x[:, :, :],
                       ds_o[xb], reads=t_x[xb])]

    def ffn_up(kgu, hbuf, hook=None):
        hb, t_h = hbuf
        for fg in range(NFG):
            slot = load_piece(kgu, fg)
            for f2 in range(2):
                fc = fg * 2 + f2
                g_ps = psA.alloc()
                u_ps = psA.alloc()
                prs = []
                for which in range(2):
                    prs.append([(slot.h[:, (which * 8 + dc) * 256 + f2 * 128:(which * 8 + dc) * 256 + f2 * 128 + 128],
                                 hb[:, dc, :]) for dc in range(8)])
                if fc == 0:
                    for dc in range(8):
                        for which, ps in ((0, g_ps), (1, u_ps)):
                            l, r = prs[which][dc]
                            fw.op(pe, lambda b, l=l, r=r, ps=ps, dc=dc: b.matmul(ps.h[:, 0:TT], lhsT=l, rhs=r,
                                                                                 start=(dc == 0), stop=(dc == 7)),
                                  reads=[slot.t, t_h[dc]], writes=[ps.t])
                else:
                    for which, ps in ((0, g_ps), (1, u_ps)):
                        fw.op(pe, mm_group(ps.h[:, 0:TT], prs[which]), reads=[slot.t] + t_h, writes=[ps.t])
                sg = scr.alloc()
                fw.op(act, lambda b, sg=sg, g_ps=g_ps: b.activation(out=sg.h[:, 0:TT], in_=g_ps.h[:, 0:TT], func=AF.Silu),
                      reads=[g_ps.t], writes=[sg.t])
                fw.op(dve, lambda b, sg=sg, u_ps=u_ps, fc=fc: b.tensor_tensor(
                    out=actb[:, fc, :], in0=sg.h[:, 0:TT], in1=u_ps.h[:, 0:TT], op=ALU.mult),
                    reads=[sg.t, u_ps.t], writes=[t_act[fc]])
            if fg == 1 and hook is not None:
                hook()

    def ffn_down(kd, xb, hook=None):
        x = xbuf[xb]
        preload_ln_table()
        for half in range(2):
            ring = psB if half == 0 else psA
            accs = [ring.alloc() for _ in range(4)]
            for (fc0, fc1) in ((0, 8), (8, 16), (16, 22)):
                slot = load_piece(kd, half, fc0 * 512, (fc1 - fc0) * 512)
                for fc in range(fc0, fc1):
                    def emit(b, fc=fc, fc0=fc0, slot=slot, accs=accs):
                        ins = None
                        for q in range(4):
                            ins = b.matmul(accs[q].h[:, 0:TT],
                                           lhsT=slot.h[:, (fc - fc0) * 512 + q * 128:(fc - fc0) * 512 + q * 128 + 128],
                                           rhs=actb[:, fc, :], start=(fc == 0), stop=(fc == NFC - 1))
                        return ins
                    fw.op(pe, emit, reads=[slot.t, t_act[fc]], writes=[a.t for a in accs])
                if half == 0 and fc0 == 0 and hook is not None:
                    hook()
            for q in range(4):
                dco = half * 4 + q
                fw.op(dve, lambda b, q=q, dco=dco, accs=accs: b.scalar_tensor_tensor(
                    out=x[:, dco, :], in0=accs[q].h[:, 0:TT], scalar=0.5, in1=x[:, dco, :],
                    op0=ALU.mult, op1=ALU.add), reads=[accs[q].t], writes=[t_x[xb][dco]])

    def gn_front(src_ap, src_tiles):
        sq = sqring.alloc()
        fw.op(act, lambda b: b.activation(out=sq.h[:, 0:TT], in_=src_ap, func=AF.Square), reads=src_tiles, writes=[sq.t])
        return sq

    def gn_back(sq, inv_n, bdmat, ring, finals):
        ss = ring.alloc()
        fw.op(pe, lambda b: b.matmul(ss.h[:, 0:TT], lhsT=bdmat, rhs=sq.h[:, 0:TT], start=True, stop=True),
              reads=[sq.t, t_cbf], writes=[ss.t])
        r = rstd_from(ss, inv_n)
        for (dst_ap, src_ap, gain_ap, psl, reads, dst_tile) in finals:
            fw.op(dve, lambda b, dst_ap=dst_ap, src_ap=src_ap, gain_ap=gain_ap, psl=psl: b.scalar_tensor_tensor(
                out=dst_ap, in0=src_ap, scalar=gain_ap, in1=r.h[psl, 0:TT], op0=ALU.mult, op1=ALU.mult),
                reads=list(reads) + [r.t, t_prm, t_drv], writes=[dst_tile])

    ALLP = slice(0, 128)

    def mixer_part1(xb, j, hook_a=None, hook_b=None):
        hb, t_h = hB
        x = xbuf[xb]
        items = [(which, hd) for which in range(2) for hd in range(4)]
        slots = {}
        pend = None
        for i, (which, hd) in enumerate(items):
            if which not in slots:
                slots[which] = load_piece("win", which)
            slot = slots[which]
            ring = psA if i % 2 == 0 else psB
            ps = ring.alloc()
            pairs = [(slot.h[:, dc * 512 + hd * 128:dc * 512 + hd * 128 + 128], hb[:, dc, :]) for dc in range(8)]
            if i == 0:
                proj_group(ps, pairs, [slot.t], per_reads=[[t_h[dc]] for dc in range(8)])
            else:
                proj_group(ps, pairs, [slot.t] + t_h)
            sq = gn_front(ps.h[:, 0:TT], [ps.t])
            if which == 0:
                finals = [(qz[0:64, hd, 0, :], ps.h[0:64, 0:TT], drv[0:64, 0:1], slice(0, 64), [ps.t], t_q[hd]),
                          (qz[64:128, hd, 1, :], ps.h[64:128, 0:TT], drv[64:128, 0:1], slice(64, 128), [ps.t], t_q[hd])]
            else:
                finals = [(kT[:, hd, j * TT:(j + 1) * TT], ps.h[:, 0:TT], prm[:, P_GK:P_GK + 1], ALLP, [ps.t], t_k[hd][j])]
            if pend is not None:
                gn_back(*pend)
            pend = (sq, 1.0 / 64, bd_bf[:, :], ring, finals)
            if i == 2 and hook_a is not None:
                hook_a()
        slot = load_piece("win", 2)
        for tb in range(4):
            ps = psA.alloc()
            pairs = [(hb[:, dc, tb * 128:(tb + 1) * 128], slot.h[:, dc * 512:(dc + 1) * 512]) for dc in range(8)]
            proj_group(ps, pairs, [slot.t] + t_h)
            if tb == 0:
                gn_back(*pend)
                pend = None
            kb = j * 4 + tb
            fw.op(act, lambda b, ps=ps, kb=kb: b.activation(out=Vb[:, kb, :], in_=ps.h[:, 0:TT], func=AF.Copy),
                  reads=[ps.t], writes=[t_v[kb]])
        sl_b = load_piece("win", 3)
        sl_c = load_piece("win", 4)
        sl_h = load_piece("win", 5)
        pend = None
        for cc in range(4):
            ring = psA if cc % 2 == 0 else psB
            pss = []
            for sl in (sl_b, sl_c, sl_h):
                ps = ring.alloc()
                pairs = [(sl.h[:, dc * 512 + cc * 128:dc * 512 + cc * 128 + 128], hb[:, dc, :]) for dc in range(8)]
                proj_group(ps, pairs, [sl.t] + t_h)
                pss.append(ps)
            if pend is not None:
                gn_back(*pend)
                pend = None
            gb_ps, gc_ps, hc_ps = pss
            hcs = scr.alloc()
            fw.op(act, lambda b, hcs=hcs, hc_ps=hc_ps: b.activation(out=hcs.h[:, 0:TT], in_=hc_ps.h[:, 0:TT], func=AF.Copy),
                  reads=[hc_ps.t], writes=[hcs.t])
            ub = scr.alloc()
            u = ub.h
            fw.op(dve, lambda b, u=u, cc=cc: b.tensor_copy(out=u[:, 0:2], in_=ucar[:, cc, :]), reads=[t_u[cc]], writes=[ub.t])
            fw.op(dve, lambda b, u=u, hcs=hcs, gc_ps=gc_ps: b.tensor_tensor(
                out=u[:, 2:TT + 2], in0=hcs.h[:, 0:TT], in1=gc_ps.h[:, 0:TT], op=ALU.mult),
                reads=[hcs.t, gc_ps.t], writes=[ub.t])
            y = scr.alloc()
            cw = lambda k, cc=cc: prm[:, P_CW + cc * 3 + k:P_CW + cc * 3 + k + 1]
            fw.op(dve, lambda b, u=u, y=y, cw=cw: b.tensor_scalar(out=y.h[:, 0:TT], in0=u[:, 2:TT + 2], scalar1=cw(2),
                                                                   scalar2=None, op0=ALU.mult),
                  reads=[ub.t, t_prm], writes=[y.t])
            fw.op(dve, lambda b, u=u, y=y, cw=cw: b.scalar_tensor_tensor(
                out=y.h[:, 0:TT], in0=u[:, 1:TT + 1], scalar=cw(1), in1=y.h[:, 0:TT], op0=ALU.mult, op1=ALU.add),
                reads=[ub.t, t_prm], writes=[y.t])
            fw.op(dve, lambda b, u=u, y=y, cw=cw: b.scalar_tensor_tensor(
                out=y.h[:, 0:TT], in0=u[:, 0:TT], scalar=cw(0), in1=y.h[:, 0:TT], op0=ALU.mult, op1=ALU.add),
                reads=[ub.t, t_prm], writes=[y.t])
            fw.op(dve, lambda b, u=u, cc=cc: b.tensor_copy(out=ucar[:, cc, :], in_=u[:, TT:TT + 2]),
                  reads=[ub.t], writes=[t_u[cc]])
            fw.op(dve, lambda b, y=y, gb_ps=gb_ps: b.tensor_tensor(out=y.h[:, 0:TT], in0=y.h[:, 0:TT], in1=gb_ps.h[:, 0:TT],
                                                                    op=ALU.mult), reads=[gb_ps.t], writes=[y.t])
            cfin = [(mixed[:, 4 + cc, :], y.h[:, 0:TT], prm[:, P_GCN + cc:P_GCN + cc + 1], ALLP, [y.t], t_mx[4 + cc])]
            if cc == 3 and hook_b is not None:
                hook_b()
            sq = gn_front(y.h[:, 0:TT], [y.t])
            pend = (sq, 1.0 / 64, bd_bf[:, :], ring, cfin)
        gn_back(*pend)
        pend = None
        nkb = 4 * j + 4
        LAG = 2

        def att_p1(hd, acc, zz):
            ls, cs = [], []
            for m in range(2):
                l = scr.alloc()
                fw.op(act, lambda b, l=l, m=m: b.activation(out=l.h[:, 0:TT], in_=zz[m].h[:, 0:TT], func=AF.Copy),
                      reads=[zz[m].t], writes=[l.t])
                ls.append(l)
            for m in (1, 0):
                c = scr.alloc()
                fw.op(dve, lambda b, c=c, m=m: b.tensor_copy(out=c.h[:, 0:TT], in_=acc[m].h[:, 0:TT]),
                      reads=[acc[m].t], writes=[c.t])
                cs.insert(0, c)
            return (hd, ls, cs)

        def att_p2(st):
            hd, ls, cs = st
            for m in range(2):
                fw.op(dve, lambda b, l=ls[m]: b.reciprocal(out=l.h[:, 0:TT], in_=l.h[:, 0:TT]), reads=[ls[m].t], writes=[ls[m].t])
            for m in range(2):
                fw.op(dve, lambda b, c=cs[m], l=ls[m]: b.tensor_tensor(out=c.h[:, 0:TT], in0=c.h[:, 0:TT], in1=l.h[:, 0:TT],
                                                                        op=ALU.mult), reads=[ls[m].t], writes=[cs[m].t])
            fw.op(dve, lambda b: b.scalar_tensor_tensor(
                out=cs[0].h[:, 0:TT], in0=cs[1].h[:, 0:TT], scalar=drv[:, 2:3], in1=cs[0].h[:, 0:TT], op0=ALU.mult, op1=ALU.add),
                reads=[cs[1].t, t_drv], writes=[cs[0].t])
            a = cs[0]
            sq = sqring.alloc()
            fw.op(dve, lambda b: b.tensor_tensor(out=sq.h[:, 0:TT], in0=a.h[:, 0:TT], in1=a.h[:, 0:TT], op=ALU.mult),
                  reads=[a.t], writes=[sq.t])
            return (hd, a, sq)

        def att_p3(st, ring):
            hd, a, sq = st
            gn_back(sq, 1.0 / 128, ones_bf[:, :], ring,
                    [(mixed[:, hd, :], a.h[:, 0:TT], drv[:, 1:2], ALLP, [a.t], t_mx[hd])])

        st1 = None
        st2 = None
        for hd in range(4):
            acc = [psB.alloc(), psB.alloc()]
            zz = [psB.alloc(), psB.alloc()]
            inflight = []
            for kb in range(nkb + LAG):
                if kb < nkb:
                    rel = kb - 4 * j
                    koff = max(0, rel) * 128
                    s3, ta, tb_ = pairA.alloc()

                    def emit_s(b, s3=s3, kb=kb, koff=koff, hd=hd):
                        ins = None
                        for m in range(2):
                            ins = b.matmul(s3[:, m, koff:TT], lhsT=kT[:, hd, kb * 128:(kb + 1) * 128],
                                           rhs=qz[:, hd, m, koff:TT], start=True, stop=True)
                        return ins
                    fw.op(pe, emit_s, reads=[t_k[hd][kb // 4], t_q[hd]], writes=[ta, tb_])
                    p = pring.alloc()
                    bcol = 128 + hd * NREL + (rel + NREL - 4)
                    fw.op(act, lambda b, p=p, s3=s3, koff=koff, bcol=bcol: b.activation(
                        out=p.h[:, :, koff:TT], in_=s3[:, :, koff:TT], func=AF.Exp, bias=cst[:, bcol:bcol + 1], scale=1.0),
                        reads=[ta, tb_, t_cst], writes=[p.t])
                    if rel >= 0:
                        fw.op(dve, lambda b, p=p, koff=koff: b.tensor_tensor(
                            out=p.h[:, :, koff:koff + 128], in0=p.h[:, :, koff:koff + 128], in1=tri2[:, :, :], op=ALU.mult),
                            reads=[t_cbf], writes=[p.t])
                    inflight.append((kb, koff, p))
                if kb == 0 and st1 is not None:
                    st2 = att_p2(st1)
                    st1 = None
                if kb >= LAG:
                    pkb, pkoff, pp = inflight.pop(0)

                    def emit_pv(b, kb=pkb, koff=pkoff, p=pp, hd=hd, acc=acc, zz=zz):
                        ins = None
                        for m in range(2):
                            b.matmul(acc[m].h[:, koff:TT], lhsT=Vb[:, kb, hd * 128:(hd + 1) * 128], rhs=p.h[:, m, koff:TT],
                                     start=(kb == 0), stop=(kb == nkb - 1))
                            ins = b.matmul(zz[m].h[:, koff:TT], lhsT=ones_bf[:, :], rhs=p.h[:, m, koff:TT],
                                           start=(kb == 0), stop=(kb == nkb - 1))
                        return ins
                    fw.op(pe, emit_pv, reads=[pp.t, t_v[pkb], t_cbf], writes=[acc[0].t, acc[1].t, zz[0].t, zz[1].t])
                if kb == nkb - 1 and st2 is not None:
                    att_p3(st2, psA)
                    st2 = None
            st1 = att_p1(hd, acc, zz)
        st2 = att_p2(st1)
        return lambda ring: att_p3(st2, ring)

    def mixer_part2(xb, tail):
        x = xbuf[xb]
        tail(psA)
        for half in range(2):
            slot = load_piece("wout", half)
            for q in range(4):
                dco = half * 4 + q
                ps = psA.alloc()
                pairs = [(slot.h[:, c * 512 + q * 128:c * 512 + q * 128 + 128], mixed[:, c, :]) for c in range(8)]
                proj_group(ps, pairs, [slot.t] + t_mx)
                fw.op(dve, lambda b, ps=ps, dco=dco: b.tensor_tensor(out=x[:, dco, :], in0=ps.h[:, 0:TT], in1=x[:, dco, :],
                                                                      op=ALU.add), reads=[ps.t], writes=[t_x[xb][dco]])

    def load_x(j):
        xb = j % 2
        fw.dma(pool, xbuf[xb][:, :, :], xT[:, j * TT:(j + 1) * TT].rearrange("(dc p) s -> p dc s", p=128), ds_x[xb],
               writes=t_x[xb])

    finals = []
    load_x(0)
    norm_to_h(0, P_G1, hA)
    ffn_up("wgu1", hA)
    if NCH > 1:
        load_x(1)
    ffn_down("wd1", 0)
    norm_to_h(0, P_GM, hB)
    for j in range(NCH):
        xb = j % 2
        nxt = j + 1 < NCH
        hook_a = None
        if j > 0:
            def hook_a(j=j, nxt=nxt):
                finals.extend(final_norm_store(j - 1))
                if nxt:
                    load_x(j + 1)
        hook_b = (lambda j=j: norm_to_h((j + 1) % 2, P_G1, hA)) if nxt else None
        tail = mixer_part1(xb, j, hook_a, hook_b)
        if nxt:
            ffn_up("wgu1", hA, hook=lambda tail=tail: tail(psB))
            mixer_part2(xb, lambda ring: None)
            ffn_down("wd1", (j + 1) % 2, hook=lambda xb=xb: norm_to_h(xb, P_G2, hB))
        else:
            mixer_part2(xb, tail)
            norm_to_h(xb, P_G2, hB)
        ffn_up("wgu2", hB)
        ffn_down("wd2", xb, hook=(lambda j=j: norm_to_h((j + 1) % 2, P_GM, hB)) if nxt else None)
    finals.extend(final_norm_store(NCH - 1))
    fw.finish(finals)
    return nc


def _consts():
    cst = np.zeros((128, 256), np.float32)
    ki = np.arange(128)
    cst[:, 0:128] = (ki[:, None] <= ki[None, :]).astype(np.float32)
    for h in range(4):
        for r in range(NREL):
            rel = r - (NREL - 4)
            cst[:, 128 + h * NREL + r] = SLOPES[h] * (ki + 128.0 * rel - 256.0)
    return cst


def _layout_weights(inp):
    f32 = lambda a: np.ascontiguousarray(np.asarray(a, dtype=np.float32))
    out = {}
    for i, tag in ((1, "ffn1"), (2, "ffn2")):
        wg = f32(inp[f"{tag}_w_gate"])[0].reshape(8, 128, NFG, 256)
        wu = f32(inp[f"{tag}_w_up"])[0].reshape(8, 128, NFG, 256)
        gu = np.stack([wg, wu], axis=0)
        out[f"wgu{i}"] = np.ascontiguousarray(gu.transpose(3, 2, 0, 1, 4)).reshape(NFG, 128, 4096)
        wd = f32(inp[f"{tag}_w_down"])[0].reshape(NFC, 128, 2, 512)
        out[f"wd{i}"] = np.ascontiguousarray(wd.transpose(2, 1, 0, 3)).reshape(2, 128, NFC * 512)
    win = f32(inp["w_in"])[0].reshape(8, 128, 6, 512)
    out["win"] = np.ascontiguousarray(win.transpose(2, 1, 0, 3)).reshape(6, 128, 4096)
    wo = f32(inp["w_out"])[0].reshape(8, 128, 2, 512)
    out["wout"] = np.ascontiguousarray(wo.transpose(2, 1, 0, 3)).reshape(2, 128, 4096)
    prm = np.zeros((128, NPRM), np.float32)
    for col, key in ((P_G1, "ffn1_norm"), (P_GM, "mix_norm"), (P_G2, "ffn2_norm"), (P_GF, "final_norm")):
        prm[:, col:col + 8] = f32(inp[key])[0].reshape(8, 128).T
    prm[:, P_GQ] = np.tile(f32(inp["q_norm"])[0], 2)
    prm[:, P_GK] = np.tile(f32(inp["k_norm"])[0], 2)
    prm[:, P_GSUB] = f32(inp["attn_subln"])[0]
    prm[:, P_GCN:P_GCN + 4] = f32(inp["conv_norm"])[0].reshape(4, 128).T
    cw = f32(inp["conv_w"])[0]
    for cc in range(4):
        for k in range(3):
            prm[:, P_CW + cc * 3 + k] = cw[k, cc * 128:(cc + 1) * 128]
    for i, key in enumerate(("lambda_q1", "lambda_k1", "lambda_q2", "lambda_k2")):
        prm[:, P_LAM + i * 64:P_LAM + (i + 1) * 64] = f32(inp[key])[0][None, :]
    out["prm"] = prm
    out["cst"] = _consts()
    return out


_NC_CACHE = {}


def _get_nc(nch):
    if nch not in _NC_CACHE:
        _NC_CACHE[nch] = build_nc(nch)
    return _NC_CACHE[nch]


def kernel(**inputs):
    x = np.asarray(inputs["x"], dtype=np.float32)
    B, S, _ = x.shape
    nch = S // TT
    shared = _layout_weights(inputs)
    in_maps = []
    for b in range(B):
        m = dict(shared)
        m["xT"] = np.ascontiguousarray(x[b].T)
        in_maps.append(m)
    nc = build_nc(nch)
    res = run_bass_kernel_spmd(nc, in_maps, core_ids=list(range(B)))
    out = np.stack([np.ascontiguousarray(np.asarray(r["outT"]).T) for r in res.results], axis=0)
    return out.astype(np.float32)
```
